# Optimizing a Trainium2 kernel written in Bass

```python
import math
import jax, jax.numpy as jnp
from jax import lax
import numpy as np

D_MODEL = 1024
BATCH = 8
SEQ = 2048
DEPTH = 1
DEC_BATCH = 128
DEC_SEQ = 8
PAST_LEN = 16384
PAGE_SIZE = 128

S5_WIDTH = D_MODEL // 2
S5_GROUP = 16
S5_GROUPS = S5_WIDTH // S5_GROUP
S5_STATE = 64
RWKV_WIDTH = D_MODEL - S5_WIDTH
RWKV_HEAD = 64
RWKV_HEADS = RWKV_WIDTH // RWKV_HEAD
W_LORA = 64
A_LORA = 64
G_LORA = 128
RWKV_COLS = 3 * RWKV_WIDTH + W_LORA + A_LORA + G_LORA
IN_COLS = S5_WIDTH + RWKV_COLS
PEER_HEADS = 8
N_KEYS = 128
N_EXPERTS = N_KEYS * N_KEYS
PEER_TOPK = 16
PEER_KEY_DIM = 128
PEER_HALF = PEER_KEY_DIM // 2
PEER_BLOCK = 128
NORM_EPS = 1e-6
GN_EPS = 64e-5

kernel_name = "hybrid_s5_rwkv7_peer_adaln_step"

F32 = jnp.float32


def rmsnorm(x, g):
    x32 = x.astype(F32)
    return x32 * lax.rsqrt(jnp.mean(x32 * x32, axis=-1, keepdims=True) + NORM_EPS) * g.astype(F32)


def s5_mixer(u, h_re0, h_im0, a_re, a_im, log_dt, b_re, b_im, c_re, c_im, d_skip, w_glu, b_glu):
    nb, T, _ = u.shape
    u32 = u.astype(F32)
    ug = u32.reshape(nb, T, S5_GROUPS, S5_GROUP)
    dt = jnp.exp(log_dt.astype(F32))[:, None]
    lam_re, lam_im = a_re.astype(F32), a_im.astype(F32)
    mag = jnp.exp(lam_re * dt)
    ang = lam_im * dt
    lb_re, lb_im = mag * jnp.cos(ang), mag * jnp.sin(ang)
    den = lam_re * lam_re + lam_im * lam_im
    n_re, n_im = lb_re - 1.0, lb_im
    coef_re = (n_re * lam_re + n_im * lam_im) / den
    coef_im = (n_im * lam_re - n_re * lam_im) / den
    b_re32, b_im32 = b_re.astype(F32), b_im.astype(F32)
    bb_re = coef_re[..., None] * b_re32 - coef_im[..., None] * b_im32
    bb_im = coef_re[..., None] * b_im32 + coef_im[..., None] * b_re32
    bu_re = jnp.einsum('btgh,gph->tbgp', ug, bb_re)
    bu_im = jnp.einsum('btgh,gph->tbgp', ug, bb_im)
    h_re0, h_im0 = h_re0.astype(F32), h_im0.astype(F32)
    bu_re = bu_re.at[0].add(lb_re * h_re0 - lb_im * h_im0)
    bu_im = bu_im.at[0].add(lb_re * h_im0 + lb_im * h_re0)
    a_re_t = jnp.broadcast_to(lb_re, (T, 1, S5_GROUPS, S5_STATE))
    a_im_t = jnp.broadcast_to(lb_im, (T, 1, S5_GROUPS, S5_STATE))

    def combine(left, right):
        ar1, ai1, br1, bi1 = left
        ar2, ai2, br2, bi2 = right
        return (ar2 * ar1 - ai2 * ai1, ar2 * ai1 + ai2 * ar1,
                ar2 * br1 - ai2 * bi1 + br2, ar2 * bi1 + ai2 * br1 + bi2)

    _, _, hs_re, hs_im = lax.associative_scan(combine, (a_re_t, a_im_t, bu_re, bu_im), axis=0)
    y = (jnp.einsum('ghp,tbgp->btgh', c_re.astype(F32), hs_re)
         - jnp.einsum('ghp,tbgp->btgh', c_im.astype(F32), hs_im))
    y = y.reshape(nb, T, S5_WIDTH) + d_skip.astype(F32) * u32
    y = jax.nn.gelu(y, approximate=False)
    y = y * jax.nn.sigmoid(y @ w_glu + b_glu)
    return y, hs_re[-1], hs_im[-1]


def rwkv7_mixer(p, shift0, S0, mu, w0, w2, a0, a2, g2, k_k, k_a, r_k, gn_w, gn_b):
    nb, T, _ = p.shape
    p = p.astype(F32)
    p_prev = jnp.concatenate([shift0.astype(F32)[:, None, :], p[:, :-1]], axis=1)
    ps = p + (p_prev - p) * mu.astype(F32)
    o1 = RWKV_WIDTH
    r, k, v, w_lo, a_lo, g_lo = jnp.split(
        ps, [o1, 2 * o1, 3 * o1, 3 * o1 + W_LORA, 3 * o1 + W_LORA + A_LORA], axis=-1)
    w_raw = -jax.nn.softplus(-(w0 + jnp.tanh(w_lo) @ w2)) - 0.5
    decay = jnp.exp(-jnp.exp(w_raw))
    a = jax.nn.sigmoid(a0 + a_lo @ a2)
    g = jax.nn.sigmoid(g_lo) @ g2

    def heads(t):
        return t.reshape(nb, T, RWKV_HEADS, RWKV_HEAD)

    kk = heads(k * k_k)
    kk = kk / jnp.maximum(jnp.linalg.norm(kk, axis=-1, keepdims=True), 1e-12)
    k = k * (1.0 + (a - 1.0) * k_a)
    r_h, w_h, k_h, v_h, a_h = heads(r), heads(decay), heads(k), heads(v), heads(a)

    def step(S, inp):
        r_t, w_t, k_t, v_t, kk_t, a_t = inp
        sa = jnp.einsum('bhvk,bhk->bhv', S, -kk_t)
        S = (S * w_t[:, :, None, :] + sa[..., None] * (kk_t * a_t)[:, :, None, :]
             + v_t[..., None] * k_t[:, :, None, :])
        return S, jnp.einsum('bhvk,bhk->bhv', S, r_t)

    xs = (jnp.swapaxes(r_h, 0, 1), jnp.swapaxes(w_h, 0, 1), jnp.swapaxes(k_h, 0, 1),
          jnp.swapaxes(v_h, 0, 1), jnp.swapaxes(kk, 0, 1), jnp.swapaxes(a_h, 0, 1))
    S_T, y = lax.scan(step, S0.astype(F32), xs)
    y = jnp.swapaxes(y, 0, 1)
    mean = jnp.mean(y, axis=-1, keepdims=True)
    var = jnp.mean(jnp.square(y - mean), axis=-1, keepdims=True)
    y = (y - mean) * lax.rsqrt(var + GN_EPS)
    y = y * gn_w.reshape(RWKV_HEADS, RWKV_HEAD) + gn_b.reshape(RWKV_HEADS, RWKV_HEAD)
    y = y + jnp.sum(r_h * k_h * r_k, axis=-1, keepdims=True) * v_h
    y = y.reshape(nb, T, RWKV_WIDTH) * g
    return y, S_T, p[:, -1]


def peer_ffn(h, w_q, keys1, keys2, u_tab, v_tab):
    T = h.shape[0]
    n_blk = -(-T // PEER_BLOCK)
    hb = jnp.pad(h, ((0, n_blk * PEER_BLOCK - T), (0, 0))).reshape(n_blk, PEER_BLOCK, D_MODEL)
    k1 = keys1.astype(F32)
    k2 = keys2.astype(F32)

    def block(xb):
        q = (xb @ w_q).astype(F32).reshape(PEER_BLOCK, PEER_HEADS, 2, PEER_HALF)
        s1 = jnp.einsum('thd,hnd->thn', q[:, :, 0], k1)
        s2 = jnp.einsum('thd,hnd->thn', q[:, :, 1], k2)
        v1, i1 = lax.top_k(s1, PEER_TOPK)
        v2, i2 = lax.top_k(s2, PEER_TOPK)
        cand = (v1[..., :, None] + v2[..., None, :]).reshape(PEER_BLOCK, PEER_HEADS, PEER_TOPK * PEER_TOPK)
        cidx = (i1[..., :, None] * N_KEYS + i2[..., None, :]).reshape(PEER_BLOCK, PEER_HEADS, PEER_TOPK * PEER_TOPK)
        sc, pos = lax.top_k(cand, PEER_TOPK)
        idx = jnp.take_along_axis(cidx, pos, axis=-1)
        gate = jax.nn.softmax(sc, axis=-1)
        act = jax.nn.gelu(jnp.einsum('thkd,td->thk', jnp.take(u_tab, idx, axis=0), xb), approximate=False)
        return jnp.einsum('thk,thkd->td', gate * act, jnp.take(v_tab, idx, axis=0))

    return lax.map(block, hb).reshape(n_blk * PEER_BLOCK, D_MODEL)[:T]


def hybrid_layer(x, c, s5_re, s5_im, wkv, shift, prm):
    nb, T, _ = x.shape
    mod = jax.nn.silu(c.astype(F32)) @ prm["w_ada"] + prm["b_ada"]
    sh1, sc1, ga1, sh2, sc2, ga2 = jnp.split(mod[:, None, :], 6, axis=-1)
    h = rmsnorm(x, prm["norm1_g"]) * (1.0 + sc1) + sh1
    proj = h @ prm["w_in"]
    y_s5, s5_re, s5_im = s5_mixer(proj[..., :S5_WIDTH], s5_re, s5_im, prm["s5_a_re"], prm["s5_a_im"],
                                  prm["s5_log_dt"], prm["s5_b_re"], prm["s5_b_im"], prm["s5_c_re"],
                                  prm["s5_c_im"], prm["s5_d"], prm["w_glu"], prm["b_glu"])
    y_rw, wkv, shift = rwkv7_mixer(proj[..., S5_WIDTH:], shift, wkv, prm["rwkv_mu"], prm["rwkv_w0"],
                                   prm["rwkv_w2"], prm["rwkv_a0"], prm["rwkv_a2"], prm["rwkv_g2"],
                                   prm["rwkv_k_k"], prm["rwkv_k_a"], prm["rwkv_r_k"],
                                   prm["rwkv_gn_w"], prm["rwkv_gn_b"])
    x = x + ga1 * (jnp.concatenate([y_s5, y_rw], axis=-1) @ prm["w_out"])
    h = rmsnorm(x, prm["norm2_g"]) * (1.0 + sc2) + sh2
    ff = peer_ffn(h.reshape(nb * T, D_MODEL), prm["peer_w_q"], prm["peer_keys1"], prm["peer_keys2"],
                  prm["peer_u"], prm["peer_v"]).reshape(nb, T, D_MODEL)
    x = x + ga2 * ff
    return x, s5_re, s5_im, wkv, shift


def setup_inputs(seed: int = 0) -> dict:
    key = jax.random.key(seed)
    ks = iter(jax.random.split(key, 48))

    def nrm(shape, scale):
        return scale * jax.random.normal(next(ks), shape, F32)

    L = DEPTH
    d = D_MODEL
    a_im = jnp.broadcast_to(jnp.pi * jnp.arange(S5_STATE, dtype=F32), (L, S5_GROUPS, S5_STATE))
    return {
        "x_prompt": nrm((BATCH, SEQ, d), 1.0),
        "x_sample": nrm((DEC_BATCH, DEC_SEQ, d), 1.0),
        "state_s5_re": nrm((L, DEC_BATCH, S5_GROUPS, S5_STATE), 0.1),
        "state_s5_im": nrm((L, DEC_BATCH, S5_GROUPS, S5_STATE), 0.1),
        "state_wkv": nrm((L, DEC_BATCH, RWKV_HEADS, RWKV_HEAD, RWKV_HEAD), 0.1),
        "state_shift": nrm((L, DEC_BATCH, RWKV_COLS), 1.0),
        "c_prompt": nrm((BATCH, d), 1.0),
        "c_sample": nrm((DEC_BATCH, d), 1.0),
        "w_ada": nrm((L, d, 6 * d), 0.5 * d ** -0.5),
        "b_ada": nrm((L, 6 * d), 0.01),
        "norm1_g": 1.0 + nrm((L, d), 0.02),
        "norm2_g": 1.0 + nrm((L, d), 0.02),
        "w_in": nrm((L, d, IN_COLS), d ** -0.5),
        "w_out": nrm((L, S5_WIDTH + RWKV_WIDTH, d), (S5_WIDTH + RWKV_WIDTH) ** -0.5),
        "s5_a_re": -0.5 + nrm((L, S5_GROUPS, S5_STATE), 0.01),
        "s5_a_im": a_im + nrm((L, S5_GROUPS, S5_STATE), 0.01),
        "s5_log_dt": jax.random.uniform(next(ks), (L, S5_GROUPS), F32, math.log(1e-3), math.log(1e-1)),
        "s5_b_re": nrm((L, S5_GROUPS, S5_STATE, S5_GROUP), (2 * S5_GROUP) ** -0.5),
        "s5_b_im": nrm((L, S5_GROUPS, S5_STATE, S5_GROUP), (2 * S5_GROUP) ** -0.5),
        "s5_c_re": nrm((L, S5_GROUPS, S5_GROUP, S5_STATE), S5_STATE ** -0.5),
        "s5_c_im": nrm((L, S5_GROUPS, S5_GROUP, S5_STATE), S5_STATE ** -0.5),
        "s5_d": nrm((L, S5_WIDTH), 1.0),
        "w_glu": nrm((L, S5_WIDTH, S5_WIDTH), S5_WIDTH ** -0.5),
        "b_glu": nrm((L, S5_WIDTH), 0.01),
        "rwkv_mu": jax.random.uniform(next(ks), (L, RWKV_COLS), F32, 0.0, 1.0),
        "rwkv_w0": jax.random.uniform(next(ks), (L, RWKV_WIDTH), F32, -6.0, -1.0),
        "rwkv_w2": nrm((L, W_LORA, RWKV_WIDTH), 0.1),
        "rwkv_a0": nrm((L, RWKV_WIDTH), 0.1),
        "rwkv_a2": nrm((L, A_LORA, RWKV_WIDTH), 0.1),
        "rwkv_g2": nrm((L, G_LORA, RWKV_WIDTH), G_LORA ** -0.5),
        "rwkv_k_k": 0.85 + nrm((L, RWKV_WIDTH), 0.05),
        "rwkv_k_a": 1.0 + nrm((L, RWKV_WIDTH), 0.05),
        "rwkv_r_k": nrm((L, RWKV_HEADS, RWKV_HEAD), 0.1),
        "rwkv_gn_w": 1.0 + nrm((L, RWKV_WIDTH), 0.02),
        "rwkv_gn_b": nrm((L, RWKV_WIDTH), 0.01),
        "peer_w_q": nrm((L, d, PEER_HEADS * PEER_KEY_DIM), d ** -0.5),
        "peer_keys1": nrm((L, PEER_HEADS, N_KEYS, PEER_HALF), PEER_HALF ** -0.5),
        "peer_keys2": nrm((L, PEER_HEADS, N_KEYS, PEER_HALF), PEER_HALF ** -0.5),
        "peer_u": nrm((L, N_EXPERTS, d), d ** -0.5),
        "peer_v": nrm((L, N_EXPERTS, d), 0.3),
        "final_norm_g": 1.0 + nrm((d,), 0.02),
    }


def reference(x_prompt, x_sample, state_s5_re, state_s5_im, state_wkv, state_shift, c_prompt, c_sample,
              w_ada, b_ada, norm1_g, norm2_g, w_in, w_out, s5_a_re, s5_a_im, s5_log_dt, s5_b_re, s5_b_im,
              s5_c_re, s5_c_im, s5_d, w_glu, b_glu, rwkv_mu, rwkv_w0, rwkv_w2, rwkv_a0, rwkv_a2, rwkv_g2,
              rwkv_k_k, rwkv_k_a, rwkv_r_k, rwkv_gn_w, rwkv_gn_b, peer_w_q, peer_keys1, peer_keys2,
              peer_u, peer_v, final_norm_g):
    nbp = x_prompt.shape[0]
    hp = x_prompt.astype(F32)
    hs = x_sample.astype(F32)
    p_re_l, p_im_l, p_wkv_l, p_sh_l = [], [], [], []
    s_re_l, s_im_l, s_wkv_l, s_sh_l = [], [], [], []
    for l in range(DEPTH):
        prm = {
            "w_ada": w_ada[l], "b_ada": b_ada[l], "norm1_g": norm1_g[l], "norm2_g": norm2_g[l],
            "w_in": w_in[l], "w_out": w_out[l], "s5_a_re": s5_a_re[l], "s5_a_im": s5_a_im[l],
            "s5_log_dt": s5_log_dt[l], "s5_b_re": s5_b_re[l], "s5_b_im": s5_b_im[l],
            "s5_c_re": s5_c_re[l], "s5_c_im": s5_c_im[l], "s5_d": s5_d[l], "w_glu": w_glu[l],
            "b_glu": b_glu[l], "rwkv_mu": rwkv_mu[l], "rwkv_w0": rwkv_w0[l], "rwkv_w2": rwkv_w2[l],
            "rwkv_a0": rwkv_a0[l], "rwkv_a2": rwkv_a2[l], "rwkv_g2": rwkv_g2[l],
            "rwkv_k_k": rwkv_k_k[l], "rwkv_k_a": rwkv_k_a[l], "rwkv_r_k": rwkv_r_k[l],
            "rwkv_gn_w": rwkv_gn_w[l], "rwkv_gn_b": rwkv_gn_b[l], "peer_w_q": peer_w_q[l],
            "peer_keys1": peer_keys1[l], "peer_keys2": peer_keys2[l], "peer_u": peer_u[l],
            "peer_v": peer_v[l],
        }
        z_s5 = jnp.zeros((nbp, S5_GROUPS, S5_STATE), F32)
        z_wkv = jnp.zeros((nbp, RWKV_HEADS, RWKV_HEAD, RWKV_HEAD), F32)
        z_sh = jnp.zeros((nbp, RWKV_COLS), F32)
        hp, pr, pi, pw, psh = hybrid_layer(hp, c_prompt, z_s5, z_s5, z_wkv, z_sh, prm)
        hs, sr, si, sw, ssh = hybrid_layer(hs, c_sample, state_s5_re[l], state_s5_im[l], state_wkv[l],
                                           state_shift[l], prm)
        p_re_l.append(pr); p_im_l.append(pi); p_wkv_l.append(pw); p_sh_l.append(psh)
        s_re_l.append(sr); s_im_l.append(si); s_wkv_l.append(sw); s_sh_l.append(ssh)
    y_prompt = rmsnorm(hp, final_norm_g).astype(x_prompt.dtype)
    y_sample = rmsnorm(hs, final_norm_g).astype(x_sample.dtype)
    s5_re_prompt = jnp.stack(p_re_l)
    s5_im_prompt = jnp.stack(p_im_l)
    wkv_prompt = jnp.stack(p_wkv_l)
    shift_prompt = jnp.stack(p_sh_l)
    s5_re_sample = jnp.stack(s_re_l)
    s5_im_sample = jnp.stack(s_im_l)
    wkv_sample = jnp.stack(s_wkv_l)
    shift_sample = jnp.stack(s_sh_l)
    return (y_prompt, y_sample, s5_re_prompt, s5_im_prompt, wkv_prompt, shift_prompt,
            s5_re_sample, s5_im_sample, wkv_sample, shift_sample)
```

```python
from contextlib import ExitStack
import numpy as np
import concourse.bass as bass
import concourse.mybir as mybir
from concourse.bass_utils import run_bass_kernel_spmd

F32 = mybir.dt.float32
BF16 = mybir.dt.bfloat16
I32 = mybir.dt.int32
U32 = mybir.dt.uint32
ALU = mybir.AluOpType
AF = mybir.ActivationFunctionType
AX = mybir.AxisListType

NCORES = 8
D = 1024
NB = 256
NPB = 8
TP = 2048
TS = 128
TT_ = TP + TS
INC = 2304


class TT:
    __slots__ = ("h", "w", "r", "dsem", "dcnt", "name", "psum")

    def __init__(self, h, name):
        self.h = h
        self.psum = False
        self.w = {}
        self.r = {}
        self.dsem = None
        self.dcnt = 0
        self.name = name

    def __getitem__(self, k):
        return self.h[k]


class DmaGroup:
    def __init__(self, kb, name):
        self.sem = kb.newsem("g_" + name)
        self.cnt = 0
        self.outs = []
        self.ins = []

    def close(self):
        for t in self.outs:
            t.w[self.sem] = self.cnt
        for t in self.ins:
            t.r[self.sem] = self.cnt
        self.outs = []
        self.ins = []


class KB:
    def __init__(self):
        self.nc = bass.Bass("TRN2", target_bir_lowering=False)
        nc = self.nc
        self.es = ExitStack()
        self.eng = {"pe": nc.tensor, "act": nc.scalar, "dve": nc.vector, "pool": nc.gpsimd, "sp": nc.sync}
        self.sem = {e: self.es.enter_context(nc.semaphore("s_" + e)) for e in self.eng}
        self.cnt = {e: 0 for e in self.eng}
        self.known = {e: {} for e in self.eng}
        self.nsem = len(self.eng)
        self.n_ins = 0
        self.n_wait = 0
        self.uid = 0
        self.root_es = self.es
        self.scopes = []
        self.free_ev = {}

    def scope_enter(self):
        self.scopes.append((self.es, []))
        self.es = ExitStack()

    def scope_exit(self):
        outer, tts = self.scopes.pop()
        for t in tts:
            self._merge(self.free_ev, t.w)
            self._merge(self.free_ev, t.r)
        self.es.close()
        self.es = outer

    def sb(self, name, shape, dtype=F32):
        self.uid += 1
        t = TT(self.es.enter_context(self.nc.sbuf_tensor(f"{name}_{self.uid}", list(shape), dtype)), name)
        t.w = dict(self.free_ev)
        t.r = dict(self.free_ev)
        if self.scopes:
            self.scopes[-1][1].append(t)
        return t

    def ps(self, name, shape, dtype=F32):
        self.uid += 1
        t = TT(self.root_es.enter_context(self.nc.psum_tensor(f"{name}_{self.uid}", list(shape), dtype)), name)
        t.psum = True
        return t

    def dram(self, name, shape, dtype=F32, kind="Internal"):
        return TT(self.nc.dram_tensor(name, list(shape), dtype, kind=kind).ap(), name)

    def newsem(self, name):
        self.nsem += 1
        return self.root_es.enter_context(self.nc.semaphore(f"{name}_{self.nsem}"))

    def _wait(self, e, waits):
        eng = self.eng[e]
        own = self.sem[e]
        kn = self.known[e]
        for s, v in waits.items():
            if e == "pe" and s is own:
                continue
            if kn.get(s, 0) >= v:
                continue
            eng.wait_ge(s, v)
            kn[s] = v
            self.n_wait += 1

    @staticmethod
    def _merge(d, src):
        for s, v in src.items():
            if d.get(s, 0) < v:
                d[s] = v

    def op(self, e, fn, reads=(), writes=()):
        waits = {}
        own = self.sem[e]
        for t in reads:
            self._merge(waits, t.w)
            if t.psum:
                for s_, v_ in t.r.items():
                    if s_ is not own and waits.get(s_, 0) < v_:
                        waits[s_] = v_
        for t in writes:
            for s_, v_ in t.w.items():
                if s_ is not own and waits.get(s_, 0) < v_:
                    waits[s_] = v_
            for s_, v_ in t.r.items():
                if s_ is not own and waits.get(s_, 0) < v_:
                    waits[s_] = v_
        self._wait(e, waits)
        ins = fn(self.eng[e])
        self.cnt[e] += 1
        c = self.cnt[e]
        ins.then_inc(own, 1)
        self.n_ins += 1
        for t in writes:
            t.w[own] = c
        for t in reads:
            t.r[own] = c
        self._co_yield()
        return ins

    def group(self, name):
        return DmaGroup(self, name)

    def interleave(self, fns, weights=None):
        import threading
        n = len(fns)
        weights = weights or [1] * n
        st = {"turn": 0, "left": weights[0], "alive": [True] * n, "cv": threading.Condition(), "ids": {}, "w": weights, "err": None}
        self._co = st

        def advance():
            k = st["turn"]
            for _ in range(n):
                k = (k + 1) % n
                if st["alive"][k]:
                    break
            st["turn"] = k
            st["left"] = st["w"][k]

        st["advance"] = advance

        def runner(i):
            st["ids"][threading.get_ident()] = i
            with st["cv"]:
                while st["turn"] != i:
                    st["cv"].wait()
            try:
                fns[i]()
            except BaseException as ex:
                st["err"] = ex
            finally:
                with st["cv"]:
                    st["alive"][i] = False
                    if any(st["alive"]):
                        advance()
                    st["cv"].notify_all()

        ths = [threading.Thread(target=runner, args=(i,)) for i in range(n)]
        for t in ths:
            t.start()
        for t in ths:
            t.join()
        self._co = None
        if st["err"] is not None:
            raise st["err"]

    def _co_yield(self):
        st = getattr(self, "_co", None)
        if st is None:
            return
        import threading
        i = st["ids"].get(threading.get_ident())
        if i is None:
            return
        with st["cv"]:
            st["left"] -= 1
            if st["left"] > 0:
                return
            st["advance"]()
            st["cv"].notify_all()
            while st["turn"] != i:
                st["cv"].wait()

    def dma(self, q, out_t, out_ap, in_t, in_ap, grp=None, **kw):
        if grp is not None:
            waits = {}
            self._merge(waits, in_t.w)
            self._merge(waits, out_t.w)
            self._merge(waits, out_t.r)
            self._wait(q, waits)
            ins = self.eng[q].dma_start(out=out_ap, in_=in_ap, **kw)
            ins.then_inc(grp.sem, 16)
            grp.cnt += 16
            grp.outs.append(out_t)
            in_t.r[grp.sem] = grp.cnt
            out_t.w[grp.sem] = grp.cnt
            self.n_ins += 1
            return ins
        if out_t.dsem is None:
            out_t.dsem = self.newsem("d_" + out_t.name)
        waits = {}
        self._merge(waits, in_t.w)
        for s, v in out_t.w.items():
            if s is out_t.dsem:
                continue
            if waits.get(s, 0) < v:
                waits[s] = v
        self._merge(waits, out_t.r)
        self._wait(q, waits)
        ins = self.eng[q].dma_start(out=out_ap, in_=in_ap, **kw)
        ins.then_inc(out_t.dsem, 16)
        out_t.dcnt += 16
        out_t.w[out_t.dsem] = out_t.dcnt
        in_t.r[out_t.dsem] = out_t.dcnt
        self.n_ins += 1
        return ins

    def finish(self, outs, e="sp"):
        waits = {}
        for t in outs:
            self._merge(waits, t.w)
        self._wait(e, waits)


CONST_LAYOUT = {}


def _make_consts():
    parts = []
    off = 0

    def add(name, arr):
        nonlocal off
        arr = np.asarray(arr, np.float32)
        assert arr.shape[0] == 128
        CONST_LAYOUT[name] = (off, arr.shape[1])
        parts.append(arr)
        off += arr.shape[1]

    q = np.arange(128)
    add("ident", np.eye(128))
    add("mu_s", (q[:, None] < q[None, :]))
    add("mu_i", (q[:, None] <= q[None, :]))
    add("ml_s", (q[None, :] < q[:, None]))
    add("bones", (q[:, None] // 64 == q[None, :] // 64))
    add("hsel", (q[:, None] // 64 == np.arange(2)[None, :]))
    add("bd16", (q[:, None] // 16 == q[None, :] // 16))
    add("eo", np.stack([(q // 16) % 2 == 0, (q // 16) % 2 == 1], 1))
    add("i2", np.concatenate([np.eye(64), np.eye(64)], 0))
    add("iota", np.broadcast_to(np.arange(128)[None, :], (128, 128)))
    add("ones", np.ones((128, 128)))
    add("m96", (q[:, None] >= 96))
    add("rst8", np.broadcast_to((q % 8 != 0)[None, :], (128, 128)))
    add("sel16", (q[:, None] // 8 == np.arange(16)[None, :]))
    same = (q[:, None] // 8 == q[None, :] // 8)
    add("smu_s", same & (q[:, None] < q[None, :]))
    add("smu_i", same & (q[:, None] <= q[None, :]))
    add("sml_s", same & (q[None, :] < q[:, None]))
    return np.concatenate(parts, 1)


CONSTS = _make_consts()
NCONST = CONSTS.shape[1]


class Builder:
    def __init__(self, dbg=()):
        self.kb = KB()
        self.dbg = set(dbg)
        self.dbg_out = {}

    def C(self, name, rows=slice(0, 128)):
        o, n = CONST_LAYOUT[name]
        return self.consts[rows, o:o + n]

    def cast(self, eng, out_t, out_ap, in_t, in_ap):
        if eng == "act":
            return self.kb.op("act", lambda e: e.copy(out_ap, in_ap), reads=[in_t], writes=[out_t])
        return self.kb.op(eng, lambda e: e.tensor_copy(out_ap, in_ap), reads=[in_t], writes=[out_t])

    def dbg_dump(self, name, t, ap, shape, dtype=F32):
        if name not in self.dbg:
            return
        kb = self.kb
        o = kb.dram("dbg_" + name, shape, dtype, "ExternalOutput")
        kb.dma("sp", o, o[:], t, ap, grp=self.g_out)
        self.dbg_out[name] = o

    def build(self):
        kb = self.kb
        X = lambda n, s, dt=F32: kb.dram(n, s, dt, "ExternalInput")
        O = lambda n, s, dt=F32: kb.dram(n, s, dt, "ExternalOutput")
        self.I = I = {}
        I["xall"] = X("xall", [TT_, D])
        I["crep"] = X("crep", [2, 128, D])
        I["consts"] = X("consts", [128, NCONST])
        I["w_ada"] = X("w_ada", [D, 6 * D])
        I["b_ada"] = X("b_ada", [6 * D])
        I["norm1_g"] = X("norm1_g", [D])
        I["norm2_g"] = X("norm2_g", [D])
        I["w_in"] = X("w_in", [D, INC])
        I["w_out"] = X("w_out", [D, D])
        I["w_glu"] = X("w_glu", [512, 512])
        I["s5R"] = X("s5R", [128, 4, 5, 64])
        I["s5P"] = X("s5P", [64, 3, 32])
        I["s5PB"] = X("s5PB", [64, 4, 512])
        I["s5Q"] = X("s5Q", [128, 3, 16])
        I["s5QC"] = X("s5QC", [128, 2, 16, 16])
        I["s5st"] = X("s5st", [128, 2, 16, 16])
        I["colp"] = X("colp", [128, 80])
        I["rowp"] = X("rowp", [3, 512])
        I["w2"] = X("w2", [64, 512])
        I["a2"] = X("a2", [64, 512])
        I["g2"] = X("g2", [128, 512])
        I["shift0"] = X("shift0", [16, 1792])
        I["wkv0"] = X("wkv0", [128, 4096])
        I["w_q"] = X("w_q", [D, D])
        I["keys1"] = X("keys1", [8, 128, 64])
        I["keys2"] = X("keys2", [8, 128, 64])
        I["peer_u"] = X("peer_u", [16384, D])
        I["peer_v"] = X("peer_v", [16384, D])
        I["fng"] = X("fng", [D])
        self.Oy = O("y", [TT_, D])
        self.O = {}
        self.O["s5p"] = O("s5p", [2, 16, 128])
        self.O["s5s"] = O("s5s", [2, 16, 16, 128])
        self.O["wkvp"] = O("wkvp", [8, 64, 64])
        self.O["wkvs"] = O("wkvs", [128, 4096])
        self.O["shp"] = O("shp", [14, 128])
        self.O["shs"] = O("shs", [16, 1792])
        self.x1s = kb.dram("x1s", [TT_, D], F32)
        self.g_out = kb.group("out")
        self.g_x1 = kb.group("x1")

        self.setup_common()
        kb.scope_enter()
        self.tab_alloc()
        kb.scope_enter()
        self.setup()
        nblk = NPB + 1
        for f_ in self.dbg:
            if f_.startswith("nblk"):
                nblk = int(f_[4:])

        def blocks():
            for blk in range(nblk):
                self.block(blk)
        tw = [12, 1]
        for f_ in self.dbg:
            if f_.startswith("tw"):
                tw = [int(x) for x in f_[2:].split("_")]
        if "no_p2" in self.dbg:
            blocks()
        elif "tabserial" in self.dbg or nblk == 0:
            blocks()
            self.tab_prep()
        else:
            kb.interleave([blocks, self.tab_prep], weights=tw)
        kb.scope_exit()
        kb.scope_exit()
        self.g_x1.close()
        self.dbg_dump("x1", self.x1s, self.x1s[:], [TT_, D])
        kb.scope_enter()
        if "no_p2" not in self.dbg:
            self.phase2()
        kb.scope_exit()
        self.g_out.close()
        kb.finish([self.Oy] + list(self.O.values()) + list(self.dbg_out.values()))
        return kb.nc

    def setup_common(self):
        kb = self.kb
        I = self.I
        g = kb.group("par")
        self.consts = kb.sb("consts", [128, NCONST])
        kb.dma("sp", self.consts, self.consts[:], I["consts"], I["consts"][:], grp=g)
        self.colp = kb.sb("colp", [128, 80])
        kb.dma("sp", self.colp, self.colp[:], I["colp"], I["colp"][:], grp=g)
        g.close()
        kb.op("dve", lambda e: e.tensor_scalar(self.colp[:, 14:28], self.colp[:, 0:14], -1.0, 1.0, ALU.mult, ALU.add), reads=[self.colp], writes=[self.colp])
        kb.op("dve", lambda e: e.tensor_scalar(self.colp[:, 56:60], self.colp[:, 52:56], -1.0, 1.0, ALU.mult, ALU.add), reads=[self.colp], writes=[self.colp])
        kb.op("dve", lambda e: e.tensor_scalar_mul(self.colp[:, 72:76], self.colp[:, 40:44], -1.0), reads=[self.colp], writes=[self.colp])
        self.gneps = kb.sb("gneps", [128, 1])
        kb.op("dve", lambda e: e.memset(self.gneps[:], 64e-5), writes=[self.gneps])
        self.identb = kb.sb("identb", [128, 128], BF16)
        kb.op("dve", lambda e: e.tensor_copy(self.identb[:], self.C("ident")), reads=[self.consts], writes=[self.identb])
        self.eps6 = kb.sb("eps6", [128, 1])
        kb.op("dve", lambda e: e.memset(self.eps6[:], 1e-6), writes=[self.eps6])
        self.pt = kb.ps("pt", [128, 1024], BF16)
        self.pb = [kb.ps(f"pb{i}", [128, 512], F32) for i in range(7)]


    def setup(self):
        kb = self.kb
        I = self.I
        self.wins = kb.dram("wins", [18, 128, 8, 128], BF16)
        self.g_win = kb.group("win")
        self.wcs = [kb.sb(f"wcs{i}", [128, 8, 128], BF16) for i in range(4)]
        self.woutb = kb.sb("woutb", [128, 8, D], BF16)
        self.wglub = kb.sb("wglub", [128, 4, 512], BF16)
        self.EW = kb.sb("EW", [128, 4, 2, 8, 128], BF16)
        self.KT = kb.sb("KT", [128, 4, 8, 128], BF16)
        self.pwr = kb.sb("pwr", [128, 13, 16])
        self.pwi = kb.sb("pwi", [128, 13, 16])
        self.CW = kb.sb("CW", [128, 16, 8, 2, 32], BF16)
        self.CWz = kb.sb("CWz", [128, 4, 8, 2, 64], BF16)
        self.s5c = kb.sb("s5c", [128, 2, 16])
        self.s5st = kb.sb("s5st", [128, 2, 16, 16])
        self.w2b = kb.sb("w2b", [128, 512], BF16)
        self.a2b = kb.sb("a2b", [128, 512], BF16)
        self.g2b = kb.sb("g2b", [128, 512], BF16)
        self.Zf = kb.sb("Zf", [128, 4, 64])
        self.Zb = kb.sb("Zb", [128, 4, 64], BF16)
        self.carry = kb.sb("carry", [128, 14])
        self.shiftT = kb.sb("shiftT", [128, 14, 16])
        self.shsT = kb.sb("shsT", [128, 14, 16])
        self.mod = [kb.sb(f"mod{n}", [128, D]) for n in range(3)]
        self.mod2s = kb.dram("mod2s", [2, 6, 128, D], F32)
        self.g_mod2 = [kb.group("mod2a"), kb.group("mod2b")]
        kb.scope_enter()
        g = kb.group("par2")
        g1 = kb.sb("g1", [128, D])
        kb.dma("sp", g1, g1[:], I["norm1_g"], I["norm1_g"][:].partition_broadcast(128), grp=g)
        g2n = kb.sb("g2n", [128, D])
        kb.dma("sp", g2n, g2n[:], I["norm2_g"], I["norm2_g"][:].partition_broadcast(128), grp=g)
        crep = kb.sb("crep", [128, 2, D])
        kb.dma("sp", crep, crep[:], I["crep"], I["crep"][:].rearrange("t p d -> p t d"), grp=g)
        g.close()
        bada = kb.sb("bada", [128, D])
        mtmp = [kb.sb(f"mtmp{t}", [128, D]) for t in range(2)]
        csil = kb.sb("csil", [128, 2, D], BF16)
        kb.op("act", lambda e: e.activation(csil[:], crep[:], AF.Silu), reads=[crep], writes=[csil])
        cT = kb.sb("cT", [128, 2, 8, 128], BF16)
        for typ in range(2):
            for c in range(8):
                kb.op("pe", lambda e, c=c, typ=typ: e.transpose(self.pt[:, c * 128:(c + 1) * 128], csil[:, typ, c * 128:(c + 1) * 128], self.identb[:]),
                      reads=[csil, self.identb], writes=[self.pt])
            kb.op("dve", lambda e, typ=typ: e.tensor_copy(cT[:, typ, :, :].rearrange("p c n -> p (c n)"), self.pt[:]), reads=[self.pt], writes=[cT])
        wst = [kb.sb(f"wst{i}", [128, 1024]) for i in range(3)]
        wsb = [kb.sb(f"wsb{i}", [128, 1024], BF16) for i in range(3)]
        it = 0
        for n in range(6):
            kb.dma("sp", bada, bada[:], I["b_ada"], I["b_ada"][n * 1024:(n + 1) * 1024].partition_broadcast(128))
            for kc in range(8):
                s = it % 3
                it += 1
                kb.dma("sp", wst[s], wst[s][:], I["w_ada"], I["w_ada"][kc * 128:(kc + 1) * 128, n * 1024:(n + 1) * 1024])
                self.cast("act" if it % 2 else "dve", wsb[s], wsb[s][:], wst[s], wst[s][:])
                for typ in range(2):
                    for hf in range(2):
                        kb.op("pe", lambda e, s=s, typ=typ, hf=hf, kc=kc: e.matmul(self.pb[typ * 2 + hf][:], cT[:, typ, kc, :], wsb[s][:, hf * 512:(hf + 1) * 512], start=(kc == 0), stop=(kc == 7)),
                              reads=[cT, wsb[s]], writes=[self.pb[typ * 2 + hf]])
            for typ in range(2):
                res = (typ == 0 and n < 3)
                m = self.mod[n] if res else mtmp[typ]
                for hf in range(2):
                    sl = slice(hf * 512, (hf + 1) * 512)
                    kb.op("dve", lambda e, m=m, typ=typ, hf=hf, sl=sl, n=n: e.tensor_tensor(m[:, sl], self.pb[typ * 2 + hf][:], bada[:, sl], ALU.add),
                          reads=[self.pb[typ * 2 + hf], bada], writes=[m])
                if n in (1, 4):
                    gg = g1 if n == 1 else g2n
                    kb.op("pool", lambda e, m=m: e.tensor_scalar_add(m[:], m[:], 1.0), reads=[m], writes=[m])
                    kb.op("pool", lambda e, m=m, gg=gg: e.tensor_mul(m[:], m[:], gg[:]), reads=[m, gg], writes=[m])
                if not res:
                    kb.dma("sp", self.mod2s, self.mod2s[typ, n], m, m[:], grp=self.g_mod2[typ])
        for g_ in self.g_mod2:
            g_.close()
        kb.scope_exit()
        kb.scope_enter()
        if "su1" in self.dbg:
            kb.scope_exit()
            return
        wst2 = [kb.sb(f"wst2{i}", [128, INC]) for i in range(2)]
        wsb2 = [kb.sb(f"wsb2{i}", [128, INC], BF16) for i in range(2)]
        for kc in range(8):
            s = kc % 2
            kb.dma("sp", wst2[s], wst2[s][:], I["w_in"], I["w_in"][kc * 128:(kc + 1) * 128, :])
            self.cast("act" if kc % 2 else "dve", wsb2[s], wsb2[s][:], wst2[s], wst2[s][:])
            kb.dma("sp", self.wins, self.wins[:, :, kc, :].rearrange("c p n -> p c n"), wsb2[s], wsb2[s][:].rearrange("p (c n) -> p c n", c=18), grp=self.g_win)
        for kc in range(8):
            s = kc % 2
            kb.dma("sp", wst2[s], wst2[s][:, 0:D], I["w_out"], I["w_out"][kc * 128:(kc + 1) * 128, :])
            self.cast("act" if kc % 2 else "dve", self.woutb, self.woutb[:, kc, :], wst2[s], wst2[s][:, 0:D])
        for kc in range(4):
            s = kc % 2
            kb.dma("sp", wst2[s], wst2[s][:, 0:512], I["w_glu"], I["w_glu"][kc * 128:(kc + 1) * 128, :])
            self.cast("act" if kc % 2 else "dve", self.wglub, self.wglub[:, kc, :], wst2[s], wst2[s][:, 0:512])
        kb.scope_exit()
        self.g_win.close()
        if "su2" in self.dbg:
            return
        kb.scope_enter()
        self.setup_s5()
        kb.scope_exit()
        if "su3" in self.dbg:
            return
        self.setup_rwkv()
        kb.op("dve", lambda e: e.memset(self.carry[:], 0.0), writes=[self.carry])

    def s5_lambda(self, name, are, aim, ldt, shape, np_, srcs):
        kb = self.kb
        F = shape[1]
        mk = lambda n: kb.sb(f"{name}_{n}", [128, F])
        dt, mag, ang, t1, t2, fi = mk("dt"), mk("mag"), mk("ang"), mk("t1"), mk("t2"), kb.sb(f"{name}_fi", [128, F], I32)
        lbr, lbi, cfr, cfi = mk("lbr"), mk("lbi"), mk("cfr"), mk("cfi")
        P = slice(0, np_)
        kb.op("act", lambda e: e.activation(dt[P], ldt, AF.Exp), reads=srcs, writes=[dt])
        kb.op("dve", lambda e: e.tensor_tensor(mag[P], are, dt[P], ALU.mult), reads=srcs + [dt], writes=[mag])
        kb.op("act", lambda e: e.activation(mag[P], mag[P], AF.Exp), reads=[mag], writes=[mag])
        kb.op("dve", lambda e: e.tensor_tensor(ang[P], aim, dt[P], ALU.mult), reads=srcs + [dt], writes=[ang])

        def sin_of(dst, shift):
            kb.op("dve", lambda e: e.tensor_scalar(t1[P], ang[P], 1.0 / (2 * np.pi), 0.5 + shift / (2 * np.pi), ALU.mult, ALU.add), reads=[ang], writes=[t1])
            kb.op("dve", lambda e: e.tensor_copy(fi[P], t1[P]), reads=[t1], writes=[fi])
            kb.op("dve", lambda e: e.tensor_copy(t2[P], fi[P]), reads=[fi], writes=[t2])
            kb.op("dve", lambda e: e.tensor_tensor(t1[P], t1[P], t2[P], ALU.subtract), reads=[t1, t2], writes=[t1])
            kb.op("dve", lambda e: e.tensor_single_scalar(t2[P], t1[P], 0.0, ALU.is_lt), reads=[t1], writes=[t2])
            kb.op("dve", lambda e: e.tensor_tensor(t1[P], t1[P], t2[P], ALU.add), reads=[t1, t2], writes=[t1])
            kb.op("dve", lambda e: e.tensor_scalar(t1[P], t1[P], 2 * np.pi, -np.pi, ALU.mult, ALU.add), reads=[t1], writes=[t1])
            kb.op("dve", lambda e: e.tensor_scalar(t1[P], t1[P], -np.pi, np.pi, ALU.max, ALU.min), reads=[t1], writes=[t1])
            kb.op("act", lambda e: e.activation(dst[P], t1[P], AF.Sin), reads=[t1], writes=[dst])

        sin_of(lbi, 0.0)
        sin_of(lbr, np.pi / 2)
        kb.op("dve", lambda e: e.tensor_tensor(lbr[P], lbr[P], mag[P], ALU.mult), reads=[lbr, mag], writes=[lbr])
        kb.op("dve", lambda e: e.tensor_tensor(lbi[P], lbi[P], mag[P], ALU.mult), reads=[lbi, mag], writes=[lbi])
        den, nre = mk("den"), mk("nre")
        kb.op("dve", lambda e: e.tensor_tensor(den[P], are, are, ALU.mult), reads=srcs, writes=[den])
        kb.op("dve", lambda e: e.tensor_tensor(t1[P], aim, aim, ALU.mult), reads=srcs, writes=[t1])
        kb.op("dve", lambda e: e.tensor_tensor(den[P], den[P], t1[P], ALU.add), reads=[den, t1], writes=[den])
        kb.op("dve", lambda e: e.reciprocal(den[P], den[P]), reads=[den], writes=[den])
        kb.op("dve", lambda e: e.tensor_scalar_add(nre[P], lbr[P], -1.0), reads=[lbr], writes=[nre])
        kb.op("dve", lambda e: e.tensor_tensor(t1[P], nre[P], are, ALU.mult), reads=srcs + [nre], writes=[t1])
        kb.op("dve", lambda e: e.tensor_tensor(t2[P], lbi[P], aim, ALU.mult), reads=srcs + [lbi], writes=[t2])
        kb.op("dve", lambda e: e.tensor_tensor(t1[P], t1[P], t2[P], ALU.add), reads=[t1, t2], writes=[t1])
        kb.op("dve", lambda e: e.tensor_tensor(cfr[P], t1[P], den[P], ALU.mult), reads=[t1, den], writes=[cfr])
        kb.op("dve", lambda e: e.tensor_tensor(t1[P], lbi[P], are, ALU.mult), reads=srcs + [lbi], writes=[t1])
        kb.op("dve", lambda e: e.tensor_tensor(t2[P], nre[P], aim, ALU.mult), reads=srcs + [nre], writes=[t2])
        kb.op("dve", lambda e: e.tensor_tensor(t1[P], t1[P], t2[P], ALU.subtract), reads=[t1, t2], writes=[t1])
        kb.op("dve", lambda e: e.tensor_tensor(cfi[P], t1[P], den[P], ALU.mult), reads=[t1, den], writes=[cfi])
        return lbr, lbi, cfr, cfi

    def cmul(self, P, outr, outi, ar, ai, br, bi, tmp, rd, wr, eng="dve"):
        kb = self.kb
        t1, t2 = tmp
        kb.op(eng, lambda e: e.tensor_tensor(t1, ar, br, ALU.mult), reads=rd, writes=wr)
        kb.op(eng, lambda e: e.tensor_tensor(t2, ai, bi, ALU.mult), reads=rd, writes=wr)
        kb.op(eng, lambda e: e.tensor_tensor(t1, t1, t2, ALU.subtract), reads=rd, writes=wr)
        kb.op(eng, lambda e: e.tensor_tensor(t2, ar, bi, ALU.mult), reads=rd, writes=wr)
        kb.op(eng, lambda e: e.tensor_tensor(outi, ai, br, ALU.mult), reads=rd, writes=wr)
        kb.op(eng, lambda e: e.tensor_tensor(outi, outi, t2, ALU.add), reads=rd, writes=wr)
        kb.op(eng, lambda e: e.tensor_copy(outr, t1), reads=rd, writes=wr)

    def setup_s5(self):
        kb = self.kb
        I = self.I
        g = kb.group("s5par")
        sR = kb.sb("sR", [128, 4, 5, 64])
        kb.dma("sp", sR, sR[:], I["s5R"], I["s5R"][:], grp=g)
        sP = kb.sb("sP", [128, 3, 32])
        kb.dma("sp", sP, sP[0:64], I["s5P"], I["s5P"][:], grp=g)
        sPB = kb.sb("sPB", [128, 4, 512])
        kb.dma("sp", sPB, sPB[0:64], I["s5PB"], I["s5PB"][:], grp=g)
        sQ = kb.sb("sQ", [128, 3, 16])
        kb.dma("sp", sQ, sQ[:], I["s5Q"], I["s5Q"][:], grp=g)
        sQC = kb.sb("sQC", [128, 2, 16, 16])
        kb.dma("sp", sQC, sQC[:], I["s5QC"], I["s5QC"][:], grp=g)
        kb.dma("sp", self.s5st, self.s5st[:], I["s5st"], I["s5st"][:], grp=g)
        g.close()
        def chainR():
            rr = kb.sb("rr", [128, 5, 256])
            for k in range(5):
                kb.op("dve", lambda e, k=k: e.tensor_copy(rr[:, k, :].rearrange("p (t q) -> p t q", t=4), sR[:, :, k, :]), reads=[sR], writes=[rr])
            lbr, lbi, cfr, cfi = self.s5_lambda("R", rr[:, 0, :], rr[:, 1, :], rr[:, 2, :], [128, 256], 128, [rr])
            cur_r, cur_i = kb.sb("curRr", [128, 256]), kb.sb("curRi", [128, 256])
            ta, tb = kb.sb("taR", [128, 256]), kb.sb("tbR", [128, 256])
            allr = [rr, lbr, lbi, cfr, cfi, cur_r, cur_i, ta, tb]
            self.cmul(None, cur_r[:], cur_i[:], cfr[:], cfi[:], rr[:, 3, :], rr[:, 4, :], (ta[:], tb[:]), allr, [cur_r, cur_i, ta, tb])
            eo = self.C("eo")
            for d in range(8):
                i = 7 - d
                for c, cur in enumerate((cur_r, cur_i)):
                    for g2 in range(2):
                        kb.op("pool", lambda e, c=c, cur=cur, g2=g2, i=i: e.tensor_scalar(
                            self.EW[:, :, c, i, g2 * 64:(g2 + 1) * 64], cur[:].rearrange("p (t q) -> p t q", t=4), eo[:, g2:g2 + 1], None, ALU.mult),
                            reads=[cur, self.consts], writes=[self.EW])
                if d < 7:
                    self.cmul(None, cur_r[:], cur_i[:], cur_r[:], cur_i[:], lbr[:], lbi[:], (ta[:], tb[:]), allr, [cur_r, cur_i, ta, tb])

        def chainP():
            P64 = slice(0, 64)
            lbrP, lbiP, cfrP, cfiP = self.s5_lambda("P", sP[P64, 0, :], sP[P64, 1, :], sP[P64, 2, :], [64, 32], 64, [sP])
            cpr, cpi = kb.sb("cpr", [128, 512]), kb.sb("cpi", [128, 512])
            tpa, tpb = kb.sb("tpa", [128, 512]), kb.sb("tpb", [128, 512])
            nci = kb.sb("nci", [128, 512])
            allp = [sPB, lbrP, lbiP, cfrP, cfiP, cpr, cpi, tpa, tpb]
            bc = lambda t: t[P64, :].to_broadcast([64, 32, 16]) if False else t[P64, :].unsqueeze(2).to_broadcast([64, 32, 16])
            v3 = lambda ap: ap.rearrange("p (g h) -> p g h", h=16)
            self.cmul(None, v3(cpr[P64]), v3(cpi[P64]), bc(cfrP), bc(cfiP), v3(sPB[P64, 0, :]), v3(sPB[P64, 1, :]), (v3(tpa[P64]), v3(tpb[P64])), allp, [cpr, cpi, tpa, tpb])
            kb.op("dve", lambda e: e.tensor_scalar_mul(nci[P64], sPB[P64, 3, :], -1.0), reads=[sPB], writes=[nci])
            dsk = self.colp[:, 36:40]
            for d in range(8):
                for t in range(4):
                    sl = slice(t * 128, (t + 1) * 128)
                    pk = self.pb[4 + (t % 2)]
                    kb.op("pe", lambda e, sl=sl, pk=pk: e.matmul(pk[:, 0:128], cpr[P64, sl], sPB[P64, 2, sl], start=True, stop=False), reads=[cpr, sPB], writes=[pk])
                    kb.op("pe", lambda e, sl=sl, pk=pk: e.matmul(pk[:, 0:128], cpi[P64, sl], nci[P64, sl], start=False, stop=True), reads=[cpi, nci], writes=[pk])
                    if d == 0:
                        kb.op("dve", lambda e, pk=pk: e.tensor_tensor(tpa[:, 0:128], pk[:, 0:128], self.C("bd16"), ALU.mult), reads=[pk, self.consts], writes=[tpa])
                        kb.op("dve", lambda e, t=t: e.scalar_tensor_tensor(self.KT[:, t, 0, :], self.C("ident"), dsk[:, t:t + 1], tpa[:, 0:128], ALU.mult, ALU.add),
                              reads=[tpa, self.consts, self.colp], writes=[self.KT])
                    else:
                        kb.op("dve", lambda e, pk=pk, t=t, d=d: e.tensor_tensor(self.KT[:, t, d, :], pk[:, 0:128], self.C("bd16"), ALU.mult), reads=[pk, self.consts], writes=[self.KT])
                if d < 7:
                    self.cmul(None, v3(cpr[P64]), v3(cpi[P64]), v3(cpr[P64]), v3(cpi[P64]), bc(lbrP), bc(lbiP), (v3(tpa[P64]), v3(tpb[P64])), allp, [cpr, cpi, tpa, tpb])

        def chainQ():
            lbrQ, lbiQ, _, _ = self.s5_lambda("Q", sQ[:, 0, :], sQ[:, 1, :], sQ[:, 2, :], [128, 16], 128, [sQ])
            tq = kb.sb("tq", [128, 2, 16])
            allq = [lbrQ, lbiQ, self.pwr, self.pwi, tq]
            kb.op("dve", lambda e: e.tensor_copy(self.pwr[:, 0, :], lbrQ[:]), reads=[lbrQ], writes=[self.pwr])
            kb.op("dve", lambda e: e.tensor_copy(self.pwi[:, 0, :], lbiQ[:]), reads=[lbiQ], writes=[self.pwi])
            for j in range(1, 8):
                self.cmul(None, self.pwr[:, j, :], self.pwi[:, j, :], self.pwr[:, j - 1, :], self.pwi[:, j - 1, :], lbrQ[:], lbiQ[:], (tq[:, 0, :], tq[:, 1, :]), allq, [self.pwr, self.pwi, tq])
            for j in range(8, 12):
                self.cmul(None, self.pwr[:, j, :], self.pwi[:, j, :], self.pwr[:, j - 1, :], self.pwi[:, j - 1, :], self.pwr[:, j - 1, :], self.pwi[:, j - 1, :], (tq[:, 0, :], tq[:, 1, :]), allq, [self.pwr, self.pwi, tq])
            kb.op("pool", lambda e: e.memset(self.CW[:], 0.0), writes=[self.CW])
            qa, qb, qc = kb.sb("qa", [128, 16, 16]), kb.sb("qb", [128, 16, 16]), kb.sb("qc", [128, 16, 16])
            pb_ = lambda t, j: t[:, j, :].unsqueeze(2).to_broadcast([128, 16, 16])
            for j in range(8):
                kb.op("dve", lambda e, j=j: e.tensor_tensor(qa[:], sQC[:, 0, :, :], pb_(self.pwr, j), ALU.mult), reads=[sQC, self.pwr], writes=[qa])
                kb.op("dve", lambda e, j=j: e.tensor_tensor(qb[:], sQC[:, 1, :, :], pb_(self.pwi, j), ALU.mult), reads=[sQC, self.pwi], writes=[qb])
                kb.op("dve", lambda e: e.tensor_tensor(qc[:], qa[:], qb[:], ALU.subtract), reads=[qa, qb], writes=[qc])
                for g2 in range(2):
                    Pq = slice(g2 * 64, (g2 + 1) * 64)
                    kb.op("dve", lambda e, j=j, g2=g2, Pq=Pq: e.tensor_copy(self.CW[Pq, :, j, 0, g2 * 16:(g2 + 1) * 16], qc[Pq]), reads=[qc], writes=[self.CW])
                kb.op("dve", lambda e, j=j: e.tensor_tensor(qa[:], sQC[:, 0, :, :], pb_(self.pwi, j), ALU.mult), reads=[sQC, self.pwi], writes=[qa])
                kb.op("dve", lambda e, j=j: e.tensor_tensor(qb[:], sQC[:, 1, :, :], pb_(self.pwr, j), ALU.mult), reads=[sQC, self.pwr], writes=[qb])
                kb.op("dve", lambda e: e.scalar_tensor_tensor(qc[:], qa[:], -1.0, qb[:], ALU.mult, ALU.subtract), reads=[qa, qb], writes=[qc])
                for g2 in range(2):
                    Pq = slice(g2 * 64, (g2 + 1) * 64)
                    kb.op("dve", lambda e, j=j, g2=g2, Pq=Pq: e.tensor_copy(self.CW[Pq, :, j, 1, g2 * 16:(g2 + 1) * 16], qc[Pq]), reads=[qc], writes=[self.CW])

        kb.interleave([chainR, chainP, chainQ])
        kb.op("pool", lambda e: e.memset(self.CWz[:], 0.0), writes=[self.CWz])
        for t in range(4):
            kb.op("pool", lambda e, t=t: e.tensor_copy(self.CWz[:, t, :, :, 32:64], self.CW[:, 4 * t + 3, :, :, :]), reads=[self.CW], writes=[self.CWz])
        kb.op("dve", lambda e: e.memset(self.s5c[:], 0.0), writes=[self.s5c])

    def setup_rwkv(self):
        kb = self.kb
        I = self.I
        kb.scope_enter()
        g = kb.group("rwpar")
        sh0 = kb.sb("sh0", [128, 1792])
        kb.dma("sp", sh0, sh0[0:16, :], I["shift0"], I["shift0"][:], grp=g)
        wl = kb.sb("wl", [128, 3, 512])
        kb.dma("sp", wl, wl[0:64, 0, :], I["w2"], I["w2"][:], grp=g)
        kb.dma("sp", wl, wl[64:128, 1, :], I["a2"], I["a2"][:], grp=g)
        kb.dma("sp", wl, wl[:, 2, :], I["g2"], I["g2"][:], grp=g)
        g.close()
        kb.op("dve", lambda e: e.tensor_copy(self.w2b[0:64, :], wl[0:64, 0, :]), reads=[wl], writes=[self.w2b])
        kb.op("dve", lambda e: e.tensor_copy(self.a2b[64:128, :], wl[64:128, 1, :]), reads=[wl], writes=[self.a2b])
        kb.op("dve", lambda e: e.tensor_copy(self.g2b[:], wl[:, 2, :]), reads=[wl], writes=[self.g2b])
        kb.op("dve", lambda e: e.memset(self.Zf[:], 0.0), writes=[self.Zf])
        kb.op("dve", lambda e: e.memset(self.Zb[:], 0.0), writes=[self.Zb])
        pk = self.pb[6]
        for r in range(14):
            kb.op("pe", lambda e, r=r: e.transpose(pk[:, r * 16:(r + 1) * 16], sh0[0:16, r * 128:(r + 1) * 128], self.C("ident", slice(0, 16))[:, 0:16]),
                  reads=[sh0, self.consts], writes=[pk])
        kb.op("dve", lambda e: e.tensor_copy(self.shiftT[:].rearrange("p r b -> p (r b)"), pk[:, 0:224]), reads=[pk], writes=[self.shiftT])
        kb.scope_exit()

    def block(self, blk):
        kb = self.kb
        I = self.I
        sample = blk == NPB
        nb = TS if sample else NB
        ntile = nb // 128
        typ = 1 if sample else 0
        row0 = TP if sample else blk * NB
        kb.scope_enter()
        self.uT = kb.sb("uT", [128, 4, nb], BF16)
        self.psT = kb.sb("psT", [128, 14, nb])
        self.ycat = kb.sb("ycat", [128, 8, nb], BF16)
        kb.scope_enter()
        self.xblk = kb.sb("xblk", [128, 1, D])
        self.sqj = kb.sb("sqj", [128, D], BF16)
        self.ss = kb.sb("ss", [128, 2])
        self.rstd = kb.sb("rstd", [128, 2])
        self.t1k = kb.sb("t1k", [128, D])
        self.hb = kb.sb("hb", [128, D], BF16)
        self.hT = kb.sb("hT", [128, 8, nb], BF16)
        self.tsh = kb.sb("tsh", [128, nb])
        A1, SH1, GA1 = self.mod[1], self.mod[0], self.mod[2]
        if sample:
            for n in range(3):
                kb.dma("sp", self.mod[n], self.mod[n][:], self.mod2s, self.mod2s[1, n])
        for j in range(ntile):
            kb.dma("sp", self.xblk, self.xblk[:, 0, :], I["xall"], I["xall"][row0 + j * 128:row0 + (j + 1) * 128, :])
            kb.op("act", lambda e, j=j: e.activation(self.sqj[:], self.xblk[:, 0, :], AF.Square, accum_out=self.ss[:, j:j + 1]), reads=[self.xblk], writes=[self.sqj, self.ss])
            kb.op("act", lambda e, j=j: e.activation(self.rstd[:, j:j + 1], self.ss[:, j:j + 1], AF.Sqrt, bias=self.eps6[:], scale=1.0 / D), reads=[self.ss, self.eps6], writes=[self.rstd])
            kb.op("dve", lambda e, j=j: e.reciprocal(self.rstd[:, j:j + 1], self.rstd[:, j:j + 1]), reads=[self.rstd], writes=[self.rstd])
            kb.op("dve", lambda e, j=j: e.scalar_tensor_tensor(self.t1k[:], self.xblk[:, 0, :], self.rstd[:, j:j + 1], A1[:], ALU.mult, ALU.mult), reads=[self.xblk, self.rstd, A1], writes=[self.t1k])
            kb.op("dve", lambda e: e.tensor_tensor(self.hb[:], self.t1k[:], SH1[:], ALU.add), reads=[self.t1k, SH1], writes=[self.hb])
            for c in range(8):
                kb.op("pe", lambda e, c=c: e.transpose(self.pt[:, c * 128:(c + 1) * 128], self.hb[:, c * 128:(c + 1) * 128], self.identb[:]), reads=[self.hb, self.identb], writes=[self.pt])
            kb.op("act", lambda e, j=j: e.copy(self.hT[:, :, j * 128:(j + 1) * 128], self.pt[:].rearrange("p (c n) -> p c n", c=8)), reads=[self.pt], writes=[self.hT])
        wcs = self.wcs
        for cc in range(3):
            kb.dma("sp", wcs[cc % 4], wcs[cc % 4][:], self.wins, self.wins[cc])
        for cc in range(18):
            pk = self.pb[cc % 2]
            wc = wcs[cc % 4]
            if cc + 3 < 18:
                kb.dma("sp", wcs[(cc + 3) % 4], wcs[(cc + 3) % 4][:], self.wins, self.wins[cc + 3])
            for k in range(8):
                kb.op("pe", lambda e, cc=cc, k=k, pk=pk: e.matmul(pk[:, 0:nb], wc[:, k, :], self.hT[:, k, 0:nb], start=(k == 0), stop=(k == 7)),
                      reads=[wc, self.hT], writes=[pk])
            if cc < 4:
                kb.op("act", lambda e, cc=cc, pk=pk: e.copy(self.uT[:, cc, 0:nb], pk[:, 0:nb]), reads=[pk], writes=[self.uT])
            else:
                r = cc - 4
                mu = self.colp[:, r:r + 1]
                omm = self.colp[:, 14 + r:15 + r]
                kb.op("act", lambda e, pk=pk, omm=omm: e.activation(self.tsh[:, 0:nb], pk[:, 0:nb], AF.Identity, scale=omm), reads=[pk, self.colp], writes=[self.tsh])
                kb.op("dve", lambda e, pk=pk, r=r, mu=mu: e.scalar_tensor_tensor(self.psT[:, r, 1:nb], pk[:, 0:nb - 1], mu, self.tsh[:, 1:nb], ALU.mult, ALU.add),
                      reads=[pk, self.colp, self.tsh], writes=[self.psT])
                if not sample:
                    kb.op("dve", lambda e, r=r, mu=mu: e.scalar_tensor_tensor(self.psT[:, r, 0:1], self.carry[:, r:r + 1], mu, self.tsh[:, 0:1], ALU.mult, ALU.add),
                          reads=[self.carry, self.colp, self.tsh], writes=[self.psT])
                    kb.op("dve", lambda e, r=r, pk=pk: e.tensor_copy(self.carry[:, r:r + 1], pk[:, nb - 1:nb]), reads=[pk], writes=[self.carry])
                else:
                    kb.op("dve", lambda e, r=r, mu=mu: e.scalar_tensor_tensor(self.psT[:, r, 0:nb:8], self.shiftT[:, r, :], mu, self.tsh[:, 0:nb:8], ALU.mult, ALU.add),
                          reads=[self.shiftT, self.colp, self.tsh], writes=[self.psT])
                    kb.op("dve", lambda e, r=r, pk=pk: e.tensor_copy(self.shsT[:, r, :], pk[:, 7:nb:8]), reads=[pk], writes=[self.shsT])
        kb.scope_exit()
        if blk == 0:
            self.dbg_dump("uT0", self.uT, self.uT[:], [128, 4, NB], BF16)
            self.dbg_dump("psT0", self.psT, self.psT[:], [128, 14, NB])
        kb.scope_enter()
        if "skip_s5" not in self.dbg:
            self.s5_block(blk, sample, nb)
        kb.scope_exit()
        kb.scope_enter()
        if "skip_rw" not in self.dbg:
            self.rwkv_block(blk, sample, nb)
        kb.scope_exit()
        kb.scope_enter()
        self.outproj(blk, sample, nb, ntile, row0, GA1)
        kb.scope_exit()
        kb.scope_exit()

    def s5_block(self, blk, sample, nb):
        kb = self.kb
        nsb = nb // 8
        if True:
            self.Ea = kb.sb("Ea", [128, 2, 16, 32])
            self.Eb = kb.sb("Eb", [128, 2, 16, 32])
            self.hst = kb.sb("hst", [128, 2, 16, 32])
            self.Cin = kb.sb("Cin", [128, 2, 16, 32], BF16)
            self.ypre = kb.sb("ypre", [128, 4, NB])
            self.ygb = kb.sb("ygb", [128, 4, NB], BF16)
            self.sig = kb.sb("sig", [128, NB])
            self.s5tmp = kb.sb("s5tmp", [128, 4, 16])
        uT, EW, KT, CW, CWz = self.uT, self.EW, self.KT, self.CW, self.CWz
        if True:
            self.uTz = kb.sb("uTz", [128, 4, NB], BF16)
        uTz = self.uTz
        kb.op("pool", lambda e: e.tensor_scalar(uTz[64:128, :, 0:nb], uT[64:128, :, 0:nb], self.C("m96", slice(64, 128)), None, ALU.mult), reads=[uT, self.consts], writes=[uTz])
        pE = [self.pb[2], self.pb[3]]
        for pair in range(16):
            tile, r0 = pair // 4, 32 * (pair % 4)
            kk_ = 32
            src = uT
            if pair % 4 == 3:
                r0, kk_, src = 64, 64, uTz
            for c in range(2):
                for i in range(8):
                    kb.op("pe", lambda e, pair=pair, tile=tile, r0=r0, c=c, i=i, kk_=kk_, src=src: e.matmul(
                        pE[c][:, pair * nsb:(pair + 1) * nsb], EW[r0:r0 + kk_, tile, c, i, :], src[r0:r0 + kk_, tile, i:nb:8], start=(i == 0), stop=(i == 7)),
                        reads=[EW, src], writes=[pE[c]])
        A, B = self.Ea, self.Eb
        for c in range(2):
            kb.op("act", lambda e, c=c: e.copy(A[:, c, :, 0:nsb], pE[c][:, 0:16 * nsb].rearrange("p (a b) -> p a b", a=16)), reads=[pE[c]], writes=[A])
        tt = lambda out, a, b, op, rd, wr: kb.op("dve", lambda e: e.tensor_tensor(out, a, b, op), reads=rd, writes=wr)
        pw, pwi = self.pwr, self.pwi
        tmp = self.s5tmp
        if not sample:
            cr, ci = self.s5c[:, 0, :], self.s5c[:, 1, :]
            rd = [pw, pwi, self.s5c, tmp, A]
            tt(tmp[:, 0, :], pw[:, 7, :], cr, ALU.mult, rd, [tmp])
            tt(tmp[:, 1, :], pwi[:, 7, :], ci, ALU.mult, rd, [tmp])
            tt(tmp[:, 2, :], pw[:, 7, :], ci, ALU.mult, rd, [tmp])
            tt(tmp[:, 3, :], pwi[:, 7, :], cr, ALU.mult, rd, [tmp])
            tt(tmp[:, 0, :], tmp[:, 0, :], tmp[:, 1, :], ALU.subtract, rd, [tmp])
            tt(tmp[:, 2, :], tmp[:, 2, :], tmp[:, 3, :], ALU.add, rd, [tmp])
            tt(A[:, 0, :, 0], A[:, 0, :, 0], tmp[:, 0, :], ALU.add, rd, [A])
            tt(A[:, 1, :, 0], A[:, 1, :, 0], tmp[:, 2, :], ALU.add, rd, [A])
            sft, k = 1, 0
            while sft < nsb:
                n = nsb - sft
                bcr = pw[:, 7 + k, :].unsqueeze(2).to_broadcast([128, 16, n])
                bci = pwi[:, 7 + k, :].unsqueeze(2).to_broadcast([128, 16, n])
                T = self.hst
                rd = [A, pw, pwi, T]
                tt(T[:, 0, :, 0:n], A[:, 0, :, 0:n], bcr, ALU.mult, rd, [T])
                tt(T[:, 1, :, 0:n], A[:, 1, :, 0:n], bci, ALU.mult, rd, [T])
                tt(T[:, 0, :, 0:n], T[:, 0, :, 0:n], T[:, 1, :, 0:n], ALU.subtract, rd, [T])
                tt(B[:, 0, :, sft:nsb], A[:, 0, :, sft:nsb], T[:, 0, :, 0:n], ALU.add, rd, [B])
                tt(T[:, 0, :, 0:n], A[:, 0, :, 0:n], bci, ALU.mult, rd, [T])
                tt(T[:, 1, :, 0:n], A[:, 1, :, 0:n], bcr, ALU.mult, rd, [T])
                tt(T[:, 0, :, 0:n], T[:, 0, :, 0:n], T[:, 1, :, 0:n], ALU.add, rd, [T])
                tt(B[:, 1, :, sft:nsb], A[:, 1, :, sft:nsb], T[:, 0, :, 0:n], ALU.add, rd, [B])
                kb.op("pool", lambda e, A=A, B=B, sft=sft: e.tensor_copy(B[:, :, :, 0:sft], A[:, :, :, 0:sft]), reads=[A], writes=[B])
                A, B = B, A
                sft *= 2
                k += 1
            kb.op("pool", lambda e: e.tensor_copy(self.Cin[:, :, :, 0], self.s5c[:]), reads=[self.s5c], writes=[self.Cin])
            kb.op("pool", lambda e, A=A: e.tensor_copy(self.Cin[:, :, :, 1:nsb], A[:, :, :, 0:nsb - 1]), reads=[A], writes=[self.Cin])
            kb.op("dve", lambda e, A=A: e.tensor_copy(self.s5c[:], A[:, :, :, nsb - 1]), reads=[A, self.Cin], writes=[self.s5c])
            if blk == NPB - 1:
                pk = self.pb[4]
                for c in range(2):
                    kb.op("pe", lambda e, c=c: e.transpose(pk[0:16, c * 128:(c + 1) * 128], self.s5c[:, c, :], self.C("ident")), reads=[self.s5c, self.consts], writes=[pk])
                kb.op("dve", lambda e: e.tensor_copy(self.hst[0:16, 0, 0, 0:256] if False else self.sig[0:16, 0:256], pk[0:16, 0:256]), reads=[pk], writes=[self.sig])
                kb.dma("sp", self.O["s5p"], self.O["s5p"][:].rearrange("c a q -> a c q"), self.sig, self.sig[0:16, 0:256].rearrange("a (c q) -> a c q", c=2), grp=self.g_out)
        else:
            st = self.s5st
            kb.op("pool", lambda e: e.tensor_copy(self.Cin[:, :, :, 0:16], st[:]), reads=[st], writes=[self.Cin])
            F_ = B
            bcr = pw[:, 7, :].unsqueeze(2).to_broadcast([128, 16, 16])
            bci = pwi[:, 7, :].unsqueeze(2).to_broadcast([128, 16, 16])
            T = self.hst
            rd = [A, pw, pwi, T, st]
            tt(T[:, 0, :, 0:16], st[:, 0], bcr, ALU.mult, rd, [T])
            tt(T[:, 1, :, 0:16], st[:, 1], bci, ALU.mult, rd, [T])
            tt(T[:, 0, :, 0:16], T[:, 0, :, 0:16], T[:, 1, :, 0:16], ALU.subtract, rd, [T])
            tt(F_[:, 0, :, 0:16], A[:, 0, :, 0:16], T[:, 0, :, 0:16], ALU.add, rd, [F_])
            tt(T[:, 0, :, 0:16], st[:, 0], bci, ALU.mult, rd, [T])
            tt(T[:, 1, :, 0:16], st[:, 1], bcr, ALU.mult, rd, [T])
            tt(T[:, 0, :, 0:16], T[:, 0, :, 0:16], T[:, 1, :, 0:16], ALU.add, rd, [T])
            tt(F_[:, 1, :, 0:16], A[:, 1, :, 0:16], T[:, 0, :, 0:16], ALU.add, rd, [F_])
            pk = self.pb[4]
            for c in range(2):
                for q4 in range(4):
                    for a in range(4):
                        kb.op("pe", lambda e, c=c, q4=q4, a=a: e.transpose(pk[0:16, a * 128:(a + 1) * 128], F_[:, c, q4 * 4 + a, 0:16], self.C("ident")), reads=[F_, self.consts], writes=[pk])
                    kb.op("dve", lambda e: e.tensor_copy(self.ypre[0:16, 0, 0:512] if False else self.ypre[0:16, 0:2, :].rearrange("p a b -> p (a b)"), pk[0:16, 0:512]), reads=[pk], writes=[self.ypre])
                    kb.dma("sp", self.O["s5s"], self.O["s5s"][c, :, q4 * 4:(q4 + 1) * 4, :], self.ypre, self.ypre[0:16, 0:2, :].rearrange("p a (b q) -> p (a b) q", q=128), grp=self.g_out)
        Cin = self.Cin
        for tile in range(4):
            pY = [self.pb[4], self.pb[1]][tile % 2]
            for j in range(8):
                osl = slice(j * nsb, (j + 1) * nsb)
                for i in range(j + 1):
                    kb.op("pe", lambda e, tile=tile, j=j, i=i, osl=osl, pY=pY: e.matmul(pY[:, osl], KT[:, tile, j - i, :], uT[:, tile, i:nb:8], start=(i == 0), stop=False),
                          reads=[KT, uT], writes=[pY])
                for pl in range(4):
                    pair = tile * 4 + pl
                    for c in range(2):
                        last = (pl == 3 and c == 1)
                        if pl < 3:
                            kb.op("pe", lambda e, pair=pair, pl=pl, j=j, c=c, osl=osl, pY=pY, last=last: e.matmul(
                                pY[32 * pl:32 * pl + 32, osl], CW[:, pair, j, c, :], Cin[:, c, pair, 0:nsb], start=False, stop=last),
                                reads=[CW, Cin], writes=[pY])
                        else:
                            kb.op("pe", lambda e, pair=pair, tile=tile, j=j, c=c, osl=osl, pY=pY, last=last: e.matmul(
                                pY[64:128, osl], CWz[:, tile, j, c, :], Cin[:, c, pair, 0:nsb], start=False, stop=last),
                                reads=[CWz, Cin], writes=[pY])
            kb.op("act", lambda e, tile=tile, pY=pY: e.copy(self.ypre[:, tile, 0:nb].rearrange("p (b j) -> p j b", j=8), pY[:, 0:8 * nsb].rearrange("p (j b) -> p j b", j=8)),
                  reads=[pY], writes=[self.ypre])
        kb.op("act", lambda e: e.activation(self.ypre[:, :, 0:nb], self.ypre[:, :, 0:nb], AF.Gelu), reads=[self.ypre], writes=[self.ypre])
        kb.op("pool", lambda e: e.tensor_copy(self.ygb[:, :, 0:nb], self.ypre[:, :, 0:nb]), reads=[self.ypre], writes=[self.ygb])
        if blk == 0:
            self.dbg_dump("yg0", self.ypre, self.ypre[:], [128, 4, NB])
        for oc in range(4):
            pk = self.pb[oc % 2]
            for c in range(4):
                kb.op("pe", lambda e, oc=oc, c=c, pk=pk: e.matmul(pk[:, 0:nb], self.wglub[:, c, oc * 128:(oc + 1) * 128], self.ygb[:, c, 0:nb], start=(c == 0), stop=(c == 3)),
                      reads=[self.wglub, self.ygb], writes=[pk])
            kb.op("act", lambda e, oc=oc, pk=pk: e.activation(self.sig[:, 0:nb], pk[:, 0:nb], AF.Sigmoid, bias=self.colp[:, 32 + oc:33 + oc]), reads=[pk, self.colp], writes=[self.sig])
            kb.op("dve", lambda e, oc=oc: e.tensor_tensor(self.ycat[:, oc, 0:nb], self.ypre[:, oc, 0:nb], self.sig[:, 0:nb], ALU.mult), reads=[self.ypre, self.sig], writes=[self.ycat])
        if blk == 0:
            self.dbg_dump("ycat0", self.ycat, self.ycat[:], [128, 8, NB], BF16)
        if sample:
            self.dbg_dump("ycatS", self.ycat, self.ycat[:], [128, 8, 128], BF16)

    def rwkv_block(self, blk, sample, nb):
        kb = self.kb
        psT, colp = self.psT, self.colp
        CP = lambda c0, hp: colp[:, c0 + hp:c0 + hp + 1]
        f32t = lambda n, sh=None: kb.sb(n, sh or [128, nb])
        elw, cum, E1, E2, E3 = f32t("elw"), f32t("cum"), f32t("E1"), f32t("E2"), f32t("E3")
        av, kk, sq, kp, tm = f32t("av"), f32t("kk"), f32t("sq"), f32t("kp"), f32t("tm")
        tanhw = kb.sb("tanhw", [128, nb], BF16)
        alob = kb.sb("alob", [128, nb], BF16)
        sigg = kb.sb("sigg", [128, nb], BF16)
        ARt = kb.sb("ARt", [128, 4, 2, nb], BF16)
        BKt = kb.sb("BKt", [128, 4, 2, nb], BF16)
        VbT = kb.sb("VbT", [128, 4, nb], BF16)
        bv = kb.sb("bv", [128, 4, nb])
        gT = kb.sb("gT", [128, 4, nb])
        gam = kb.sb("gam", [128, 4, 16])
        S = slice(0, nb)
        nch = nb // 128
        kb.op("act", lambda e: e.activation(tanhw[0:64, S], psT[0:64, 12, S], AF.Tanh), reads=[psT], writes=[tanhw])
        kb.op("pool", lambda e: e.tensor_copy(alob[64:128, S], psT[64:128, 12, S]), reads=[psT], writes=[alob])
        kb.op("act", lambda e: e.activation(sigg[:, S], psT[:, 13, S], AF.Sigmoid), reads=[psT], writes=[sigg])
        bones = self.C("bones")
        for hp in range(4):
            hs = slice(hp * 128, (hp + 1) * 128)
            p0, p1, p2 = self.pb[0], self.pb[1], self.pb[2]
            kb.op("pe", lambda e: e.matmul(p0[:, S], self.w2b[0:64, hs], tanhw[0:64, S], start=True, stop=True), reads=[self.w2b, tanhw], writes=[p0])
            kb.op("pe", lambda e: e.matmul(p1[:, S], self.a2b[64:128, hs], alob[64:128, S], start=True, stop=True), reads=[self.a2b, alob], writes=[p1])
            kb.op("pe", lambda e: e.matmul(p2[:, S], self.g2b[:, hs], sigg[:, S], start=True, stop=True), reads=[self.g2b, sigg], writes=[p2])
            kb.op("act", lambda e: e.activation(elw[:, S], p0[:, S], AF.Exp, bias=CP(72, hp), scale=-1.0), reads=[p0, colp], writes=[elw])
            kb.op("act", lambda e: e.activation(elw[:, S], elw[:, S], AF.Ln, bias=1.0), reads=[elw], writes=[elw])
            kb.op("act", lambda e: e.activation(elw[:, S], elw[:, S], AF.Exp, bias=-0.5, scale=-1.0), reads=[elw], writes=[elw])
            kb.op("act", lambda e: e.activation(av[:, S], p1[:, S], AF.Sigmoid, bias=CP(44, hp)), reads=[p1, colp], writes=[av])
            kb.op("act", lambda e: e.copy(gT[:, hp, S], p2[:, S]), reads=[p2], writes=[gT])
            for c in range(nch):
                cs = slice(c * 128, (c + 1) * 128)
                d0 = self.C("rst8") if sample else self.C("ones")
                kb.op("dve", lambda e, cs=cs, d0=d0: e.tensor_tensor_scan(cum[:, cs], d0, elw[:, cs], 0.0, ALU.mult, ALU.add), reads=[elw, self.consts], writes=[cum])
            kb.op("act", lambda e: e.activation(E1[:, S], cum[:, S], AF.Exp, scale=-1.0), reads=[cum], writes=[E1])
            kb.op("act", lambda e: e.activation(E2[:, S], cum[:, S], AF.Exp), reads=[cum], writes=[E2])
            kb.op("act", lambda e: e.activation(E3[:, S], elw[:, S], AF.Exp), reads=[elw], writes=[E3])
            kb.op("dve", lambda e: e.tensor_tensor(E3[:, S], E3[:, S], E1[:, S], ALU.mult), reads=[E3, E1], writes=[E3])
            if sample:
                kb.op("dve", lambda e, hp=hp: e.tensor_copy(gam[:, hp, :], E1[:, 7:nb:8]), reads=[E1], writes=[gam])
            else:
                kb.op("dve", lambda e, hp=hp: e.tensor_copy(gam[:, hp, 0:nch], E1[:, 127:nb:128]), reads=[E1], writes=[gam])
            kb.op("dve", lambda e, hp=hp: e.tensor_scalar(kk[:, S], psT[:, 4 + hp, S], CP(48, hp), None, ALU.mult), reads=[psT, colp], writes=[kk])
            kb.op("pool", lambda e: e.tensor_tensor(sq[:, S], kk[:, S], kk[:, S], ALU.mult), reads=[kk], writes=[sq])
            kb.op("pe", lambda e: e.matmul(p0[:, S], bones, sq[:, S], start=True, stop=True), reads=[self.consts, sq], writes=[p0])
            kb.op("act", lambda e: e.activation(sq[:, S], p0[:, S], AF.Sqrt), reads=[p0], writes=[sq])
            kb.op("dve", lambda e: e.tensor_scalar_max(sq[:, S], sq[:, S], 1e-12), reads=[sq], writes=[sq])
            kb.op("dve", lambda e: e.reciprocal(sq[:, S], sq[:, S]), reads=[sq], writes=[sq])
            kb.op("dve", lambda e: e.tensor_tensor(kk[:, S], kk[:, S], sq[:, S], ALU.mult), reads=[kk, sq], writes=[kk])
            kb.op("dve", lambda e, hp=hp: e.tensor_scalar(tm[:, S], av[:, S], CP(52, hp), CP(56, hp), ALU.mult, ALU.add), reads=[av, colp], writes=[tm])
            kb.op("dve", lambda e, hp=hp: e.tensor_tensor(kp[:, S], psT[:, 4 + hp, S], tm[:, S], ALU.mult), reads=[psT, tm], writes=[kp])
            kb.op("dve", lambda e, hp=hp: e.scalar_tensor_tensor(ARt[:, hp, 0, S], kk[:, S], -1.0, E3[:, S], ALU.mult, ALU.mult), reads=[kk, E3], writes=[ARt])
            kb.op("dve", lambda e, hp=hp: e.tensor_tensor(ARt[:, hp, 1, S], psT[:, hp, S], E1[:, S], ALU.mult), reads=[psT, E1], writes=[ARt])
            kb.op("pool", lambda e: e.tensor_tensor(tm[:, S], kk[:, S], av[:, S], ALU.mult), reads=[kk, av], writes=[tm])
            kb.op("dve", lambda e, hp=hp: e.tensor_tensor(BKt[:, hp, 0, S], tm[:, S], E2[:, S], ALU.mult), reads=[tm, E2], writes=[BKt])
            kb.op("dve", lambda e, hp=hp: e.tensor_tensor(BKt[:, hp, 1, S], kp[:, S], E2[:, S], ALU.mult), reads=[kp, E2], writes=[BKt])
            kb.op("pool", lambda e, hp=hp: e.tensor_copy(VbT[:, hp, S], psT[:, 8 + hp, S]), reads=[psT], writes=[VbT])
            kb.op("dve", lambda e, hp=hp: e.scalar_tensor_tensor(tm[:, S], psT[:, hp, S], CP(60, hp), kp[:, S], ALU.mult, ALU.mult), reads=[psT, colp, kp], writes=[tm])
            kb.op("pe", lambda e: e.matmul(p1[:, S], bones, tm[:, S], start=True, stop=True), reads=[self.consts, tm], writes=[p1])
            kb.op("dve", lambda e, hp=hp: e.tensor_tensor(bv[:, hp, S], p1[:, S], psT[:, 8 + hp, S], ALU.mult), reads=[p1, psT], writes=[bv])
        if blk == 0:
            self.dbg_dump("ARt0", ARt, ARt[:], [128, 4, 2, NB], BF16)
            self.dbg_dump("BKt0", BKt, BKt[:], [128, 4, 2, NB], BF16)
        if ("b0only" in self.dbg and blk > 0) or "stop_prep" in self.dbg or ("no_sample" in self.dbg and sample) or ("no_prompt" in self.dbg and not sample):
            return
        nsq = 3 if sample else 6
        mk = lambda n: self.C(("s" if sample else "") + n)
        MUS, MUI, MLS = mk("mu_s"), mk("mu_i"), mk("ml_s")
        ident = self.C("ident")
        tokm2 = [kb.sb(f"tokm{i}", [128, 4, 128], BF16) for i in range(2)]
        UD = BF16 if sample else F32
        Wt = [[kb.sb(f"W{h}{i}", [128, 128], UD) for i in range(2)] for h in range(4)]
        At = [[kb.sb(f"A{h}{i}", [128, 128], UD) for i in range(2)] for h in range(4)]
        IW = [[kb.sb(f"IW{h}{i}", [128, 128], UD) for i in range(2)] for h in range(4)]
        Xfin = [kb.sb(f"Xfin{h}", [128, 128], BF16) for h in range(4)]
        PT = [kb.sb(f"PT{h}", [128, 128], BF16) for h in range(4)]
        MT = [kb.sb(f"MT{h}", [128, 128], BF16) for h in range(4)]
        QT = [kb.sb(f"QT{h}", [128, 128], BF16) for h in range(4)]
        Xt = [[kb.sb(f"X{h}{i}", [128, 128], UD) for i in range(2)] for h in range(4)]
        RhT = kb.sb("RhT", [128, 128], BF16)
        GTt = kb.sb("GTt", [128, 64], BF16)
        ysb = kb.sb("ysb", [128, 128])
        dd = kb.sb("dd", [128, 128])
        d2 = kb.sb("d2", [128, 128])
        rs_ = kb.sb("rs_", [128, 128])
        if sample:
            Zsb = kb.sb("Zsb", [128, 4, 16, 64], BF16)
            Znh = kb.sb("Znh", [128, 16, 64])
            kb.dma("pool", Zsb, Zsb[:].rearrange("p a b v -> p (a b v)"), self.I["wkv0"], self.I["wkv0"][:])
            Bex = kb.sb("Bex", [128, 16, 64], BF16)
            Uex = kb.sb("Uex", [128, 16, 64], BF16)
            Vex = kb.sb("Vex", [128, 16, 64], BF16)
            GTs = kb.sb("GTs", [128, 16, 64], BF16)
            Hs = kb.sb("Hs", [128, 16, 64])
        pb = self.pb
        i2 = self.C("i2")
        for c in range(nch):
            cs = slice(c * 128, (c + 1) * 128)
            for q2 in range(2):
              hps = [2 * q2, 2 * q2 + 1]
              for i, hp in enumerate(hps):
                tokm = tokm2[i]
                srcs = [ARt[:, hp, 0, cs], BKt[:, hp, 0, cs], BKt[:, hp, 1, cs], VbT[:, hp, cs]]
                for ii, sap in enumerate(srcs):
                    kb.op("pe", lambda e, ii=ii, sap=sap: e.transpose(self.pt[:, ii * 128:(ii + 1) * 128], sap, self.identb[:]), reads=[ARt, BKt, VbT, self.identb], writes=[self.pt])
                kb.op("act", lambda e: e.copy(tokm[:].rearrange("p a n -> p (a n)"), self.pt[:, 0:512]), reads=[self.pt], writes=[tokm])
              for i, hp in enumerate(hps):
                tokm = tokm2[i]
                for h2 in range(2):
                    hq = 2 * i + h2
                    P_ = slice(h2 * 64, (h2 + 1) * 64)
                    W, A, Iw, X = Wt[hq], At[hq], IW[hq], Xt[hq]
                    pa, pbk = pb[hq], pb[4]
                    kb.op("pe", lambda e: e.matmul(pa[:, 0:256].rearrange("p (a n) -> p a n", a=2), BKt[P_, hp, 0, cs], ARt[P_, hp, :, cs], start=True, stop=True), reads=[BKt, ARt], writes=[pa])
                    kb.op("pe", lambda e: e.matmul(pa[:, 256:512].rearrange("p (a n) -> p a n", a=2), BKt[P_, hp, 1, cs], ARt[P_, hp, :, cs], start=True, stop=True), reads=[BKt, ARt], writes=[pa])
                    kb.op("pe", lambda e: e.matmul(pbk[:, 0:128], ARt[P_, hp, 0, cs], BKt[P_, hp, 0, cs], start=True, stop=True), reads=[BKt, ARt], writes=[pbk])
                    kb.op("dve", lambda e: e.tensor_tensor(W[0][:], pa[:, 0:128], MUS, ALU.mult), reads=[pa, self.consts], writes=[W[0]])
                    kb.op("dve", lambda e: e.tensor_tensor(PT[hq][:], pa[:, 128:256], MUI, ALU.mult), reads=[pa, self.consts], writes=[PT[hq]])
                    kb.op("dve", lambda e: e.tensor_tensor(MT[hq][:], pa[:, 256:384], MUS, ALU.mult), reads=[pa, self.consts], writes=[MT[hq]])
                    kb.op("dve", lambda e: e.tensor_tensor(QT[hq][:], pa[:, 384:512], MUI, ALU.mult), reads=[pa, self.consts], writes=[QT[hq]])
                    kb.op("dve", lambda e: e.tensor_tensor(A[0][:], pbk[:, 0:128], MLS, ALU.mult), reads=[pbk, self.consts], writes=[A[0]])
                    kb.op("pool", lambda e: e.tensor_tensor(Iw[0][:], W[0][:], ident, ALU.add), reads=[W[0], self.consts], writes=[Iw[0]])
                    kb.op("pe", lambda e: e.matmul(pbk[:, 128:192], MT[hq][:], tokm[:, 3, h2 * 64:(h2 + 1) * 64], start=True, stop=True), reads=[MT[hq], tokm], writes=[pbk])
                    kb.op("pool", lambda e: e.tensor_copy(X[0][:, 0:64], tokm[:, 0, h2 * 64:(h2 + 1) * 64]), reads=[tokm], writes=[X[0]])
                    kb.op("act", lambda e: e.copy(X[0][:, 64:128], pbk[:, 128:192]), reads=[pbk], writes=[X[0]])
              for j in range(nsq + 1):
                a, b = j % 2, (j + 1) % 2
                for hq in range(4):
                    W, A, Iw, X = Wt[hq], At[hq], IW[hq], Xt[hq]
                    pk = pb[hq]
                    kb.op("pe", lambda e: e.matmul(pk[:, 256:384], Iw[a][:], X[a][:], start=True, stop=True), reads=[Iw[a], X[a]], writes=[pk])
                    if j < nsq:
                        kb.op("pe", lambda e: e.matmul(pk[:, 0:128], A[a][:], W[a][:], start=True, stop=True), reads=[A[a], W[a]], writes=[pk])
                    if j < nsq - 1:
                        kb.op("pe", lambda e: e.matmul(pk[:, 128:256], W[a][:], A[a][:], start=True, stop=True), reads=[A[a], W[a]], writes=[pk])
                    dst = Xfin[hq] if j == nsq else X[b]
                    if hq % 2 == 0:
                        kb.op("dve", lambda e: e.tensor_copy(dst[:], pk[:, 256:384]), reads=[pk], writes=[dst])
                    else:
                        kb.op("act", lambda e: e.copy(dst[:], pk[:, 256:384]), reads=[pk], writes=[dst])
                    if j < nsq:
                        kb.op("dve", lambda e: e.tensor_tensor(Iw[b][:], pk[:, 0:128], ident, ALU.add), reads=[pk, self.consts], writes=[Iw[b]])
                    if j < nsq - 1:
                        kb.op("act", lambda e: e.copy(W[b][:], pk[:, 0:128]), reads=[pk], writes=[W[b]])
                        kb.op("act", lambda e: e.copy(A[b][:], pk[:, 128:256]), reads=[pk], writes=[A[b]])
              for i, hp in enumerate(hps):
                tokm = tokm2[i]
                XF = [Xfin[2 * i], Xfin[2 * i + 1]]
                PTl = [PT[2 * i], PT[2 * i + 1]]
                QTl = [QT[2 * i], QT[2 * i + 1]]
                pR, pG, pH, pYT = pb[2], pb[3], pb[4], pb[2]
                for h2 in range(2):
                    P_ = slice(h2 * 64, (h2 + 1) * 64)
                    X = XF[h2]
                    kb.op("pe", lambda e, P_=P_, X=X, h2=h2: e.matmul(pR[P_, 0:128], X[:, 0:64], PTl[h2][:], start=True, stop=True), reads=[X, PTl[h2]], writes=[pR])
                    kb.op("pe", lambda e, P_=P_, X=X, h2=h2: e.matmul(pG[P_, 0:64], X[:, 0:64], tokm[:, 1, h2 * 64:(h2 + 1) * 64], start=True, stop=True), reads=[X, tokm], writes=[pG])
                kb.op("dve", lambda e: e.tensor_tensor(RhT[:], pR[:, 0:128], ARt[:, hp, 1, cs], ALU.add), reads=[pR, ARt], writes=[RhT])
                if sample:
                    kb.op("dve", lambda e: e.tensor_tensor(GTt[:], pG[:, 0:64], i2, ALU.add), reads=[pG, self.consts], writes=[GTt])
                else:
                    kb.op("dve", lambda e: e.tensor_copy(GTt[:], pG[:, 0:64]), reads=[pG], writes=[GTt])
                if not sample:
                    for h2 in range(2):
                        P_ = slice(h2 * 64, (h2 + 1) * 64)
                        X = XF[h2]
                        vt = tokm[:, 3, h2 * 64:(h2 + 1) * 64]
                        kb.op("pe", lambda e, P_=P_, X=X, h2=h2: e.matmul(pYT[P_, 128:256], X[:, 64:128], PTl[h2][:], start=True, stop=False), reads=[X, PTl[h2]], writes=[pYT])
                        kb.op("pe", lambda e, P_=P_, vt=vt, h2=h2: e.matmul(pYT[P_, 128:256], vt, QTl[h2][:], start=False, stop=False), reads=[tokm, QTl[h2]], writes=[pYT])
                        kb.op("pe", lambda e, P_=P_: e.matmul(pYT[P_, 128:256], self.Zb[P_, hp, :], RhT[P_, :], start=False, stop=True), reads=[self.Zb, RhT], writes=[pYT])
                        kb.op("pe", lambda e, P_=P_, X=X, h2=h2: e.matmul(pH[P_, 0:64], tokm[:, 1, h2 * 64:(h2 + 1) * 64], X[:, 64:128], start=True, stop=False), reads=[tokm, X], writes=[pH])
                        kb.op("pe", lambda e, P_=P_, vt=vt, h2=h2: e.matmul(pH[P_, 0:64], tokm[:, 2, h2 * 64:(h2 + 1) * 64], vt, start=False, stop=False), reads=[tokm], writes=[pH])
                        kb.op("pe", lambda e, P_=P_: e.matmul(pH[P_, 0:64], GTt[P_, :], self.Zb[P_, hp, :], start=False, stop=True), reads=[GTt, self.Zb], writes=[pH])
                    kb.op("dve", lambda e: e.tensor_tensor(self.Zf[:, hp, :], pH[:, 0:64], self.Zf[:, hp, :], ALU.add), reads=[pH, self.Zf], writes=[self.Zf])
                    kb.op("dve", lambda e, c=c: e.tensor_scalar(self.Zf[:, hp, :], self.Zf[:, hp, :], gam[:, hp, c:c + 1], None, ALU.mult), reads=[self.Zf, gam], writes=[self.Zf])
                    kb.op("pool", lambda e: e.tensor_copy(self.Zb[:, hp, :], self.Zf[:, hp, :]), reads=[self.Zf], writes=[self.Zb])
                else:
                    sel = self.C("sel16")
                    for h2 in range(2):
                        P_ = slice(h2 * 64, (h2 + 1) * 64)
                        X = XF[h2]
                        vt = tokm[:, 3, h2 * 64:(h2 + 1) * 64]
                        kb.op("pe", lambda e, P_=P_, X=X, h2=h2: e.matmul(pYT[P_, 128:256], X[:, 64:128], PTl[h2][:], start=True, stop=False), reads=[X, PTl[h2]], writes=[pYT])
                        kb.op("pe", lambda e, P_=P_, vt=vt, h2=h2: e.matmul(pYT[P_, 128:256], vt, QTl[h2][:], start=False, stop=True), reads=[tokm, QTl[h2]], writes=[pYT])
                        for b in range(16):
                            kb.op("pe", lambda e, P_=P_, b=b: e.matmul(pb[0][P_, b * 8:(b + 1) * 8], Zsb[P_, hp, b, :], RhT[P_, b * 8:(b + 1) * 8], start=True, stop=True), reads=[Zsb, RhT], writes=[pb[0]])
                        bx = lambda ap: ap.unsqueeze(1).to_broadcast([128, 16, 64])
                        sx = sel.unsqueeze(2).to_broadcast([128, 16, 64])
                        kb.op("dve", lambda e, h2=h2: e.tensor_tensor(Bex[:], bx(tokm[:, 1, h2 * 64:(h2 + 1) * 64]), sx, ALU.mult), reads=[tokm, self.consts], writes=[Bex])
                        kb.op("dve", lambda e, X=X: e.tensor_tensor(Uex[:], bx(X[:, 64:128]), sx, ALU.mult), reads=[X, self.consts], writes=[Uex])
                        kb.op("pool", lambda e, vt=vt: e.tensor_tensor(Vex[:], bx(vt), sx, ALU.mult), reads=[tokm, self.consts], writes=[Vex])
                        for hf in range(2):
                            bsl = slice(hf * 8, (hf + 1) * 8)
                            pg, ph = pb[3], pb[4]
                            kb.op("pe", lambda e, P_=P_, X=X, bsl=bsl, pg=pg: e.matmul(pg[P_, :], X[:, 0:64], Bex[:, bsl, :], start=True, stop=True), reads=[X, Bex], writes=[pg])
                            kb.op("pe", lambda e, P_=P_, bsl=bsl, ph=ph, h2=h2: e.matmul(ph[P_, :], tokm[:, 1, h2 * 64:(h2 + 1) * 64], Uex[:, bsl, :], start=True, stop=False), reads=[tokm, Uex], writes=[ph])
                            kb.op("pe", lambda e, P_=P_, bsl=bsl, ph=ph, h2=h2: e.matmul(ph[P_, :], tokm[:, 2, h2 * 64:(h2 + 1) * 64], Vex[:, bsl, :], start=False, stop=True), reads=[tokm, Vex], writes=[ph])
                            kb.op("dve", lambda e, P_=P_, bsl=bsl, pg=pg: e.tensor_tensor(GTs[P_, bsl, :], pg[P_, :].rearrange("p (b k) -> p b k", b=8), i2[P_, :].unsqueeze(1).to_broadcast([64, 8, 64]), ALU.add), reads=[pg, self.consts], writes=[GTs])
                            kb.op("act", lambda e, P_=P_, bsl=bsl, ph=ph: e.copy(Hs[P_, bsl, :], ph[P_, :].rearrange("p (b k) -> p b k", b=8)), reads=[ph], writes=[Hs])
                        for b in range(16):
                            kb.op("pe", lambda e, P_=P_, b=b: e.matmul(pb[1][P_, (b % 8) * 64:(b % 8 + 1) * 64], GTs[P_, b, :], Zsb[P_, hp, b, :], start=True, stop=True), reads=[GTs, Zsb], writes=[pb[1]])
                            if b % 8 == 7:
                                bsl = slice(b - 7, b + 1)
                                kb.op("dve", lambda e, P_=P_, bsl=bsl: e.tensor_tensor(Hs[P_, bsl, :], Hs[P_, bsl, :], pb[1][P_, :].rearrange("p (b k) -> p b k", b=8), ALU.add), reads=[Hs, pb[1]], writes=[Hs])
                    kb.op("dve", lambda e: e.tensor_tensor(Znh[:], Hs[:], gam[:, hp, :].unsqueeze(2).to_broadcast([128, 16, 64]), ALU.mult), reads=[Hs, gam], writes=[Znh])
                    kb.dma("sp", self.O["wkvs"], self.O["wkvs"][:, hp * 1024:(hp + 1) * 1024], Znh, Znh[:].rearrange("p b v -> p (b v)"), grp=self.g_out)
                    kb.op("dve", lambda e: e.tensor_copy(ysb[:], pb[0][:, 0:128]), reads=[pb[0]], writes=[ysb])
                if sample:
                    kb.op("dve", lambda e: e.tensor_tensor(ysb[:], ysb[:], pYT[:, 128:256], ALU.add), reads=[ysb, pYT], writes=[ysb])
                else:
                    kb.op("act", lambda e: e.copy(ysb[:], pYT[:, 128:256]), reads=[pYT], writes=[ysb])
                pm = pb[3]
                kb.op("pe", lambda e: e.matmul(pm[:, 0:128], bones, ysb[:], start=True, stop=True), reads=[self.consts, ysb], writes=[pm])
                kb.op("dve", lambda e: e.scalar_tensor_tensor(dd[:], pm[:, 0:128], -1.0 / 64, ysb[:], ALU.mult, ALU.add), reads=[pm, ysb], writes=[dd])
                kb.op("pool", lambda e: e.tensor_tensor(d2[:], dd[:], dd[:], ALU.mult), reads=[dd], writes=[d2])
                kb.op("pe", lambda e: e.matmul(pm[:, 128:256], bones, d2[:], start=True, stop=True), reads=[self.consts, d2], writes=[pm])
                kb.op("act", lambda e: e.activation(rs_[:], pm[:, 128:256], AF.Sqrt, bias=self.gneps[:], scale=1.0 / 64), reads=[pm, self.gneps], writes=[rs_])
                kb.op("dve", lambda e: e.reciprocal(rs_[:], rs_[:]), reads=[rs_], writes=[rs_])
                kb.op("dve", lambda e: e.tensor_tensor(dd[:], dd[:], rs_[:], ALU.mult), reads=[dd, rs_], writes=[dd])
                kb.op("dve", lambda e: e.tensor_scalar(dd[:], dd[:], CP(64, hp), CP(68, hp), ALU.mult, ALU.add), reads=[dd, colp], writes=[dd])
                kb.op("dve", lambda e: e.tensor_tensor(dd[:], dd[:], bv[:, hp, cs], ALU.add), reads=[dd, bv], writes=[dd])
                kb.op("dve", lambda e: e.tensor_tensor(self.ycat[:, 4 + hp, cs], dd[:], gT[:, hp, cs], ALU.mult), reads=[dd, gT], writes=[self.ycat])
        if blk == 0:
            self.dbg_dump("yrw0", self.ycat, self.ycat[:], [128, 8, NB], BF16)
        if sample:
            self.dbg_dump("yrwS", self.ycat, self.ycat[:], [128, 8, 128], BF16)
            shso = Hs
            shsov = Hs[0:16, 0:8, :].rearrange("p a b -> p (a b)")
            for r in range(14):
                kb.op("pe", lambda e, r=r: e.transpose(pb[4][0:16, (r % 4) * 128:(r % 4 + 1) * 128], self.shsT[:, r, :], ident), reads=[self.shsT, self.consts], writes=[pb[4]])
                if r % 4 == 3 or r == 13:
                    r0 = r - (r % 4)
                    n = r - r0 + 1
                    kb.op("dve", lambda e, n=n: e.tensor_copy(shsov[:, 0:n * 128], pb[4][0:16, 0:n * 128]), reads=[pb[4]], writes=[shso])
                    kb.dma("sp", self.O["shs"], self.O["shs"][:, r0 * 128:(r + 1) * 128], shso, shsov[:, 0:n * 128], grp=self.g_out)
        if blk == NPB - 1 and "no_tail" not in self.dbg:
            wv = self.O["wkvp"][:].rearrange("(a h2) k v -> h2 k a v", h2=2)
            for h2 in range(2):
                kb.dma("sp", self.O["wkvp"], wv[h2], self.Zf, self.Zf[h2 * 64:(h2 + 1) * 64, :, :], grp=self.g_out)
            kb.op("pe", lambda e: e.transpose(pb[4][0:14, 0:128], self.carry[:], ident), reads=[self.carry, self.consts], writes=[pb[4]])
            kb.op("dve", lambda e: e.tensor_copy(ysb[0:14, :], pb[4][0:14, 0:128]), reads=[pb[4]], writes=[ysb])
            kb.dma("sp", self.O["shp"], self.O["shp"][:], ysb, ysb[0:14, :], grp=self.g_out)

    def outproj(self, blk, sample, nb, ntile, row0, GA1):
        kb = self.kb
        self.x1t = kb.sb("x1t", [128, D])
        xt = kb.sb("xt", [128, D])
        for j in range(ntile):
            kb.dma("sp", xt, xt[:], self.I["xall"], self.I["xall"][row0 + j * 128:row0 + (j + 1) * 128, :])
            for hf in range(2):
                pk = self.pb[hf]
                for c in range(8):
                    kb.op("pe", lambda e, c=c, hf=hf, pk=pk: e.matmul(pk[:], self.ycat[:, c, j * 128:(j + 1) * 128], self.woutb[:, c, hf * 512:(hf + 1) * 512], start=(c == 0), stop=(c == 7)),
                          reads=[self.ycat, self.woutb], writes=[pk])
                sl = slice(hf * 512, (hf + 1) * 512)
                kb.op("dve", lambda e, pk=pk, sl=sl: e.tensor_tensor(self.x1t[:, sl], pk[:], GA1[:, sl], ALU.mult), reads=[pk, GA1], writes=[self.x1t])
                kb.op("dve", lambda e, sl=sl: e.tensor_tensor(self.x1t[:, sl], self.x1t[:, sl], xt[:, sl], ALU.add), reads=[self.x1t, xt], writes=[self.x1t])
            kb.dma("sp", self.x1s, self.x1s[row0 + j * 128:row0 + (j + 1) * 128, :], self.x1t, self.x1t[:], grp=self.g_x1)

    def tab_alloc(self):
        kb = self.kb
        self.UTs = kb.dram("UTs", [128, 128, 8, 128], BF16)
        self.Vs = kb.dram("Vs", [128, 128, D], BF16)
        self.g_tab = [[kb.group(f"tabu{i}"), kb.group(f"tabv{i}")] for i in range(2)]
        self.tb_uf = [kb.sb(f"uf{i}", [128, D]) for i in range(2)]
        self.tb_vf = [kb.sb(f"vf{i}", [128, D]) for i in range(2)]
        self.tb_utb = [kb.sb(f"utb{i}", [128, 8, 128], BF16) for i in range(2)]
        self.tb_vbb = [kb.sb(f"vbb{i}", [128, D], BF16) for i in range(2)]

    def tab_prep(self):
        kb = self.kb
        I = self.I
        pb = self.pb
        ident = self.C("ident")
        uf, vf, utb, vbb = self.tb_uf, self.tb_vf, self.tb_utb, self.tb_vbb
        pu = [pb[5], pb[6]]

        def tload(k):
            kb.dma("sp", uf[k % 2], uf[k % 2][:], I["peer_u"], I["peer_u"][k * 128:(k + 1) * 128, :])
            kb.dma("sp", vf[k % 2], vf[k % 2][:], I["peer_v"], I["peer_v"][k * 128:(k + 1) * 128, :])
        tload(0)
        for k in range(128):
            u, v, ub, vb = uf[k % 2], vf[k % 2], utb[k % 2], vbb[k % 2]
            if k + 1 < 128:
                tload(k + 1)
            for c in range(8):
                kb.op("pe", lambda e: e.transpose(pu[c // 4][:, (c % 4) * 128:(c % 4 + 1) * 128], u[:, c * 128:(c + 1) * 128], ident), reads=[u, self.consts], writes=[pu[c // 4]])
            kb.op("act", lambda e: e.copy(ub[:, 0:4, :].rearrange("p a n -> p (a n)"), pu[0][:]), reads=[pu[0]], writes=[ub])
            kb.op("dve", lambda e: e.tensor_copy(ub[:, 4:8, :].rearrange("p a n -> p (a n)"), pu[1][:]), reads=[pu[1]], writes=[ub])
            self.cast("pool", vb, vb[:, 0:512], v, v[:, 0:512])
            self.cast("act" if k % 2 else "dve", vb, vb[:, 512:1024], v, v[:, 512:1024])
            kb.dma("pool", self.UTs, self.UTs[k], ub, ub[:], grp=self.g_tab[k % 2][0])
            kb.dma("pool", self.Vs, self.Vs[k], vb, vb[:], grp=self.g_tab[k % 2][1])
        for gg in self.g_tab:
            for g_ in gg:
                g_.close()

    def phase2(self):
        kb = self.kb
        I = self.I
        pb = self.pb
        ident = self.C("ident")
        NEG = -1.0e30
        g = kb.group("p2par")
        fng = kb.sb("fng", [128, D])
        kb.dma("sp", fng, fng[:], I["fng"], I["fng"][:].partition_broadcast(128), grp=g)
        g.close()
        keysT = kb.sb("keysT", [128, 8, 128], BF16)
        wqb = kb.sb("wqb", [128, 8, D], BF16)
        mod2 = [kb.sb(f"m2{n}", [128, D]) for n in range(3)]
        kb.scope_enter()
        g = kb.group("p2k")
        K12 = kb.sb("K12", [128, 8, 128])
        kb.dma("sp", K12, K12[:, :, 0:64], I["keys1"], I["keys1"][:].rearrange("h n d -> n h d"), grp=g)
        kb.dma("sp", K12, K12[:, :, 64:128], I["keys2"], I["keys2"][:].rearrange("h n d -> n h d"), grp=g)
        g.close()
        for h in range(8):
            kb.op("pe", lambda e, h=h: e.transpose(pb[h // 4][:, (h % 4) * 128:(h % 4 + 1) * 128], K12[:, h, :], ident), reads=[K12, self.consts], writes=[pb[h // 4]])
        for q in range(2):
            kb.op("dve", lambda e, q=q: e.tensor_copy(keysT[:, q * 4:(q + 1) * 4, :].rearrange("p a n -> p (a n)"), pb[q][:]), reads=[pb[q]], writes=[keysT])
        if "p2a" in self.dbg:
            kb.scope_exit()
            return
        wst = [kb.sb(f"wq{i}", [128, D]) for i in range(2)]
        for kc in range(8):
            s_ = kc % 2
            kb.dma("sp", wst[s_], wst[s_][:], I["w_q"], I["w_q"][kc * 128:(kc + 1) * 128, :])
            self.cast("act" if kc % 2 else "dve", wqb, wqb[:, kc, :], wst[s_], wst[s_][:])
        UTs, Vs = self.UTs, self.Vs
        kb.scope_exit()
        if "p2b" in self.dbg:
            return
        x1g = [kb.sb(f"x1g{i}", [128, D]) for i in range(2)]
        x1r = kb.sb("x1r", [128, D])
        h2Ts = [kb.sb(f"h2T{i}", [128, 8, NB], BF16) for i in range(2)]
        qT = kb.sb("qT", [128, 8, 128], BF16)
        sqj = kb.sb("sqj2", [128, D], BF16)
        t1k = kb.sb("t1k2", [128, D])
        hb = kb.sb("hb2", [128, D], BF16)
        ss = kb.sb("ss2", [128, 4])
        rstd = kb.sb("rstd2", [128, 4])
        sc = kb.sb("sc", [128, 8, 2, 128])
        v16 = kb.sb("v16", [128, 8, 2, 16])
        v16b = kb.sb("v16b", [128, 8, 2, 8])
        s16b = kb.sb("s16b", [128, 8, 8])
        i16 = kb.sb("i16", [128, 8, 2, 16], U32)
        i16f = kb.sb("i16f", [128, 8, 2, 16])
        cand = kb.sb("cand", [128, 8, 16, 16])
        s16 = kb.sb("s16", [128, 8, 16])
        p16 = kb.sb("p16", [128, 8, 16], U32)
        pa_i = kb.sb("pa_i", [128, 8, 16], U32)
        paf = kb.sb("paf", [128, 2, 8, 16])
        oh = kb.sb("oh", [128, 8, 16, 16])
        sm = kb.sb("sm", [128, 8])
        jt = kb.sb("jt", [128, 3, 128])
        jTs = [[kb.sb(f"jT{p_}{j_}", [128, 3, 128], BF16) for j_ in range(2)] for p_ in range(2)]
        iotab = kb.sb("iotab", [128, 128], BF16)
        kb.op("dve", lambda e: e.tensor_copy(iotab[:], self.C("iota")), reads=[self.consts], writes=[iotab])
        OH1gs = [kb.sb(f"OH1g{i}", [128, 16, 128], BF16) for i in range(2)]
        OH2s = [kb.sb(f"OH2{i}", [128, 16, 128], BF16) for i in range(2)]
        Wall = kb.sb("Wall", [128, NB, 128], BF16)
        ubs = [kb.sb(f"ubs{i}", [128, 8, 128], BF16) for i in range(4)]
        vbs = [kb.sb(f"vbs{i}", [128, D], BF16) for i in range(4)]
        gsb = kb.sb("gsb", [128, NB], BF16)
        WA = [kb.sb(f"WA{i}", [128, NB], BF16) for i in range(2)]
        x2 = kb.sb("x2", [128, D])
        iota = self.C("iota")
        SH2, A2, GA2 = mod2
        pq = pb[6]

        def ginfo(gi):
            sample = gi == NPB
            nt = TS if sample else NB
            return sample, nt, nt // 128, (TP if sample else gi * NB), (1 if sample else 0)

        def front(gi):
            sample, nt, ntile, row0, typ = ginfo(gi)
            h2T = h2Ts[gi % 2]
            if gi == 0 or sample:
                for n in range(2):
                    kb.dma("sp", mod2[n], mod2[n][:], self.mod2s, self.mod2s[typ, 3 + n])
            for j in range(ntile):
                ts_ = slice(j * 128, (j + 1) * 128)
                kb.dma("sp", x1g[j], x1g[j][:], self.x1s, self.x1s[row0 + j * 128:row0 + (j + 1) * 128, :])
                kb.op("act", lambda e: e.activation(sqj[:], x1g[j][:], AF.Square, accum_out=ss[:, j:j + 1]), reads=[x1g[j]], writes=[sqj, ss])
                kb.op("act", lambda e: e.activation(rstd[:, j:j + 1], ss[:, j:j + 1], AF.Sqrt, bias=self.eps6[:], scale=1.0 / D), reads=[ss, self.eps6], writes=[rstd])
                kb.op("dve", lambda e: e.reciprocal(rstd[:, j:j + 1], rstd[:, j:j + 1]), reads=[rstd], writes=[rstd])
                kb.op("dve", lambda e: e.scalar_tensor_tensor(t1k[:], x1g[j][:], rstd[:, j:j + 1], A2[:], ALU.mult, ALU.mult), reads=[x1g[j], rstd, A2], writes=[t1k])
                kb.op("pool", lambda e: e.tensor_tensor(hb[:], t1k[:], SH2[:], ALU.add), reads=[t1k, SH2], writes=[hb])
                for c in range(8):
                    kb.op("pe", lambda e: e.transpose(self.pt[:, c * 128:(c + 1) * 128], hb[:, c * 128:(c + 1) * 128], self.identb[:]), reads=[hb, self.identb], writes=[self.pt])
                kb.op("act", lambda e: e.copy(h2T[:, :, ts_], self.pt[:].rearrange("p (c n) -> p c n", c=8)), reads=[self.pt], writes=[h2T])
                for hg in range(2):
                    for hh in range(4):
                        h = hg * 4 + hh
                        for k in range(8):
                            kb.op("pe", lambda e: e.matmul(pq[:, hh * 128:(hh + 1) * 128], wqb[:, k, h * 128:(h + 1) * 128], h2T[:, k, ts_], start=(k == 0), stop=(k == 7)), reads=[wqb, h2T], writes=[pq])
                    kb.op("act", lambda e: e.copy(qT[:, hg * 4:(hg + 1) * 4, :], pq[:].rearrange("p (a n) -> p a n", a=4)), reads=[pq], writes=[qT])
                for f in range(2):
                    P_ = slice(f * 64, (f + 1) * 64)
                    for hg in range(2):
                        for hh in range(4):
                            h = hg * 4 + hh
                            kb.op("pe", lambda e: e.matmul(pq[:, hh * 128:(hh + 1) * 128], qT[P_, h, :], keysT[P_, h, :], start=True, stop=True), reads=[qT, keysT], writes=[pq])
                        kb.op("act", lambda e: e.copy(sc[:, hg * 4:(hg + 1) * 4, f, :], pq[:].rearrange("p (a n) -> p a n", a=4)), reads=[pq], writes=[sc])
                HF = [(h, f) for h in range(8) for f in range(2)]
                sctb = oh[:].rearrange("p a b c -> p (a b c)").rearrange("p (q n) -> p q n", q=16)
                for h, f in HF:
                    kb.op("dve", lambda e, h=h, f=f: e.max(v16[:, h, f, 0:8], sc[:, h, f, :]), reads=[sc], writes=[v16])
                for h, f in HF:
                    kb.op("dve", lambda e, h=h, f=f: e.max_index(i16[:, h, f, 0:8], v16[:, h, f, 0:8], sc[:, h, f, :]), reads=[sc, v16], writes=[i16])
                for q, (h, f) in enumerate(HF):
                    kb.op("dve", lambda e, h=h, f=f, q=q: e.match_replace(sctb[:, q, :], v16[:, h, f, 0:8], sc[:, h, f, :], NEG), reads=[sc, v16], writes=[oh])
                for q, (h, f) in enumerate(HF):
                    kb.op("dve", lambda e, h=h, f=f, q=q: e.max(v16b[:, h, f, :], sctb[:, q, :]), reads=[oh], writes=[v16b])
                for q, (h, f) in enumerate(HF):
                    kb.op("dve", lambda e, h=h, f=f, q=q: e.max_index(i16[:, h, f, 8:16], v16b[:, h, f, :], sctb[:, q, :]), reads=[oh, v16b], writes=[i16])
                kb.op("dve", lambda e: e.tensor_copy(v16[:, :, :, 8:16], v16b[:]), reads=[v16b], writes=[v16])
                kb.op("dve", lambda e: e.tensor_copy(i16f[:], i16[:]), reads=[i16], writes=[i16f])
                kb.op("dve", lambda e: e.tensor_tensor(cand[:], v16[:, :, 0, :].unsqueeze(3).to_broadcast([128, 8, 16, 16]), v16[:, :, 1, :].unsqueeze(2).to_broadcast([128, 8, 16, 16]), ALU.add), reads=[v16], writes=[cand])
                candf = lambda h: cand[:, h, :, :].rearrange("p a b -> p (a b)")
                sc2 = sc[:].rearrange("p h f n -> p h (f n)")
                for h in range(8):
                    kb.op("dve", lambda e, h=h: e.max(s16[:, h, 0:8], candf(h)), reads=[cand], writes=[s16])
                for h in range(8):
                    kb.op("dve", lambda e, h=h: e.max_index(p16[:, h, 0:8], s16[:, h, 0:8], candf(h)), reads=[cand, s16], writes=[p16])
                for h in range(8):
                    kb.op("dve", lambda e, h=h: e.match_replace(sc2[:, h, :], s16[:, h, 0:8], candf(h), NEG), reads=[cand, s16], writes=[sc])
                for h in range(8):
                    kb.op("dve", lambda e, h=h: e.max(s16b[:, h, :], sc2[:, h, :]), reads=[sc], writes=[s16b])
                for h in range(8):
                    kb.op("dve", lambda e, h=h: e.max_index(p16[:, h, 8:16], s16b[:, h, :], sc2[:, h, :]), reads=[sc, s16b], writes=[p16])
                kb.op("dve", lambda e: e.tensor_copy(s16[:, :, 8:16], s16b[:]), reads=[s16b], writes=[s16])
                gate = jt[:, 2, :].rearrange("p (h k) -> p h k", h=8)
                kb.op("dve", lambda e: e.tensor_tensor(gate, s16[:], s16[:, :, 0:1].to_broadcast([128, 8, 16]), ALU.subtract), reads=[s16], writes=[jt])
                kb.op("act", lambda e: e.activation(gate, gate, AF.Exp), reads=[jt], writes=[jt])
                kb.op("dve", lambda e: e.tensor_reduce(sm[:], gate, AX.X, ALU.add), reads=[jt], writes=[sm])
                kb.op("dve", lambda e: e.reciprocal(sm[:], sm[:]), reads=[sm], writes=[sm])
                kb.op("dve", lambda e: e.tensor_tensor(gate, gate, sm[:].unsqueeze(2).to_broadcast([128, 8, 16]), ALU.mult), reads=[jt, sm], writes=[jt])
                kb.op("dve", lambda e: e.tensor_single_scalar(pa_i[:], p16[:], 4, ALU.logical_shift_right), reads=[p16], writes=[pa_i])
                kb.op("dve", lambda e: e.tensor_copy(paf[:, 0], pa_i[:]), reads=[pa_i], writes=[paf])
                kb.op("dve", lambda e: e.tensor_single_scalar(pa_i[:], p16[:], 15, ALU.bitwise_and), reads=[p16], writes=[pa_i])
                kb.op("dve", lambda e: e.tensor_copy(paf[:, 1], pa_i[:]), reads=[pa_i], writes=[paf])
                io16 = iota[:, 0:16].unsqueeze(1).unsqueeze(1).to_broadcast([128, 8, 16, 16])
                for f in range(2):
                    kb.op("dve", lambda e, f=f: e.tensor_tensor(oh[:], io16, paf[:, f].unsqueeze(3).to_broadcast([128, 8, 16, 16]), ALU.is_equal), reads=[paf, self.consts], writes=[oh])
                    kb.op("dve", lambda e, f=f: e.tensor_tensor(oh[:], oh[:], i16f[:, :, f, :].unsqueeze(2).to_broadcast([128, 8, 16, 16]), ALU.mult), reads=[oh, i16f], writes=[oh])
                    kb.op("dve", lambda e, f=f: e.tensor_reduce(jt[:, f, :].rearrange("p (h k) -> p h k", h=8), oh[:], AX.X, ALU.add), reads=[oh], writes=[jt])
                jTc = jTs[gi % 2][j]
                for q in range(3):
                    kb.op("pe", lambda e: e.transpose(pq[:, q * 128:(q + 1) * 128], jt[:, q, :], ident), reads=[jt, self.consts], writes=[pq])
                kb.op("act", lambda e: e.copy(jTc[:].rearrange("p a n -> p (a n)"), pq[:, 0:384]), reads=[pq], writes=[jTc])

        def mainA(gi):
            sample, nt, ntile, row0, typ = ginfo(gi)
            if gi == 0 or sample:
                kb.dma("sp", mod2[2], mod2[2][:], self.mod2s, self.mod2s[typ, 5])
            for j in range(ntile):
                jTc = jTs[gi % 2][j]
                for qq in range(8):
                    t0 = qq * 16
                    OH1g, OH2 = OH1gs[qq % 2], OH2s[qq % 2]
                    io3 = iotab[:].unsqueeze(1).to_broadcast([128, 16, 128])
                    bc = lambda q, t0=t0: jTc[:, q, t0:t0 + 16].unsqueeze(2).to_broadcast([128, 16, 128])
                    kb.op("dve", lambda e: e.tensor_tensor(OH2[:], io3, bc(1), ALU.is_equal), reads=[jTc, iotab], writes=[OH2])
                    kb.op("dve", lambda e: e.tensor_tensor(OH1g[:], io3, bc(0), ALU.is_equal), reads=[jTc, iotab], writes=[OH1g])
                    kb.op("pool" if qq % 2 else "dve", lambda e: e.tensor_tensor(OH1g[:], OH1g[:], bc(2), ALU.mult), reads=[OH1g, jTc], writes=[OH1g])
                    for t4 in range(4):
                        pk = pb[4 + t4 % 2]
                        for u in range(4):
                            t = t4 * 4 + u
                            kb.op("pe", lambda e, t=t, u=u, pk=pk: e.matmul(pk[:, u * 128:(u + 1) * 128], OH2[:, t, :], OH1g[:, t, :], start=True, stop=True), reads=[OH2, OH1g], writes=[pk])
                        tb = j * 128 + t0 + t4 * 4
                        kb.op("act", lambda e, tb=tb, pk=pk: e.copy(Wall[:, tb:tb + 4, :].rearrange("p a n -> p (a n)"), pk[:]), reads=[pk], writes=[Wall])

        def mainB(gi):
            sample, nt, ntile, row0, typ = ginfo(gi)
            h2T = h2Ts[gi % 2]

            def sload(k):
                kb.dma("sp", ubs[k % 4], ubs[k % 4][:], UTs, UTs[k])
                kb.dma("sp", vbs[k % 4], vbs[k % 4][:], Vs, Vs[k])
            sload(0)
            sload(1)
            sload(2)

            def Dmm(k):
                ub, pD = ubs[k % 4], pb[4 + k % 2]
                for c in range(8):
                    kb.op("pe", lambda e: e.matmul(pD[:, 0:nt], ub[:, c, :], h2T[:, c, 0:nt], start=(c == 0), stop=(c == 7)), reads=[ub, h2T], writes=[pD])
            Dmm(0)
            for k in range(128):
                vb = vbs[k % 4]
                pD = pb[4 + k % 2]
                wa = WA[k % 2]
                if k + 1 < 128:
                    Dmm(k + 1)
                kb.op("act", lambda e: e.activation(gsb[:, 0:nt], pD[:, 0:nt], AF.Gelu), reads=[pD], writes=[gsb])
                kb.op("dve", lambda e: e.tensor_tensor(wa[:, 0:nt], gsb[:, 0:nt], Wall[:, 0:nt, k], ALU.mult), reads=[gsb, Wall], writes=[wa])
                for a in range(ntile):
                    for hf in range(2):
                        po = pb[a * 2 + hf]
                        kb.op("pe", lambda e: e.matmul(po[:], wa[:, a * 128:(a + 1) * 128], vb[:, hf * 512:(hf + 1) * 512], start=(k == 0), stop=(k == 127)), reads=[wa, vb], writes=[po])
                if k + 3 < 128:
                    sload(k + 3)
            for a in range(ntile):
                kb.dma("sp", x1r, x1r[:], self.x1s, self.x1s[row0 + a * 128:row0 + (a + 1) * 128, :])
                for hf in range(2):
                    sl = slice(hf * 512, (hf + 1) * 512)
                    po = pb[a * 2 + hf]
                    kb.op("dve", lambda e: e.tensor_tensor(x2[:, sl], po[:], GA2[:, sl], ALU.mult), reads=[po, GA2], writes=[x2])
                    kb.op("dve", lambda e: e.tensor_tensor(x2[:, sl], x2[:, sl], x1r[:, sl], ALU.add), reads=[x2, x1r], writes=[x2])
                kb.op("act", lambda e: e.activation(sqj[:], x2[:], AF.Square, accum_out=ss[:, 2 + a:3 + a]), reads=[x2], writes=[sqj, ss])
                kb.op("act", lambda e: e.activation(rstd[:, 2 + a:3 + a], ss[:, 2 + a:3 + a], AF.Sqrt, bias=self.eps6[:], scale=1.0 / D), reads=[ss, self.eps6], writes=[rstd])
                kb.op("dve", lambda e: e.reciprocal(rstd[:, 2 + a:3 + a], rstd[:, 2 + a:3 + a]), reads=[rstd], writes=[rstd])
                kb.op("dve", lambda e: e.scalar_tensor_tensor(x2[:], x2[:], rstd[:, 2 + a:3 + a], fng[:], ALU.mult, ALU.mult), reads=[x2, rstd, fng], writes=[x2])
                kb.dma("sp", self.Oy, self.Oy[row0 + a * 128:row0 + (a + 1) * 128, :], x2, x2[:], grp=self.g_out)

        ngr = NPB + 1
        for f_ in self.dbg:
            if f_.startswith("ngr"):
                ngr = int(f_[3:])
        wts = [9, 2]
        for f_ in self.dbg:
            if f_.startswith("wt"):
                wts = [int(x) for x in f_[2:].split("_")]
        front(0)
        def main(gi):
            mainA(gi)
            mainB(gi)
        for gi in range(ngr):
            if gi + 1 < ngr and "p2serial" not in self.dbg:
                kb.interleave([lambda gi=gi: main(gi), lambda gi=gi: front(gi + 1)], weights=wts)
            else:
                main(gi)
                if gi + 1 < ngr:
                    front(gi + 1)


def host_inputs(inp, core):
    f = lambda a: np.ascontiguousarray(a, dtype=np.float32)
    m = {}
    xs = inp["x_sample"][16 * core:16 * core + 16].reshape(128, D)
    m["xall"] = f(np.concatenate([inp["x_prompt"][core], xs], 0))
    cp = np.broadcast_to(inp["c_prompt"][core][None, :], (128, D))
    cs = np.repeat(inp["c_sample"][16 * core:16 * core + 16], 8, axis=0)
    m["crep"] = f(np.stack([cp, cs], 0))
    m["consts"] = CONSTS
    for k in ("w_ada", "b_ada", "norm1_g", "norm2_g", "w_in", "w_out", "w_glu"):
        m[k] = f(inp[k][0])
    are, aim, ldt = inp["s5_a_re"][0], inp["s5_a_im"][0], inp["s5_log_dt"][0]
    bre, bim, cre, cim = inp["s5_b_re"][0], inp["s5_b_im"][0], inp["s5_c_re"][0], inp["s5_c_im"][0]
    R = np.zeros((4, 128, 5, 64), np.float32)
    gidx = (np.arange(512) // 16).reshape(4, 128)
    hidx = (np.arange(512) % 16).reshape(4, 128)
    R[:, :, 0, :] = are[gidx]
    R[:, :, 1, :] = aim[gidx]
    R[:, :, 2, :] = ldt[gidx][..., None]
    R[:, :, 3, :] = bre[gidx, :, hidx]
    R[:, :, 4, :] = bim[gidx, :, hidx]
    m["s5R"] = f(R.transpose(1, 0, 2, 3))
    Pm = np.stack([are.T, aim.T, np.broadcast_to(ldt[None, :], (64, 32))], 1)
    m["s5P"] = f(Pm)
    PB = np.stack([bre.transpose(1, 0, 2).reshape(64, 512), bim.transpose(1, 0, 2).reshape(64, 512),
                   cre.transpose(2, 0, 1).reshape(64, 512), cim.transpose(2, 0, 1).reshape(64, 512)], 1)
    m["s5PB"] = f(PB)
    def qlay(a_gp):
        return a_gp.reshape(16, 2, 64).transpose(1, 2, 0).reshape(128, 16)
    m["s5Q"] = f(np.stack([qlay(are), qlay(aim), qlay(np.broadcast_to(ldt[:, None], (32, 64)))], 1))
    def qlay3(c_ghp):
        return c_ghp.reshape(16, 2, 16, 64).transpose(1, 3, 0, 2).reshape(128, 16, 16)
    m["s5QC"] = f(np.stack([qlay3(cre), qlay3(cim)], 1))
    sr = inp["state_s5_re"][0, 16 * core:16 * core + 16]
    si = inp["state_s5_im"][0, 16 * core:16 * core + 16]
    def qst(s):
        return s.reshape(16, 16, 2, 64).transpose(2, 3, 1, 0).reshape(128, 16, 16)
    m["s5st"] = f(np.stack([qst(sr), qst(si)], 1))
    colp = np.zeros((128, 80), np.float32)
    mu = inp["rwkv_mu"][0]
    colp[:, 0:14] = mu.reshape(14, 128).T
    colp[:, 14:28] = 0.0
    colp[:, 36:40] = inp["s5_d"][0].reshape(4, 128).T
    colp[:, 32:36] = inp["b_glu"][0].reshape(4, 128).T
    c4 = lambda a: np.asarray(a).reshape(4, 128).T
    colp[:, 40:44] = c4(inp["rwkv_w0"][0])
    colp[:, 44:48] = c4(inp["rwkv_a0"][0])
    colp[:, 48:52] = c4(inp["rwkv_k_k"][0])
    colp[:, 52:56] = c4(inp["rwkv_k_a"][0])
    colp[:, 60:64] = c4(inp["rwkv_r_k"][0].reshape(512))
    colp[:, 64:68] = c4(inp["rwkv_gn_w"][0])
    colp[:, 68:72] = c4(inp["rwkv_gn_b"][0])
    m["colp"] = colp
    wk = inp["state_wkv"][0, 16 * core:16 * core + 16]
    m["wkv0"] = f(wk.reshape(16, 4, 2, 64, 64).transpose(2, 4, 1, 0, 3).reshape(128, 4096))
    m["rowp"] = f(np.stack([inp["rwkv_gn_w"][0], inp["rwkv_gn_b"][0], inp["rwkv_gn_b"][0]], 0))
    m["w2"] = f(inp["rwkv_w2"][0])
    m["a2"] = f(inp["rwkv_a2"][0])
    m["g2"] = f(inp["rwkv_g2"][0])
    m["shift0"] = f(inp["state_shift"][0, 16 * core:16 * core + 16])
    m["w_q"] = f(inp["peer_w_q"][0])
    m["keys1"] = f(inp["peer_keys1"][0])
    m["keys2"] = f(inp["peer_keys2"][0])
    m["peer_u"] = f(inp["peer_u"][0])
    m["peer_v"] = f(inp["peer_v"][0])
    m["fng"] = f(inp["final_norm_g"])
    return m


_CACHE = {}


def kernel(**inputs):
    inp = {k: np.asarray(v) for k, v in inputs.items()}
    if "nc" not in _CACHE:
        _CACHE["nc"] = Builder().build()
    nc = _CACHE["nc"]
    in_maps = [host_inputs(inp, c) for c in range(NCORES)]
    res = run_bass_kernel_spmd(nc, in_maps, core_ids=list(range(NCORES)))
    R = res.results
    nc_ = NCORES
    y_p = np.stack([R[c]["y"][:TP] for c in range(nc_)], 0)
    y_s = np.concatenate([R[c]["y"][TP:].reshape(16, 8, D) for c in range(nc_)], 0)
    s5p = np.stack([R[c]["s5p"].reshape(2, 32, 64) for c in range(nc_)], 1)
    s5s = np.concatenate([R[c]["s5s"].reshape(2, 16, 32, 64) for c in range(nc_)], 1)
    wkvp = np.stack([R[c]["wkvp"].transpose(0, 2, 1) for c in range(nc_)], 0)
    shp = np.stack([R[c]["shp"].reshape(1792) for c in range(nc_)], 0)
    wkvs = np.concatenate([R[c]["wkvs"].reshape(2, 64, 4, 16, 64).transpose(3, 2, 0, 4, 1).reshape(16, 8, 64, 64) for c in range(nc_)], 0)
    shs = np.concatenate([R[c]["shs"] for c in range(nc_)], 0)
    f = lambda a: np.ascontiguousarray(a, dtype=np.float32)
    return (f(y_p), f(y_s), f(s5p[0][None]), f(s5p[1][None]), f(wkvp[None]), f(shp[None]),
            f(s5s[0][None]), f(s5s[1][None]), f(wkvs[None]), f(shs[None]))
```

```python
from contextlib import ExitStack
import numpy as np
import concourse.bass as bass
import concourse.mybir as mybir
from concourse.bass_utils import run_bass_kernel_spmd

F32 = mybir.dt.float32
BF16 = mybir.dt.bfloat16
I32 = mybir.dt.int32
U32 = mybir.dt.uint32
ALU = mybir.AluOpType
AF = mybir.ActivationFunctionType
AX = mybir.AxisListType

NCORES = 8
D = 1024
NB = 256
NPB = 8
TP = 2048
TS = 128
TT_ = TP + TS
INC = 2304


class TT:
    __slots__ = ("h", "w", "r", "dsem", "dcnt", "name", "psum")

    def __init__(self, h, name):
        self.h = h
        self.psum = False
        self.w = {}
        self.r = {}
        self.dsem = None
        self.dcnt = 0
        self.name = name

    def __getitem__(self, k):
        return self.h[k]


class DmaGroup:
    def __init__(self, kb, name):
        self.sem = kb.newsem("g_" + name)
        self.cnt = 0
        self.outs = []
        self.ins = []

    def close(self):
        for t in self.outs:
            t.w[self.sem] = self.cnt
        for t in self.ins:
            t.r[self.sem] = self.cnt
        self.outs = []
        self.ins = []


class KB:
    def __init__(self):
        self.nc = bass.Bass("TRN2", target_bir_lowering=False)
        nc = self.nc
        self.es = ExitStack()
        self.eng = {"pe": nc.tensor, "act": nc.scalar, "dve": nc.vector, "pool": nc.gpsimd, "sp": nc.sync}
        self.sem = {e: self.es.enter_context(nc.semaphore("s_" + e)) for e in self.eng}
        self.cnt = {e: 0 for e in self.eng}
        self.known = {e: {} for e in self.eng}
        self.nsem = len(self.eng)
        self.n_ins = 0
        self.n_wait = 0
        self.uid = 0
        self.root_es = self.es
        self.scopes = []
        self.free_ev = {}

    def scope_enter(self):
        self.scopes.append((self.es, []))
        self.es = ExitStack()

    def scope_exit(self):
        outer, tts = self.scopes.pop()
        for t in tts:
            self._merge(self.free_ev, t.w)
            self._merge(self.free_ev, t.r)
        self.es.close()
        self.es = outer

    def sb(self, name, shape, dtype=F32):
        self.uid += 1
        t = TT(self.es.enter_context(self.nc.sbuf_tensor(f"{name}_{self.uid}", list(shape), dtype)), name)
        t.w = dict(self.free_ev)
        t.r = dict(self.free_ev)
        if self.scopes:
            self.scopes[-1][1].append(t)
        return t

    def ps(self, name, shape, dtype=F32):
        self.uid += 1
        t = TT(self.root_es.enter_context(self.nc.psum_tensor(f"{name}_{self.uid}", list(shape), dtype)), name)
        t.psum = True
        return t

    def dram(self, name, shape, dtype=F32, kind="Internal"):
        return TT(self.nc.dram_tensor(name, list(shape), dtype, kind=kind).ap(), name)

    def newsem(self, name):
        self.nsem += 1
        return self.root_es.enter_context(self.nc.semaphore(f"{name}_{self.nsem}"))

    def _wait(self, e, waits):
        eng = self.eng[e]
        own = self.sem[e]
        kn = self.known[e]
        for s, v in waits.items():
            if e == "pe" and s is own:
                continue
            if kn.get(s, 0) >= v:
                continue
            eng.wait_ge(s, v)
            kn[s] = v
            self.n_wait += 1

    @staticmethod
    def _merge(d, src):
        for s, v in src.items():
            if d.get(s, 0) < v:
                d[s] = v

    def op(self, e, fn, reads=(), writes=()):
        waits = {}
        own = self.sem[e]
        for t in reads:
            self._merge(waits, t.w)
            if t.psum:
                for s_, v_ in t.r.items():
                    if s_ is not own and waits.get(s_, 0) < v_:
                        waits[s_] = v_
        for t in writes:
            for s_, v_ in t.w.items():
                if s_ is not own and waits.get(s_, 0) < v_:
                    waits[s_] = v_
            for s_, v_ in t.r.items():
                if s_ is not own and waits.get(s_, 0) < v_:
                    waits[s_] = v_
        self._wait(e, waits)
        ins = fn(self.eng[e])
        self.cnt[e] += 1
        c = self.cnt[e]
        ins.then_inc(own, 1)
        self.n_ins += 1
        for t in writes:
            t.w[own] = c
        for t in reads:
            t.r[own] = c
        self._co_yield()
        return ins

    def group(self, name):
        return DmaGroup(self, name)

    def interleave(self, fns, weights=None):
        import threading
        n = len(fns)
        weights = weights or [1] * n
        st = {"turn": 0, "left": weights[0], "alive": [True] * n, "cv": threading.Condition(), "ids": {}, "w": weights, "err": None}
        self._co = st

        def advance():
            k = st["turn"]
            for _ in range(n):
                k = (k + 1) % n
                if st["alive"][k]:
                    break
            st["turn"] = k
            st["left"] = st["w"][k]

        st["advance"] = advance

        def runner(i):
            st["ids"][threading.get_ident()] = i
            with st["cv"]:
                while st["turn"] != i:
                    st["cv"].wait()
            try:
                fns[i]()
            except BaseException as ex:
                st["err"] = ex
            finally:
                with st["cv"]:
                    st["alive"][i] = False
                    if any(st["alive"]):
                        advance()
                    st["cv"].notify_all()

        ths = [threading.Thread(target=runner, args=(i,)) for i in range(n)]
        for t in ths:
            t.start()
        for t in ths:
            t.join()
        self._co = None
        if st["err"] is not None:
            raise st["err"]

    def _co_yield(self):
        st = getattr(self, "_co", None)
        if st is None:
            return
        import threading
        i = st["ids"].get(threading.get_ident())
        if i is None:
            return
        with st["cv"]:
            st["left"] -= 1
            if st["left"] > 0:
                return
            st["advance"]()
            st["cv"].notify_all()
            while st["turn"] != i:
                st["cv"].wait()

    def dma(self, q, out_t, out_ap, in_t, in_ap, grp=None, **kw):
        if grp is not None:
            waits = {}
            self._merge(waits, in_t.w)
            self._merge(waits, out_t.w)
            self._merge(waits, out_t.r)
            self._wait(q, waits)
            ins = self.eng[q].dma_start(out=out_ap, in_=in_ap, **kw)
            ins.then_inc(grp.sem, 16)
            grp.cnt += 16
            grp.outs.append(out_t)
            in_t.r[grp.sem] = grp.cnt
            out_t.w[grp.sem] = grp.cnt
            self.n_ins += 1
            return ins
        if out_t.dsem is None:
            out_t.dsem = self.newsem("d_" + out_t.name)
        waits = {}
        self._merge(waits, in_t.w)
        for s, v in out_t.w.items():
            if s is out_t.dsem:
                continue
            if waits.get(s, 0) < v:
                waits[s] = v
        self._merge(waits, out_t.r)
        self._wait(q, waits)
        ins = self.eng[q].dma_start(out=out_ap, in_=in_ap, **kw)
        ins.then_inc(out_t.dsem, 16)
        out_t.dcnt += 16
        out_t.w[out_t.dsem] = out_t.dcnt
        in_t.r[out_t.dsem] = out_t.dcnt
        self.n_ins += 1
        return ins

    def finish(self, outs, e="sp"):
        waits = {}
        for t in outs:
            self._merge(waits, t.w)
        self._wait(e, waits)


CONST_LAYOUT = {}


def _make_consts():
    parts = []
    off = 0

    def add(name, arr):
        nonlocal off
        arr = np.asarray(arr, np.float32)
        assert arr.shape[0] == 128
        CONST_LAYOUT[name] = (off, arr.shape[1])
        parts.append(arr)
        off += arr.shape[1]

    q = np.arange(128)
    add("ident", np.eye(128))
    add("mu_s", (q[:, None] < q[None, :]))
    add("mu_i", (q[:, None] <= q[None, :]))
    add("ml_s", (q[None, :] < q[:, None]))
    add("bones", (q[:, None] // 64 == q[None, :] // 64))
    add("hsel", (q[:, None] // 64 == np.arange(2)[None, :]))
    add("bd16", (q[:, None] // 16 == q[None, :] // 16))
    add("eo", np.stack([(q // 16) % 2 == 0, (q // 16) % 2 == 1], 1))
    add("i2", np.concatenate([np.eye(64), np.eye(64)], 0))
    add("iota", np.broadcast_to(np.arange(128)[None, :], (128, 128)))
    add("ones", np.ones((128, 128)))
    add("m96", (q[:, None] >= 96))
    add("rst8", np.broadcast_to((q % 8 != 0)[None, :], (128, 128)))
    add("sel16", (q[:, None] // 8 == np.arange(16)[None, :]))
    same = (q[:, None] // 8 == q[None, :] // 8)
    add("smu_s", same & (q[:, None] < q[None, :]))
    add("smu_i", same & (q[:, None] <= q[None, :]))
    add("sml_s", same & (q[None, :] < q[:, None]))
    return np.concatenate(parts, 1)


CONSTS = _make_consts()
NCONST = CONSTS.shape[1]


class Builder:
    def __init__(self, dbg=()):
        self.kb = KB()
        self.dbg = set(dbg)
        self.dbg_out = {}

    def C(self, name, rows=slice(0, 128)):
        o, n = CONST_LAYOUT[name]
        return self.consts[rows, o:o + n]

    def cast(self, eng, out_t, out_ap, in_t, in_ap):
        if eng == "act":
            return self.kb.op("act", lambda e: e.copy(out_ap, in_ap), reads=[in_t], writes=[out_t])
        return self.kb.op(eng, lambda e: e.tensor_copy(out_ap, in_ap), reads=[in_t], writes=[out_t])

    def dbg_dump(self, name, t, ap, shape, dtype=F32):
        if name not in self.dbg:
            return
        kb = self.kb
        o = kb.dram("dbg_" + name, shape, dtype, "ExternalOutput")
        kb.dma("sp", o, o[:], t, ap, grp=self.g_out)
        self.dbg_out[name] = o

    def build(self):
        kb = self.kb
        X = lambda n, s, dt=F32: kb.dram(n, s, dt, "ExternalInput")
        O = lambda n, s, dt=F32: kb.dram(n, s, dt, "ExternalOutput")
        self.I = I = {}
        I["xall"] = X("xall", [TT_, D])
        I["crep"] = X("crep", [2, 128, D])
        I["consts"] = X("consts", [128, NCONST])
        I["w_ada"] = X("w_ada", [D, 6 * D])
        I["b_ada"] = X("b_ada", [6 * D])
        I["norm1_g"] = X("norm1_g", [D])
        I["norm2_g"] = X("norm2_g", [D])
        I["w_in"] = X("w_in", [D, INC])
        I["w_out"] = X("w_out", [D, D])
        I["w_glu"] = X("w_glu", [512, 512])
        I["s5R"] = X("s5R", [128, 4, 5, 64])
        I["s5P"] = X("s5P", [64, 3, 32])
        I["s5PB"] = X("s5PB", [64, 4, 512])
        I["s5Q"] = X("s5Q", [128, 3, 16])
        I["s5QC"] = X("s5QC", [128, 2, 16, 16])
        I["s5st"] = X("s5st", [128, 2, 16, 16])
        I["colp"] = X("colp", [128, 80])
        I["rowp"] = X("rowp", [3, 512])
        I["w2"] = X("w2", [64, 512])
        I["a2"] = X("a2", [64, 512])
        I["g2"] = X("g2", [128, 512])
        I["shift0"] = X("shift0", [16, 1792])
        I["wkv0"] = X("wkv0", [128, 4096])
        I["w_q"] = X("w_q", [D, D])
        I["keys1"] = X("keys1", [8, 128, 64])
        I["keys2"] = X("keys2", [8, 128, 64])
        I["peer_u"] = X("peer_u", [16384, D])
        I["peer_v"] = X("peer_v", [16384, D])
        I["fng"] = X("fng", [D])
        self.Oy = O("y", [TT_, D])
        self.O = {}
        self.O["s5p"] = O("s5p", [2, 16, 128])
        self.O["s5s"] = O("s5s", [2, 16, 16, 128])
        self.O["wkvp"] = O("wkvp", [8, 64, 64])
        self.O["wkvs"] = O("wkvs", [128, 4096])
        self.O["shp"] = O("shp", [14, 128])
        self.O["shs"] = O("shs", [16, 1792])
        self.x1s = kb.dram("x1s", [TT_, D], F32)
        self.g_out = kb.group("out")
        self.g_x1 = kb.group("x1")

        self.setup_common()
        kb.scope_enter()
        self.tab_alloc()
        kb.scope_enter()
        self.setup()
        nblk = NPB + 1
        for f_ in self.dbg:
            if f_.startswith("nblk"):
                nblk = int(f_[4:])

        def blocks():
            for blk in range(nblk):
                self.block(blk)
        tw = [12, 1]
        for f_ in self.dbg:
            if f_.startswith("tw"):
                tw = [int(x) for x in f_[2:].split("_")]
        if "no_p2" in self.dbg:
            blocks()
        elif "tabserial" in self.dbg or nblk == 0:
            blocks()
            self.tab_prep()
        else:
            kb.interleave([blocks, self.tab_prep], weights=tw)
        kb.scope_exit()
        kb.scope_exit()
        self.g_x1.close()
        self.dbg_dump("x1", self.x1s, self.x1s[:], [TT_, D])
        kb.scope_enter()
        if "no_p2" not in self.dbg:
            self.phase2()
        kb.scope_exit()
        self.g_out.close()
        kb.finish([self.Oy] + list(self.O.values()) + list(self.dbg_out.values()))
        return kb.nc

    def setup_common(self):
        kb = self.kb
        I = self.I
        g = kb.group("par")
        self.consts = kb.sb("consts", [128, NCONST])
        kb.dma("sp", self.consts, self.consts[:], I["consts"], I["consts"][:], grp=g)
        self.colp = kb.sb("colp", [128, 80])
        kb.dma("sp", self.colp, self.colp[:], I["colp"], I["colp"][:], grp=g)
        g.close()
        kb.op("dve", lambda e: e.tensor_scalar(self.colp[:, 14:28], self.colp[:, 0:14], -1.0, 1.0, ALU.mult, ALU.add), reads=[self.colp], writes=[self.colp])
        kb.op("dve", lambda e: e.tensor_scalar(self.colp[:, 56:60], self.colp[:, 52:56], -1.0, 1.0, ALU.mult, ALU.add), reads=[self.colp], writes=[self.colp])
        kb.op("dve", lambda e: e.tensor_scalar_mul(self.colp[:, 72:76], self.colp[:, 40:44], -1.0), reads=[self.colp], writes=[self.colp])
        self.gneps = kb.sb("gneps", [128, 1])
        kb.op("dve", lambda e: e.memset(self.gneps[:], 64e-5), writes=[self.gneps])
        self.identb = kb.sb("identb", [128, 128], BF16)
        kb.op("dve", lambda e: e.tensor_copy(self.identb[:], self.C("ident")), reads=[self.consts], writes=[self.identb])
        self.eps6 = kb.sb("eps6", [128, 1])
        kb.op("dve", lambda e: e.memset(self.eps6[:], 1e-6), writes=[self.eps6])
        self.pt = kb.ps("pt", [128, 1024], BF16)
        self.pb = [kb.ps(f"pb{i}", [128, 512], F32) for i in range(7)]


    def setup(self):
        kb = self.kb
        I = self.I
        self.wins = kb.dram("wins", [18, 128, 8, 128], BF16)
        self.g_win = kb.group("win")
        self.wcs = [kb.sb(f"wcs{i}", [128, 8, 128], BF16) for i in range(4)]
        self.woutb = kb.sb("woutb", [128, 8, D], BF16)
        self.wglub = kb.sb("wglub", [128, 4, 512], BF16)
        self.EW = kb.sb("EW", [128, 4, 2, 8, 128], BF16)
        self.KT = kb.sb("KT", [128, 4, 8, 128], BF16)
        self.pwr = kb.sb("pwr", [128, 13, 16])
        self.pwi = kb.sb("pwi", [128, 13, 16])
        self.CW = kb.sb("CW", [128, 16, 8, 2, 32], BF16)
        self.CWz = kb.sb("CWz", [128, 4, 8, 2, 64], BF16)
        self.s5c = kb.sb("s5c", [128, 2, 16])
        self.s5st = kb.sb("s5st", [128, 2, 16, 16])
        self.w2b = kb.sb("w2b", [128, 512], BF16)
        self.a2b = kb.sb("a2b", [128, 512], BF16)
        self.g2b = kb.sb("g2b", [128, 512], BF16)
        self.Zf = kb.sb("Zf", [128, 4, 64])
        self.Zb = kb.sb("Zb", [128, 4, 64], BF16)
        self.carry = kb.sb("carry", [128, 14])
        self.shiftT = kb.sb("shiftT", [128, 14, 16])
        self.shsT = kb.sb("shsT", [128, 14, 16])
        self.mod = [kb.sb(f"mod{n}", [128, D]) for n in range(3)]
        self.mod2s = kb.dram("mod2s", [2, 6, 128, D], F32)
        self.g_mod2 = [kb.group("mod2a"), kb.group("mod2b")]
        kb.scope_enter()
        g = kb.group("par2")
        g1 = kb.sb("g1", [128, D])
        kb.dma("sp", g1, g1[:], I["norm1_g"], I["norm1_g"][:].partition_broadcast(128), grp=g)
        g2n = kb.sb("g2n", [128, D])
        kb.dma("sp", g2n, g2n[:], I["norm2_g"], I["norm2_g"][:].partition_broadcast(128), grp=g)
        crep = kb.sb("crep", [128, 2, D])
        kb.dma("sp", crep, crep[:], I["crep"], I["crep"][:].rearrange("t p d -> p t d"), grp=g)
        g.close()
        bada = kb.sb("bada", [128, D])
        mtmp = [kb.sb(f"mtmp{t}", [128, D]) for t in range(2)]
        csil = kb.sb("csil", [128, 2, D], BF16)
        kb.op("act", lambda e: e.activation(csil[:], crep[:], AF.Silu), reads=[crep], writes=[csil])
        cT = kb.sb("cT", [128, 2, 8, 128], BF16)
        for typ in range(2):
            for c in range(8):
                kb.op("pe", lambda e, c=c, typ=typ: e.transpose(self.pt[:, c * 128:(c + 1) * 128], csil[:, typ, c * 128:(c + 1) * 128], self.identb[:]),
                      reads=[csil, self.identb], writes=[self.pt])
            kb.op("dve", lambda e, typ=typ: e.tensor_copy(cT[:, typ, :, :].rearrange("p c n -> p (c n)"), self.pt[:]), reads=[self.pt], writes=[cT])
        wst = [kb.sb(f"wst{i}", [128, 1024]) for i in range(3)]
        wsb = [kb.sb(f"wsb{i}", [128, 1024], BF16) for i in range(3)]
        it = 0
        for n in range(6):
            kb.dma("sp", bada, bada[:], I["b_ada"], I["b_ada"][n * 1024:(n + 1) * 1024].partition_broadcast(128))
            for kc in range(8):
                s = it % 3
                it += 1
                kb.dma("sp", wst[s], wst[s][:], I["w_ada"], I["w_ada"][kc * 128:(kc + 1) * 128, n * 1024:(n + 1) * 1024])
                self.cast("act" if it % 2 else "dve", wsb[s], wsb[s][:], wst[s], wst[s][:])
                for typ in range(2):
                    for hf in range(2):
                        kb.op("pe", lambda e, s=s, typ=typ, hf=hf, kc=kc: e.matmul(self.pb[typ * 2 + hf][:], cT[:, typ, kc, :], wsb[s][:, hf * 512:(hf + 1) * 512], start=(kc == 0), stop=(kc == 7)),
                              reads=[cT, wsb[s]], writes=[self.pb[typ * 2 + hf]])
            for typ in range(2):
                res = (typ == 0 and n < 3)
                m = self.mod[n] if res else mtmp[typ]
                for hf in range(2):
                    sl = slice(hf * 512, (hf + 1) * 512)
                    kb.op("dve", lambda e, m=m, typ=typ, hf=hf, sl=sl, n=n: e.tensor_tensor(m[:, sl], self.pb[typ * 2 + hf][:], bada[:, sl], ALU.add),
                          reads=[self.pb[typ * 2 + hf], bada], writes=[m])
                if n in (1, 4):
                    gg = g1 if n == 1 else g2n
                    kb.op("pool", lambda e, m=m: e.tensor_scalar_add(m[:], m[:], 1.0), reads=[m], writes=[m])
                    kb.op("pool", lambda e, m=m, gg=gg: e.tensor_mul(m[:], m[:], gg[:]), reads=[m, gg], writes=[m])
                if not res:
                    kb.dma("sp", self.mod2s, self.mod2s[typ, n], m, m[:], grp=self.g_mod2[typ])
        for g_ in self.g_mod2:
            g_.close()
        kb.scope_exit()
        kb.scope_enter()
        if "su1" in self.dbg:
            kb.scope_exit()
            return
        wst2 = [kb.sb(f"wst2{i}", [128, INC]) for i in range(2)]
        wsb2 = [kb.sb(f"wsb2{i}", [128, INC], BF16) for i in range(2)]
        for kc in range(8):
            s = kc % 2
            kb.dma("sp", wst2[s], wst2[s][:], I["w_in"], I["w_in"][kc * 128:(kc + 1) * 128, :])
            self.cast("act" if kc % 2 else "dve", wsb2[s], wsb2[s][:], wst2[s], wst2[s][:])
            kb.dma("sp", self.wins, self.wins[:, :, kc, :].rearrange("c p n -> p c n"), wsb2[s], wsb2[s][:].rearrange("p (c n) -> p c n", c=18), grp=self.g_win)
        for kc in range(8):
            s = kc % 2
            kb.dma("sp", wst2[s], wst2[s][:, 0:D], I["w_out"], I["w_out"][kc * 128:(kc + 1) * 128, :])
            self.cast("act" if kc % 2 else "dve", self.woutb, self.woutb[:, kc, :], wst2[s], wst2[s][:, 0:D])
        for kc in range(4):
            s = kc % 2
            kb.dma("sp", wst2[s], wst2[s][:, 0:512], I["w_glu"], I["w_glu"][kc * 128:(kc + 1) * 128, :])
            self.cast("act" if kc % 2 else "dve", self.wglub, self.wglub[:, kc, :], wst2[s], wst2[s][:, 0:512])
        kb.scope_exit()
        self.g_win.close()
        if "su2" in self.dbg:
            return
        kb.scope_enter()
        self.setup_s5()
        kb.scope_exit()
        if "su3" in self.dbg:
            return
        self.setup_rwkv()
        kb.op("dve", lambda e: e.memset(self.carry[:], 0.0), writes=[self.carry])

    def s5_lambda(self, name, are, aim, ldt, shape, np_, srcs):
        kb = self.kb
        F = shape[1]
        mk = lambda n: kb.sb(f"{name}_{n}", [128, F])
        dt, mag, ang, t1, t2, fi = mk("dt"), mk("mag"), mk("ang"), mk("t1"), mk("t2"), kb.sb(f"{name}_fi", [128, F], I32)
        lbr, lbi, cfr, cfi = mk("lbr"), mk("lbi"), mk("cfr"), mk("cfi")
        P = slice(0, np_)
        kb.op("act", lambda e: e.activation(dt[P], ldt, AF.Exp), reads=srcs, writes=[dt])
        kb.op("dve", lambda e: e.tensor_tensor(mag[P], are, dt[P], ALU.mult), reads=srcs + [dt], writes=[mag])
        kb.op("act", lambda e: e.activation(mag[P], mag[P], AF.Exp), reads=[mag], writes=[mag])
        kb.op("dve", lambda e: e.tensor_tensor(ang[P], aim, dt[P], ALU.mult), reads=srcs + [dt], writes=[ang])

        def sin_of(dst, shift):
            kb.op("dve", lambda e: e.tensor_scalar(t1[P], ang[P], 1.0 / (2 * np.pi), 0.5 + shift / (2 * np.pi), ALU.mult, ALU.add), reads=[ang], writes=[t1])
            kb.op("dve", lambda e: e.tensor_copy(fi[P], t1[P]), reads=[t1], writes=[fi])
            kb.op("dve", lambda e: e.tensor_copy(t2[P], fi[P]), reads=[fi], writes=[t2])
            kb.op("dve", lambda e: e.tensor_tensor(t1[P], t1[P], t2[P], ALU.subtract), reads=[t1, t2], writes=[t1])
            kb.op("dve", lambda e: e.tensor_single_scalar(t2[P], t1[P], 0.0, ALU.is_lt), reads=[t1], writes=[t2])
            kb.op("dve", lambda e: e.tensor_tensor(t1[P], t1[P], t2[P], ALU.add), reads=[t1, t2], writes=[t1])
            kb.op("dve", lambda e: e.tensor_scalar(t1[P], t1[P], 2 * np.pi, -np.pi, ALU.mult, ALU.add), reads=[t1], writes=[t1])
            kb.op("dve", lambda e: e.tensor_scalar(t1[P], t1[P], -np.pi, np.pi, ALU.max, ALU.min), reads=[t1], writes=[t1])
            kb.op("act", lambda e: e.activation(dst[P], t1[P], AF.Sin), reads=[t1], writes=[dst])

        sin_of(lbi, 0.0)
        sin_of(lbr, np.pi / 2)
        kb.op("dve", lambda e: e.tensor_tensor(lbr[P], lbr[P], mag[P], ALU.mult), reads=[lbr, mag], writes=[lbr])
        kb.op("dve", lambda e: e.tensor_tensor(lbi[P], lbi[P], mag[P], ALU.mult), reads=[lbi, mag], writes=[lbi])
        den, nre = mk("den"), mk("nre")
        kb.op("dve", lambda e: e.tensor_tensor(den[P], are, are, ALU.mult), reads=srcs, writes=[den])
        kb.op("dve", lambda e: e.tensor_tensor(t1[P], aim, aim, ALU.mult), reads=srcs, writes=[t1])
        kb.op("dve", lambda e: e.tensor_tensor(den[P], den[P], t1[P], ALU.add), reads=[den, t1], writes=[den])
        kb.op("dve", lambda e: e.reciprocal(den[P], den[P]), reads=[den], writes=[den])
        kb.op("dve", lambda e: e.tensor_scalar_add(nre[P], lbr[P], -1.0), reads=[lbr], writes=[nre])
        kb.op("dve", lambda e: e.tensor_tensor(t1[P], nre[P], are, ALU.mult), reads=srcs + [nre], writes=[t1])
        kb.op("dve", lambda e: e.tensor_tensor(t2[P], lbi[P], aim, ALU.mult), reads=srcs + [lbi], writes=[t2])
        kb.op("dve", lambda e: e.tensor_tensor(t1[P], t1[P], t2[P], ALU.add), reads=[t1, t2], writes=[t1])
        kb.op("dve", lambda e: e.tensor_tensor(cfr[P], t1[P], den[P], ALU.mult), reads=[t1, den], writes=[cfr])
        kb.op("dve", lambda e: e.tensor_tensor(t1[P], lbi[P], are, ALU.mult), reads=srcs + [lbi], writes=[t1])
        kb.op("dve", lambda e: e.tensor_tensor(t2[P], nre[P], aim, ALU.mult), reads=srcs + [nre], writes=[t2])
        kb.op("dve", lambda e: e.tensor_tensor(t1[P], t1[P], t2[P], ALU.subtract), reads=[t1, t2], writes=[t1])
        kb.op("dve", lambda e: e.tensor_tensor(cfi[P], t1[P], den[P], ALU.mult), reads=[t1, den], writes=[cfi])
        return lbr, lbi, cfr, cfi

    def cmul(self, P, outr, outi, ar, ai, br, bi, tmp, rd, wr, eng="dve"):
        kb = self.kb
        t1, t2 = tmp
        kb.op(eng, lambda e: e.tensor_tensor(t1, ar, br, ALU.mult), reads=rd, writes=wr)
        kb.op(eng, lambda e: e.tensor_tensor(t2, ai, bi, ALU.mult), reads=rd, writes=wr)
        kb.op(eng, lambda e: e.tensor_tensor(t1, t1, t2, ALU.subtract), reads=rd, writes=wr)
        kb.op(eng, lambda e: e.tensor_tensor(t2, ar, bi, ALU.mult), reads=rd, writes=wr)
        kb.op(eng, lambda e: e.tensor_tensor(outi, ai, br, ALU.mult), reads=rd, writes=wr)
        kb.op(eng, lambda e: e.tensor_tensor(outi, outi, t2, ALU.add), reads=rd, writes=wr)
        kb.op(eng, lambda e: e.tensor_copy(outr, t1), reads=rd, writes=wr)

    def setup_s5(self):
        kb = self.kb
        I = self.I
        g = kb.group("s5par")
        sR = kb.sb("sR", [128, 4, 5, 64])
        kb.dma("sp", sR, sR[:], I["s5R"], I["s5R"][:], grp=g)
        sP = kb.sb("sP", [128, 3, 32])
        kb.dma("sp", sP, sP[0:64], I["s5P"], I["s5P"][:], grp=g)
        sPB = kb.sb("sPB", [128, 4, 512])
        kb.dma("sp", sPB, sPB[0:64], I["s5PB"], I["s5PB"][:], grp=g)
        sQ = kb.sb("sQ", [128, 3, 16])
        kb.dma("sp", sQ, sQ[:], I["s5Q"], I["s5Q"][:], grp=g)
        sQC = kb.sb("sQC", [128, 2, 16, 16])
        kb.dma("sp", sQC, sQC[:], I["s5QC"], I["s5QC"][:], grp=g)
        kb.dma("sp", self.s5st, self.s5st[:], I["s5st"], I["s5st"][:], grp=g)
        g.close()
        def chainR():
            rr = kb.sb("rr", [128, 5, 256])
            for k in range(5):
                kb.op("dve", lambda e, k=k: e.tensor_copy(rr[:, k, :].rearrange("p (t q) -> p t q", t=4), sR[:, :, k, :]), reads=[sR], writes=[rr])
            lbr, lbi, cfr, cfi = self.s5_lambda("R", rr[:, 0, :], rr[:, 1, :], rr[:, 2, :], [128, 256], 128, [rr])
            cur_r, cur_i = kb.sb("curRr", [128, 256]), kb.sb("curRi", [128, 256])
            ta, tb = kb.sb("taR", [128, 256]), kb.sb("tbR", [128, 256])
            allr = [rr, lbr, lbi, cfr, cfi, cur_r, cur_i, ta, tb]
            self.cmul(None, cur_r[:], cur_i[:], cfr[:], cfi[:], rr[:, 3, :], rr[:, 4, :], (ta[:], tb[:]), allr, [cur_r, cur_i, ta, tb])
            eo = self.C("eo")
            for d in range(8):
                i = 7 - d
                for c, cur in enumerate((cur_r, cur_i)):
                    for g2 in range(2):
                        kb.op("pool", lambda e, c=c, cur=cur, g2=g2, i=i: e.tensor_scalar(
                            self.EW[:, :, c, i, g2 * 64:(g2 + 1) * 64], cur[:].rearrange("p (t q) -> p t q", t=4), eo[:, g2:g2 + 1], None, ALU.mult),
                            reads=[cur, self.consts], writes=[self.EW])
                if d < 7:
                    self.cmul(None, cur_r[:], cur_i[:], cur_r[:], cur_i[:], lbr[:], lbi[:], (ta[:], tb[:]), allr, [cur_r, cur_i, ta, tb])

        def chainP():
            P64 = slice(0, 64)
            lbrP, lbiP, cfrP, cfiP = self.s5_lambda("P", sP[P64, 0, :], sP[P64, 1, :], sP[P64, 2, :], [64, 32], 64, [sP])
            cpr, cpi = kb.sb("cpr", [128, 512]), kb.sb("cpi", [128, 512])
            tpa, tpb = kb.sb("tpa", [128, 512]), kb.sb("tpb", [128, 512])
            nci = kb.sb("nci", [128, 512])
            allp = [sPB, lbrP, lbiP, cfrP, cfiP, cpr, cpi, tpa, tpb]
            bc = lambda t: t[P64, :].to_broadcast([64, 32, 16]) if False else t[P64, :].unsqueeze(2).to_broadcast([64, 32, 16])
            v3 = lambda ap: ap.rearrange("p (g h) -> p g h", h=16)
            self.cmul(None, v3(cpr[P64]), v3(cpi[P64]), bc(cfrP), bc(cfiP), v3(sPB[P64, 0, :]), v3(sPB[P64, 1, :]), (v3(tpa[P64]), v3(tpb[P64])), allp, [cpr, cpi, tpa, tpb])
            kb.op("dve", lambda e: e.tensor_scalar_mul(nci[P64], sPB[P64, 3, :], -1.0), reads=[sPB], writes=[nci])
            dsk = self.colp[:, 36:40]
            for d in range(8):
                for t in range(4):
                    sl = slice(t * 128, (t + 1) * 128)
                    pk = self.pb[4 + (t % 2)]
                    kb.op("pe", lambda e, sl=sl, pk=pk: e.matmul(pk[:, 0:128], cpr[P64, sl], sPB[P64, 2, sl], start=True, stop=False), reads=[cpr, sPB], writes=[pk])
                    kb.op("pe", lambda e, sl=sl, pk=pk: e.matmul(pk[:, 0:128], cpi[P64, sl], nci[P64, sl], start=False, stop=True), reads=[cpi, nci], writes=[pk])
                    if d == 0:
                        kb.op("dve", lambda e, pk=pk: e.tensor_tensor(tpa[:, 0:128], pk[:, 0:128], self.C("bd16"), ALU.mult), reads=[pk, self.consts], writes=[tpa])
                        kb.op("dve", lambda e, t=t: e.scalar_tensor_tensor(self.KT[:, t, 0, :], self.C("ident"), dsk[:, t:t + 1], tpa[:, 0:128], ALU.mult, ALU.add),
                              reads=[tpa, self.consts, self.colp], writes=[self.KT])
                    else:
                        kb.op("dve", lambda e, pk=pk, t=t, d=d: e.tensor_tensor(self.KT[:, t, d, :], pk[:, 0:128], self.C("bd16"), ALU.mult), reads=[pk, self.consts], writes=[self.KT])
                if d < 7:
                    self.cmul(None, v3(cpr[P64]), v3(cpi[P64]), v3(cpr[P64]), v3(cpi[P64]), bc(lbrP), bc(lbiP), (v3(tpa[P64]), v3(tpb[P64])), allp, [cpr, cpi, tpa, tpb])

        def chainQ():
            lbrQ, lbiQ, _, _ = self.s5_lambda("Q", sQ[:, 0, :], sQ[:, 1, :], sQ[:, 2, :], [128, 16], 128, [sQ])
            tq = kb.sb("tq", [128, 2, 16])
            allq = [lbrQ, lbiQ, self.pwr, self.pwi, tq]
            kb.op("dve", lambda e: e.tensor_copy(self.pwr[:, 0, :], lbrQ[:]), reads=[lbrQ], writes=[self.pwr])
            kb.op("dve", lambda e: e.tensor_copy(self.pwi[:, 0, :], lbiQ[:]), reads=[lbiQ], writes=[self.pwi])
            for j in range(1, 8):
                self.cmul(None, self.pwr[:, j, :], self.pwi[:, j, :], self.pwr[:, j - 1, :], self.pwi[:, j - 1, :], lbrQ[:], lbiQ[:], (tq[:, 0, :], tq[:, 1, :]), allq, [self.pwr, self.pwi, tq])
            for j in range(8, 12):
                self.cmul(None, self.pwr[:, j, :], self.pwi[:, j, :], self.pwr[:, j - 1, :], self.pwi[:, j - 1, :], self.pwr[:, j - 1, :], self.pwi[:, j - 1, :], (tq[:, 0, :], tq[:, 1, :]), allq, [self.pwr, self.pwi, tq])
            kb.op("pool", lambda e: e.memset(self.CW[:], 0.0), writes=[self.CW])
            qa, qb, qc = kb.sb("qa", [128, 16, 16]), kb.sb("qb", [128, 16, 16]), kb.sb("qc", [128, 16, 16])
            pb_ = lambda t, j: t[:, j, :].unsqueeze(2).to_broadcast([128, 16, 16])
            for j in range(8):
                kb.op("dve", lambda e, j=j: e.tensor_tensor(qa[:], sQC[:, 0, :, :], pb_(self.pwr, j), ALU.mult), reads=[sQC, self.pwr], writes=[qa])
                kb.op("dve", lambda e, j=j: e.tensor_tensor(qb[:], sQC[:, 1, :, :], pb_(self.pwi, j), ALU.mult), reads=[sQC, self.pwi], writes=[qb])
                kb.op("dve", lambda e: e.tensor_tensor(qc[:], qa[:], qb[:], ALU.subtract), reads=[qa, qb], writes=[qc])
                for g2 in range(2):
                    Pq = slice(g2 * 64, (g2 + 1) * 64)
                    kb.op("dve", lambda e, j=j, g2=g2, Pq=Pq: e.tensor_copy(self.CW[Pq, :, j, 0, g2 * 16:(g2 + 1) * 16], qc[Pq]), reads=[qc], writes=[self.CW])
                kb.op("dve", lambda e, j=j: e.tensor_tensor(qa[:], sQC[:, 0, :, :], pb_(self.pwi, j), ALU.mult), reads=[sQC, self.pwi], writes=[qa])
                kb.op("dve", lambda e, j=j: e.tensor_tensor(qb[:], sQC[:, 1, :, :], pb_(self.pwr, j), ALU.mult), reads=[sQC, self.pwr], writes=[qb])
                kb.op("dve", lambda e: e.scalar_tensor_tensor(qc[:], qa[:], -1.0, qb[:], ALU.mult, ALU.subtract), reads=[qa, qb], writes=[qc])
                for g2 in range(2):
                    Pq = slice(g2 * 64, (g2 + 1) * 64)
                    kb.op("dve", lambda e, j=j, g2=g2, Pq=Pq: e.tensor_copy(self.CW[Pq, :, j, 1, g2 * 16:(g2 + 1) * 16], qc[Pq]), reads=[qc], writes=[self.CW])

        kb.interleave([chainR, chainP, chainQ])
        kb.op("pool", lambda e: e.memset(self.CWz[:], 0.0), writes=[self.CWz])
        for t in range(4):
            kb.op("pool", lambda e, t=t: e.tensor_copy(self.CWz[:, t, :, :, 32:64], self.CW[:, 4 * t + 3, :, :, :]), reads=[self.CW], writes=[self.CWz])
        kb.op("dve", lambda e: e.memset(self.s5c[:], 0.0), writes=[self.s5c])

    def setup_rwkv(self):
        kb = self.kb
        I = self.I
        kb.scope_enter()
        g = kb.group("rwpar")
        sh0 = kb.sb("sh0", [128, 1792])
        kb.dma("sp", sh0, sh0[0:16, :], I["shift0"], I["shift0"][:], grp=g)
        wl = kb.sb("wl", [128, 3, 512])
        kb.dma("sp", wl, wl[0:64, 0, :], I["w2"], I["w2"][:], grp=g)
        kb.dma("sp", wl, wl[64:128, 1, :], I["a2"], I["a2"][:], grp=g)
        kb.dma("sp", wl, wl[:, 2, :], I["g2"], I["g2"][:], grp=g)
        g.close()
        kb.op("dve", lambda e: e.tensor_copy(self.w2b[0:64, :], wl[0:64, 0, :]), reads=[wl], writes=[self.w2b])
        kb.op("dve", lambda e: e.tensor_copy(self.a2b[64:128, :], wl[64:128, 1, :]), reads=[wl], writes=[self.a2b])
        kb.op("dve", lambda e: e.tensor_copy(self.g2b[:], wl[:, 2, :]), reads=[wl], writes=[self.g2b])
        kb.op("dve", lambda e: e.memset(self.Zf[:], 0.0), writes=[self.Zf])
        kb.op("dve", lambda e: e.memset(self.Zb[:], 0.0), writes=[self.Zb])
        pk = self.pb[6]
        for r in range(14):
            kb.op("pe", lambda e, r=r: e.transpose(pk[:, r * 16:(r + 1) * 16], sh0[0:16, r * 128:(r + 1) * 128], self.C("ident", slice(0, 16))[:, 0:16]),
                  reads=[sh0, self.consts], writes=[pk])
        kb.op("dve", lambda e: e.tensor_copy(self.shiftT[:].rearrange("p r b -> p (r b)"), pk[:, 0:224]), reads=[pk], writes=[self.shiftT])
        kb.scope_exit()

    def block(self, blk):
        kb = self.kb
        I = self.I
        sample = blk == NPB
        nb = TS if sample else NB
        ntile = nb // 128
        typ = 1 if sample else 0
        row0 = TP if sample else blk * NB
        kb.scope_enter()
        self.uT = kb.sb("uT", [128, 4, nb], BF16)
        self.psT = kb.sb("psT", [128, 14, nb])
        self.ycat = kb.sb("ycat", [128, 8, nb], BF16)
        kb.scope_enter()
        self.xblk = kb.sb("xblk", [128, 1, D])
        self.sqj = kb.sb("sqj", [128, D], BF16)
        self.ss = kb.sb("ss", [128, 2])
        self.rstd = kb.sb("rstd", [128, 2])
        self.t1k = kb.sb("t1k", [128, D])
        self.hb = kb.sb("hb", [128, D], BF16)
        self.hT = kb.sb("hT", [128, 8, nb], BF16)
        self.tsh = kb.sb("tsh", [128, nb])
        A1, SH1, GA1 = self.mod[1], self.mod[0], self.mod[2]
        if sample:
            for n in range(3):
                kb.dma("sp", self.mod[n], self.mod[n][:], self.mod2s, self.mod2s[1, n])
        for j in range(ntile):
            kb.dma("sp", self.xblk, self.xblk[:, 0, :], I["xall"], I["xall"][row0 + j * 128:row0 + (j + 1) * 128, :])
            kb.op("act", lambda e, j=j: e.activation(self.sqj[:], self.xblk[:, 0, :], AF.Square, accum_out=self.ss[:, j:j + 1]), reads=[self.xblk], writes=[self.sqj, self.ss])
            kb.op("act", lambda e, j=j: e.activation(self.rstd[:, j:j + 1], self.ss[:, j:j + 1], AF.Sqrt, bias=self.eps6[:], scale=1.0 / D), reads=[self.ss, self.eps6], writes=[self.rstd])
            kb.op("dve", lambda e, j=j: e.reciprocal(self.rstd[:, j:j + 1], self.rstd[:, j:j + 1]), reads=[self.rstd], writes=[self.rstd])
            kb.op("dve", lambda e, j=j: e.scalar_tensor_tensor(self.t1k[:], self.xblk[:, 0, :], self.rstd[:, j:j + 1], A1[:], ALU.mult, ALU.mult), reads=[self.xblk, self.rstd, A1], writes=[self.t1k])
            kb.op("dve", lambda e: e.tensor_tensor(self.hb[:], self.t1k[:], SH1[:], ALU.add), reads=[self.t1k, SH1], writes=[self.hb])
            for c in range(8):
                kb.op("pe", lambda e, c=c: e.transpose(self.pt[:, c * 128:(c + 1) * 128], self.hb[:, c * 128:(c + 1) * 128], self.identb[:]), reads=[self.hb, self.identb], writes=[self.pt])
            kb.op("act", lambda e, j=j: e.copy(self.hT[:, :, j * 128:(j + 1) * 128], self.pt[:].rearrange("p (c n) -> p c n", c=8)), reads=[self.pt], writes=[self.hT])
        wcs = self.wcs
        for cc in range(3):
            kb.dma("sp", wcs[cc % 4], wcs[cc % 4][:], self.wins, self.wins[cc])
        for cc in range(18):
            pk = self.pb[cc % 2]
            wc = wcs[cc % 4]
            if cc + 3 < 18:
                kb.dma("sp", wcs[(cc + 3) % 4], wcs[(cc + 3) % 4][:], self.wins, self.wins[cc + 3])
            for k in range(8):
                kb.op("pe", lambda e, cc=cc, k=k, pk=pk: e.matmul(pk[:, 0:nb], wc[:, k, :], self.hT[:, k, 0:nb], start=(k == 0), stop=(k == 7)),
                      reads=[wc, self.hT], writes=[pk])
            if cc < 4:
                kb.op("act", lambda e, cc=cc, pk=pk: e.copy(self.uT[:, cc, 0:nb], pk[:, 0:nb]), reads=[pk], writes=[self.uT])
            else:
                r = cc - 4
                mu = self.colp[:, r:r + 1]
                omm = self.colp[:, 14 + r:15 + r]
                kb.op("act", lambda e, pk=pk, omm=omm: e.activation(self.tsh[:, 0:nb], pk[:, 0:nb], AF.Identity, scale=omm), reads=[pk, self.colp], writes=[self.tsh])
                kb.op("dve", lambda e, pk=pk, r=r, mu=mu: e.scalar_tensor_tensor(self.psT[:, r, 1:nb], pk[:, 0:nb - 1], mu, self.tsh[:, 1:nb], ALU.mult, ALU.add),
                      reads=[pk, self.colp, self.tsh], writes=[self.psT])
                if not sample:
                    kb.op("dve", lambda e, r=r, mu=mu: e.scalar_tensor_tensor(self.psT[:, r, 0:1], self.carry[:, r:r + 1], mu, self.tsh[:, 0:1], ALU.mult, ALU.add),
                          reads=[self.carry, self.colp, self.tsh], writes=[self.psT])
                    kb.op("dve", lambda e, r=r, pk=pk: e.tensor_copy(self.carry[:, r:r + 1], pk[:, nb - 1:nb]), reads=[pk], writes=[self.carry])
                else:
                    kb.op("dve", lambda e, r=r, mu=mu: e.scalar_tensor_tensor(self.psT[:, r, 0:nb:8], self.shiftT[:, r, :], mu, self.tsh[:, 0:nb:8], ALU.mult, ALU.add),
                          reads=[self.shiftT, self.colp, self.tsh], writes=[self.psT])
                    kb.op("dve", lambda e, r=r, pk=pk: e.tensor_copy(self.shsT[:, r, :], pk[:, 7:nb:8]), reads=[pk], writes=[self.shsT])
        kb.scope_exit()
        if blk == 0:
            self.dbg_dump("uT0", self.uT, self.uT[:], [128, 4, NB], BF16)
            self.dbg_dump("psT0", self.psT, self.psT[:], [128, 14, NB])
        kb.scope_enter()
        if "skip_s5" not in self.dbg:
            self.s5_block(blk, sample, nb)
        kb.scope_exit()
        kb.scope_enter()
        if "skip_rw" not in self.dbg:
            self.rwkv_block(blk, sample, nb)
        kb.scope_exit()
        kb.scope_enter()
        self.outproj(blk, sample, nb, ntile, row0, GA1)
        kb.scope_exit()
        kb.scope_exit()

    def s5_block(self, blk, sample, nb):
        kb = self.kb
        nsb = nb // 8
        if True:
            self.Ea = kb.sb("Ea", [128, 2, 16, 32])
            self.Eb = kb.sb("Eb", [128, 2, 16, 32])
            self.hst = kb.sb("hst", [128, 2, 16, 32])
            self.Cin = kb.sb("Cin", [128, 2, 16, 32], BF16)
            self.ypre = kb.sb("ypre", [128, 4, NB])
            self.ygb = kb.sb("ygb", [128, 4, NB], BF16)
            self.sig = kb.sb("sig", [128, NB])
            self.s5tmp = kb.sb("s5tmp", [128, 4, 16])
        uT, EW, KT, CW, CWz = self.uT, self.EW, self.KT, self.CW, self.CWz
        if True:
            self.uTz = kb.sb("uTz", [128, 4, NB], BF16)
        uTz = self.uTz
        kb.op("pool", lambda e: e.tensor_scalar(uTz[64:128, :, 0:nb], uT[64:128, :, 0:nb], self.C("m96", slice(64, 128)), None, ALU.mult), reads=[uT, self.consts], writes=[uTz])
        pE = [self.pb[2], self.pb[3]]
        for pair in range(16):
            tile, r0 = pair // 4, 32 * (pair % 4)
            kk_ = 32
            src = uT
            if pair % 4 == 3:
                r0, kk_, src = 64, 64, uTz
            for c in range(2):
                for i in range(8):
                    kb.op("pe", lambda e, pair=pair, tile=tile, r0=r0, c=c, i=i, kk_=kk_, src=src: e.matmul(
                        pE[c][:, pair * nsb:(pair + 1) * nsb], EW[r0:r0 + kk_, tile, c, i, :], src[r0:r0 + kk_, tile, i:nb:8], start=(i == 0), stop=(i == 7)),
                        reads=[EW, src], writes=[pE[c]])
        A, B = self.Ea, self.Eb
        for c in range(2):
            kb.op("act", lambda e, c=c: e.copy(A[:, c, :, 0:nsb], pE[c][:, 0:16 * nsb].rearrange("p (a b) -> p a b", a=16)), reads=[pE[c]], writes=[A])
        tt = lambda out, a, b, op, rd, wr: kb.op("dve", lambda e: e.tensor_tensor(out, a, b, op), reads=rd, writes=wr)
        pw, pwi = self.pwr, self.pwi
        tmp = self.s5tmp
        if not sample:
            cr, ci = self.s5c[:, 0, :], self.s5c[:, 1, :]
            rd = [pw, pwi, self.s5c, tmp, A]
            tt(tmp[:, 0, :], pw[:, 7, :], cr, ALU.mult, rd, [tmp])
            tt(tmp[:, 1, :], pwi[:, 7, :], ci, ALU.mult, rd, [tmp])
            tt(tmp[:, 2, :], pw[:, 7, :], ci, ALU.mult, rd, [tmp])
            tt(tmp[:, 3, :], pwi[:, 7, :], cr, ALU.mult, rd, [tmp])
            tt(tmp[:, 0, :], tmp[:, 0, :], tmp[:, 1, :], ALU.subtract, rd, [tmp])
            tt(tmp[:, 2, :], tmp[:, 2, :], tmp[:, 3, :], ALU.add, rd, [tmp])
            tt(A[:, 0, :, 0], A[:, 0, :, 0], tmp[:, 0, :], ALU.add, rd, [A])
            tt(A[:, 1, :, 0], A[:, 1, :, 0], tmp[:, 2, :], ALU.add, rd, [A])
            sft, k = 1, 0
            while sft < nsb:
                n = nsb - sft
                bcr = pw[:, 7 + k, :].unsqueeze(2).to_broadcast([128, 16, n])
                bci = pwi[:, 7 + k, :].unsqueeze(2).to_broadcast([128, 16, n])
                T = self.hst
                rd = [A, pw, pwi, T]
                tt(T[:, 0, :, 0:n], A[:, 0, :, 0:n], bcr, ALU.mult, rd, [T])
                tt(T[:, 1, :, 0:n], A[:, 1, :, 0:n], bci, ALU.mult, rd, [T])
                tt(T[:, 0, :, 0:n], T[:, 0, :, 0:n], T[:, 1, :, 0:n], ALU.subtract, rd, [T])
                tt(B[:, 0, :, sft:nsb], A[:, 0, :, sft:nsb], T[:, 0, :, 0:n], ALU.add, rd, [B])
                tt(T[:, 0, :, 0:n], A[:, 0, :, 0:n], bci, ALU.mult, rd, [T])
                tt(T[:, 1, :, 0:n], A[:, 1, :, 0:n], bcr, ALU.mult, rd, [T])
                tt(T[:, 0, :, 0:n], T[:, 0, :, 0:n], T[:, 1, :, 0:n], ALU.add, rd, [T])
                tt(B[:, 1, :, sft:nsb], A[:, 1, :, sft:nsb], T[:, 0, :, 0:n], ALU.add, rd, [B])
                kb.op("pool", lambda e, A=A, B=B, sft=sft: e.tensor_copy(B[:, :, :, 0:sft], A[:, :, :, 0:sft]), reads=[A], writes=[B])
                A, B = B, A
                sft *= 2
                k += 1
            kb.op("dve", lambda e: e.tensor_copy(self.Cin[:, :, :, 0], self.s5c[:]), reads=[self.s5c], writes=[self.Cin])
            kb.op("dve", lambda e, A=A: e.tensor_copy(self.Cin[:, :, :, 1:nsb], A[:, :, :, 0:nsb - 1]), reads=[A], writes=[self.Cin])
            kb.op("dve", lambda e, A=A: e.tensor_copy(self.s5c[:], A[:, :, :, nsb - 1]), reads=[A, self.Cin], writes=[self.s5c])
            if blk == NPB - 1:
                pk = self.pb[4]
                for c in range(2):
                    kb.op("pe", lambda e, c=c: e.transpose(pk[0:16, c * 128:(c + 1) * 128], self.s5c[:, c, :], self.C("ident")), reads=[self.s5c, self.consts], writes=[pk])
                kb.op("dve", lambda e: e.tensor_copy(self.hst[0:16, 0, 0, 0:256] if False else self.sig[0:16, 0:256], pk[0:16, 0:256]), reads=[pk], writes=[self.sig])
                kb.dma("sp", self.O["s5p"], self.O["s5p"][:].rearrange("c a q -> a c q"), self.sig, self.sig[0:16, 0:256].rearrange("a (c q) -> a c q", c=2), grp=self.g_out)
        else:
            st = self.s5st
            kb.op("pool", lambda e: e.tensor_copy(self.Cin[:, :, :, 0:16], st[:]), reads=[st], writes=[self.Cin])
            F_ = B
            bcr = pw[:, 7, :].unsqueeze(2).to_broadcast([128, 16, 16])
            bci = pwi[:, 7, :].unsqueeze(2).to_broadcast([128, 16, 16])
            T = self.hst
            rd = [A, pw, pwi, T, st]
            tt(T[:, 0, :, 0:16], st[:, 0], bcr, ALU.mult, rd, [T])
            tt(T[:, 1, :, 0:16], st[:, 1], bci, ALU.mult, rd, [T])
            tt(T[:, 0, :, 0:16], T[:, 0, :, 0:16], T[:, 1, :, 0:16], ALU.subtract, rd, [T])
            tt(F_[:, 0, :, 0:16], A[:, 0, :, 0:16], T[:, 0, :, 0:16], ALU.add, rd, [F_])
            tt(T[:, 0, :, 0:16], st[:, 0], bci, ALU.mult, rd, [T])
            tt(T[:, 1, :, 0:16], st[:, 1], bcr, ALU.mult, rd, [T])
            tt(T[:, 0, :, 0:16], T[:, 0, :, 0:16], T[:, 1, :, 0:16], ALU.add, rd, [T])
            tt(F_[:, 1, :, 0:16], A[:, 1, :, 0:16], T[:, 0, :, 0:16], ALU.add, rd, [F_])
            pk = self.pb[4]
            for c in range(2):
                for q4 in range(4):
                    for a in range(4):
                        kb.op("pe", lambda e, c=c, q4=q4, a=a: e.transpose(pk[0:16, a * 128:(a + 1) * 128], F_[:, c, q4 * 4 + a, 0:16], self.C("ident")), reads=[F_, self.consts], writes=[pk])
                    kb.op("dve", lambda e: e.tensor_copy(self.ypre[0:16, 0, 0:512] if False else self.ypre[0:16, 0:2, :].rearrange("p a b -> p (a b)"), pk[0:16, 0:512]), reads=[pk], writes=[self.ypre])
                    kb.dma("sp", self.O["s5s"], self.O["s5s"][c, :, q4 * 4:(q4 + 1) * 4, :], self.ypre, self.ypre[0:16, 0:2, :].rearrange("p a (b q) -> p (a b) q", q=128), grp=self.g_out)
        Cin = self.Cin
        for tile in range(4):
            pY = [self.pb[4], self.pb[1]][tile % 2]
            for j in range(8):
                osl = slice(j * nsb, (j + 1) * nsb)
                for i in range(j + 1):
                    kb.op("pe", lambda e, tile=tile, j=j, i=i, osl=osl, pY=pY: e.matmul(pY[:, osl], KT[:, tile, j - i, :], uT[:, tile, i:nb:8], start=(i == 0), stop=False),
                          reads=[KT, uT], writes=[pY])
                for pl in range(4):
                    pair = tile * 4 + pl
                    for c in range(2):
                        last = (pl == 3 and c == 1)
                        if pl < 3:
                            kb.op("pe", lambda e, pair=pair, pl=pl, j=j, c=c, osl=osl, pY=pY, last=last: e.matmul(
                                pY[32 * pl:32 * pl + 32, osl], CW[:, pair, j, c, :], Cin[:, c, pair, 0:nsb], start=False, stop=last),
                                reads=[CW, Cin], writes=[pY])
                        else:
                            kb.op("pe", lambda e, pair=pair, tile=tile, j=j, c=c, osl=osl, pY=pY, last=last: e.matmul(
                                pY[64:128, osl], CWz[:, tile, j, c, :], Cin[:, c, pair, 0:nsb], start=False, stop=last),
                                reads=[CWz, Cin], writes=[pY])
            kb.op("act", lambda e, tile=tile, pY=pY: e.copy(self.ypre[:, tile, 0:nb].rearrange("p (b j) -> p j b", j=8), pY[:, 0:8 * nsb].rearrange("p (j b) -> p j b", j=8)),
                  reads=[pY], writes=[self.ypre])
        kb.op("act", lambda e: e.activation(self.ypre[:, :, 0:nb], self.ypre[:, :, 0:nb], AF.Gelu), reads=[self.ypre], writes=[self.ypre])
        kb.op("dve", lambda e: e.tensor_copy(self.ygb[:, :, 0:nb], self.ypre[:, :, 0:nb]), reads=[self.ypre], writes=[self.ygb])
        if blk == 0:
            self.dbg_dump("yg0", self.ypre, self.ypre[:], [128, 4, NB])
        for oc in range(4):
            pk = self.pb[oc % 2]
            for c in range(4):
                kb.op("pe", lambda e, oc=oc, c=c, pk=pk: e.matmul(pk[:, 0:nb], self.wglub[:, c, oc * 128:(oc + 1) * 128], self.ygb[:, c, 0:nb], start=(c == 0), stop=(c == 3)),
                      reads=[self.wglub, self.ygb], writes=[pk])
            kb.op("act", lambda e, oc=oc, pk=pk: e.activation(self.sig[:, 0:nb], pk[:, 0:nb], AF.Sigmoid, bias=self.colp[:, 32 + oc:33 + oc]), reads=[pk, self.colp], writes=[self.sig])
            kb.op("dve", lambda e, oc=oc: e.tensor_tensor(self.ycat[:, oc, 0:nb], self.ypre[:, oc, 0:nb], self.sig[:, 0:nb], ALU.mult), reads=[self.ypre, self.sig], writes=[self.ycat])
        if blk == 0:
            self.dbg_dump("ycat0", self.ycat, self.ycat[:], [128, 8, NB], BF16)
        if sample:
            self.dbg_dump("ycatS", self.ycat, self.ycat[:], [128, 8, 128], BF16)

    def rwkv_block(self, blk, sample, nb):
        kb = self.kb
        psT, colp = self.psT, self.colp
        CP = lambda c0, hp: colp[:, c0 + hp:c0 + hp + 1]
        f32t = lambda n, sh=None: kb.sb(n, sh or [128, nb])
        elw, cum, E1, E2, E3 = f32t("elw"), f32t("cum"), f32t("E1"), f32t("E2"), f32t("E3")
        av, kk, sq, kp, tm = f32t("av"), f32t("kk"), f32t("sq"), f32t("kp"), f32t("tm")
        tanhw = kb.sb("tanhw", [128, nb], BF16)
        alob = kb.sb("alob", [128, nb], BF16)
        sigg = kb.sb("sigg", [128, nb], BF16)
        ARt = kb.sb("ARt", [128, 4, 2, nb], BF16)
        BKt = kb.sb("BKt", [128, 4, 2, nb], BF16)
        VbT = kb.sb("VbT", [128, 4, nb], BF16)
        bv = kb.sb("bv", [128, 4, nb])
        gT = kb.sb("gT", [128, 4, nb])
        gam = kb.sb("gam", [128, 4, 16])
        S = slice(0, nb)
        nch = nb // 128
        kb.op("act", lambda e: e.activation(tanhw[0:64, S], psT[0:64, 12, S], AF.Tanh), reads=[psT], writes=[tanhw])
        kb.op("pool", lambda e: e.tensor_copy(alob[64:128, S], psT[64:128, 12, S]), reads=[psT], writes=[alob])
        kb.op("act", lambda e: e.activation(sigg[:, S], psT[:, 13, S], AF.Sigmoid), reads=[psT], writes=[sigg])
        bones = self.C("bones")
        for hp in range(4):
            hs = slice(hp * 128, (hp + 1) * 128)
            p0, p1, p2 = self.pb[0], self.pb[1], self.pb[2]
            kb.op("pe", lambda e: e.matmul(p0[:, S], self.w2b[0:64, hs], tanhw[0:64, S], start=True, stop=True), reads=[self.w2b, tanhw], writes=[p0])
            kb.op("pe", lambda e: e.matmul(p1[:, S], self.a2b[64:128, hs], alob[64:128, S], start=True, stop=True), reads=[self.a2b, alob], writes=[p1])
            kb.op("pe", lambda e: e.matmul(p2[:, S], self.g2b[:, hs], sigg[:, S], start=True, stop=True), reads=[self.g2b, sigg], writes=[p2])
            kb.op("act", lambda e: e.activation(elw[:, S], p0[:, S], AF.Exp, bias=CP(72, hp), scale=-1.0), reads=[p0, colp], writes=[elw])
            kb.op("act", lambda e: e.activation(elw[:, S], elw[:, S], AF.Ln, bias=1.0), reads=[elw], writes=[elw])
            kb.op("act", lambda e: e.activation(elw[:, S], elw[:, S], AF.Exp, bias=-0.5, scale=-1.0), reads=[elw], writes=[elw])
            kb.op("act", lambda e: e.activation(av[:, S], p1[:, S], AF.Sigmoid, bias=CP(44, hp)), reads=[p1, colp], writes=[av])
            kb.op("act", lambda e: e.copy(gT[:, hp, S], p2[:, S]), reads=[p2], writes=[gT])
            for c in range(nch):
                cs = slice(c * 128, (c + 1) * 128)
                d0 = self.C("rst8") if sample else self.C("ones")
                kb.op("dve", lambda e, cs=cs, d0=d0: e.tensor_tensor_scan(cum[:, cs], d0, elw[:, cs], 0.0, ALU.mult, ALU.add), reads=[elw, self.consts], writes=[cum])
            kb.op("act", lambda e: e.activation(E1[:, S], cum[:, S], AF.Exp, scale=-1.0), reads=[cum], writes=[E1])
            kb.op("act", lambda e: e.activation(E2[:, S], cum[:, S], AF.Exp), reads=[cum], writes=[E2])
            kb.op("act", lambda e: e.activation(E3[:, S], elw[:, S], AF.Exp), reads=[elw], writes=[E3])
            kb.op("dve", lambda e: e.tensor_tensor(E3[:, S], E3[:, S], E1[:, S], ALU.mult), reads=[E3, E1], writes=[E3])
            if sample:
                kb.op("dve", lambda e, hp=hp: e.tensor_copy(gam[:, hp, :], E1[:, 7:nb:8]), reads=[E1], writes=[gam])
            else:
                kb.op("dve", lambda e, hp=hp: e.tensor_copy(gam[:, hp, 0:nch], E1[:, 127:nb:128]), reads=[E1], writes=[gam])
            kb.op("dve", lambda e, hp=hp: e.tensor_scalar(kk[:, S], psT[:, 4 + hp, S], CP(48, hp), None, ALU.mult), reads=[psT, colp], writes=[kk])
            kb.op("dve", lambda e: e.tensor_tensor(sq[:, S], kk[:, S], kk[:, S], ALU.mult), reads=[kk], writes=[sq])
            kb.op("pe", lambda e: e.matmul(p0[:, S], bones, sq[:, S], start=True, stop=True), reads=[self.consts, sq], writes=[p0])
            kb.op("act", lambda e: e.activation(sq[:, S], p0[:, S], AF.Sqrt), reads=[p0], writes=[sq])
            kb.op("dve", lambda e: e.tensor_scalar_max(sq[:, S], sq[:, S], 1e-12), reads=[sq], writes=[sq])
            kb.op("dve", lambda e: e.reciprocal(sq[:, S], sq[:, S]), reads=[sq], writes=[sq])
            kb.op("dve", lambda e: e.tensor_tensor(kk[:, S], kk[:, S], sq[:, S], ALU.mult), reads=[kk, sq], writes=[kk])
            kb.op("dve", lambda e, hp=hp: e.tensor_scalar(tm[:, S], av[:, S], CP(52, hp), CP(56, hp), ALU.mult, ALU.add), reads=[av, colp], writes=[tm])
            kb.op("dve", lambda e, hp=hp: e.tensor_tensor(kp[:, S], psT[:, 4 + hp, S], tm[:, S], ALU.mult), reads=[psT, tm], writes=[kp])
            kb.op("dve", lambda e, hp=hp: e.scalar_tensor_tensor(ARt[:, hp, 0, S], kk[:, S], -1.0, E3[:, S], ALU.mult, ALU.mult), reads=[kk, E3], writes=[ARt])
            kb.op("dve", lambda e, hp=hp: e.tensor_tensor(ARt[:, hp, 1, S], psT[:, hp, S], E1[:, S], ALU.mult), reads=[psT, E1], writes=[ARt])
            kb.op("dve", lambda e: e.tensor_tensor(tm[:, S], kk[:, S], av[:, S], ALU.mult), reads=[kk, av], writes=[tm])
            kb.op("dve", lambda e, hp=hp: e.tensor_tensor(BKt[:, hp, 0, S], tm[:, S], E2[:, S], ALU.mult), reads=[tm, E2], writes=[BKt])
            kb.op("dve", lambda e, hp=hp: e.tensor_tensor(BKt[:, hp, 1, S], kp[:, S], E2[:, S], ALU.mult), reads=[kp, E2], writes=[BKt])
            kb.op("pool", lambda e, hp=hp: e.tensor_copy(VbT[:, hp, S], psT[:, 8 + hp, S]), reads=[psT], writes=[VbT])
            kb.op("dve", lambda e, hp=hp: e.scalar_tensor_tensor(tm[:, S], psT[:, hp, S], CP(60, hp), kp[:, S], ALU.mult, ALU.mult), reads=[psT, colp, kp], writes=[tm])
            kb.op("pe", lambda e: e.matmul(p1[:, S], bones, tm[:, S], start=True, stop=True), reads=[self.consts, tm], writes=[p1])
            kb.op("dve", lambda e, hp=hp: e.tensor_tensor(bv[:, hp, S], p1[:, S], psT[:, 8 + hp, S], ALU.mult), reads=[p1, psT], writes=[bv])
        if blk == 0:
            self.dbg_dump("ARt0", ARt, ARt[:], [128, 4, 2, NB], BF16)
            self.dbg_dump("BKt0", BKt, BKt[:], [128, 4, 2, NB], BF16)
        if ("b0only" in self.dbg and blk > 0) or "stop_prep" in self.dbg or ("no_sample" in self.dbg and sample) or ("no_prompt" in self.dbg and not sample):
            return
        nsq = 3 if sample else 6
        mk = lambda n: self.C(("s" if sample else "") + n)
        MUS, MUI, MLS = mk("mu_s"), mk("mu_i"), mk("ml_s")
        ident = self.C("ident")
        tokm2 = [kb.sb(f"tokm{i}", [128, 4, 128], BF16) for i in range(2)]
        UD = BF16 if sample else F32
        Wt = [[kb.sb(f"W{h}{i}", [128, 128], UD) for i in range(2)] for h in range(4)]
        At = [[kb.sb(f"A{h}{i}", [128, 128], UD) for i in range(2)] for h in range(4)]
        IW = [[kb.sb(f"IW{h}{i}", [128, 128], UD) for i in range(2)] for h in range(4)]
        Xfin = [kb.sb(f"Xfin{h}", [128, 128], BF16) for h in range(4)]
        PT = [kb.sb(f"PT{h}", [128, 128], BF16) for h in range(4)]
        MT = [kb.sb(f"MT{h}", [128, 128], BF16) for h in range(4)]
        QT = [kb.sb(f"QT{h}", [128, 128], BF16) for h in range(4)]
        Xt = [[kb.sb(f"X{h}{i}", [128, 128], UD) for i in range(2)] for h in range(4)]
        RhT = kb.sb("RhT", [128, 128], BF16)
        GTt = kb.sb("GTt", [128, 64], BF16)
        ysb = kb.sb("ysb", [128, 128])
        dd = kb.sb("dd", [128, 128])
        d2 = kb.sb("d2", [128, 128])
        rs_ = kb.sb("rs_", [128, 128])
        if sample:
            Zsb = kb.sb("Zsb", [128, 4, 16, 64], BF16)
            Znh = kb.sb("Znh", [128, 16, 64])
            kb.dma("pool", Zsb, Zsb[:].rearrange("p a b v -> p (a b v)"), self.I["wkv0"], self.I["wkv0"][:])
            Bex = kb.sb("Bex", [128, 16, 64], BF16)
            Uex = kb.sb("Uex", [128, 16, 64], BF16)
            Vex = kb.sb("Vex", [128, 16, 64], BF16)
            GTs = kb.sb("GTs", [128, 16, 64], BF16)
            Hs = kb.sb("Hs", [128, 16, 64])
        pb = self.pb
        i2 = self.C("i2")
        for c in range(nch):
            cs = slice(c * 128, (c + 1) * 128)
            for q2 in range(2):
              hps = [2 * q2, 2 * q2 + 1]
              for i, hp in enumerate(hps):
                tokm = tokm2[i]
                srcs = [ARt[:, hp, 0, cs], BKt[:, hp, 0, cs], BKt[:, hp, 1, cs], VbT[:, hp, cs]]
                for ii, sap in enumerate(srcs):
                    kb.op("pe", lambda e, ii=ii, sap=sap: e.transpose(self.pt[:, ii * 128:(ii + 1) * 128], sap, self.identb[:]), reads=[ARt, BKt, VbT, self.identb], writes=[self.pt])
                kb.op("act", lambda e: e.copy(tokm[:].rearrange("p a n -> p (a n)"), self.pt[:, 0:512]), reads=[self.pt], writes=[tokm])
              for i, hp in enumerate(hps):
                tokm = tokm2[i]
                for h2 in range(2):
                    hq = 2 * i + h2
                    P_ = slice(h2 * 64, (h2 + 1) * 64)
                    W, A, Iw, X = Wt[hq], At[hq], IW[hq], Xt[hq]
                    pa, pbk = pb[hq], pb[4]
                    kb.op("pe", lambda e: e.matmul(pa[:, 0:256].rearrange("p (a n) -> p a n", a=2), BKt[P_, hp, 0, cs], ARt[P_, hp, :, cs], start=True, stop=True), reads=[BKt, ARt], writes=[pa])
                    kb.op("pe", lambda e: e.matmul(pa[:, 256:512].rearrange("p (a n) -> p a n", a=2), BKt[P_, hp, 1, cs], ARt[P_, hp, :, cs], start=True, stop=True), reads=[BKt, ARt], writes=[pa])
                    kb.op("pe", lambda e: e.matmul(pbk[:, 0:128], ARt[P_, hp, 0, cs], BKt[P_, hp, 0, cs], start=True, stop=True), reads=[BKt, ARt], writes=[pbk])
                    kb.op("dve", lambda e: e.tensor_tensor(W[0][:], pa[:, 0:128], MUS, ALU.mult), reads=[pa, self.consts], writes=[W[0]])
                    kb.op("dve", lambda e: e.tensor_tensor(PT[hq][:], pa[:, 128:256], MUI, ALU.mult), reads=[pa, self.consts], writes=[PT[hq]])
                    kb.op("dve", lambda e: e.tensor_tensor(MT[hq][:], pa[:, 256:384], MUS, ALU.mult), reads=[pa, self.consts], writes=[MT[hq]])
                    kb.op("dve", lambda e: e.tensor_tensor(QT[hq][:], pa[:, 384:512], MUI, ALU.mult), reads=[pa, self.consts], writes=[QT[hq]])
                    kb.op("dve", lambda e: e.tensor_tensor(A[0][:], pbk[:, 0:128], MLS, ALU.mult), reads=[pbk, self.consts], writes=[A[0]])
                    kb.op("pool", lambda e: e.tensor_tensor(Iw[0][:], W[0][:], ident, ALU.add), reads=[W[0], self.consts], writes=[Iw[0]])
                    kb.op("pe", lambda e: e.matmul(pbk[:, 128:192], MT[hq][:], tokm[:, 3, h2 * 64:(h2 + 1) * 64], start=True, stop=True), reads=[MT[hq], tokm], writes=[pbk])
                    kb.op("pool", lambda e: e.tensor_copy(X[0][:, 0:64], tokm[:, 0, h2 * 64:(h2 + 1) * 64]), reads=[tokm], writes=[X[0]])
                    kb.op("act", lambda e: e.copy(X[0][:, 64:128], pbk[:, 128:192]), reads=[pbk], writes=[X[0]])
              for j in range(nsq + 1):
                a, b = j % 2, (j + 1) % 2
                for hq in range(4):
                    W, A, Iw, X = Wt[hq], At[hq], IW[hq], Xt[hq]
                    pk = pb[hq]
                    kb.op("pe", lambda e: e.matmul(pk[:, 256:384], Iw[a][:], X[a][:], start=True, stop=True), reads=[Iw[a], X[a]], writes=[pk])
                    if j < nsq:
                        kb.op("pe", lambda e: e.matmul(pk[:, 0:128], A[a][:], W[a][:], start=True, stop=True), reads=[A[a], W[a]], writes=[pk])
                    if j < nsq - 1:
                        kb.op("pe", lambda e: e.matmul(pk[:, 128:256], W[a][:], A[a][:], start=True, stop=True), reads=[A[a], W[a]], writes=[pk])
                    dst = Xfin[hq] if j == nsq else X[b]
                    if hq % 2 == 0:
                        kb.op("dve", lambda e: e.tensor_copy(dst[:], pk[:, 256:384]), reads=[pk], writes=[dst])
                    else:
                        kb.op("act", lambda e: e.copy(dst[:], pk[:, 256:384]), reads=[pk], writes=[dst])
                    if j < nsq:
                        kb.op("dve", lambda e: e.tensor_tensor(Iw[b][:], pk[:, 0:128], ident, ALU.add), reads=[pk, self.consts], writes=[Iw[b]])
                    if j < nsq - 1:
                        kb.op("act", lambda e: e.copy(W[b][:], pk[:, 0:128]), reads=[pk], writes=[W[b]])
                        kb.op("act", lambda e: e.copy(A[b][:], pk[:, 128:256]), reads=[pk], writes=[A[b]])
              for i, hp in enumerate(hps):
                tokm = tokm2[i]
                XF = [Xfin[2 * i], Xfin[2 * i + 1]]
                PTl = [PT[2 * i], PT[2 * i + 1]]
                QTl = [QT[2 * i], QT[2 * i + 1]]
                pR, pG, pH, pYT = pb[2], pb[3], pb[4], pb[2]
                for h2 in range(2):
                    P_ = slice(h2 * 64, (h2 + 1) * 64)
                    X = XF[h2]
                    kb.op("pe", lambda e, P_=P_, X=X, h2=h2: e.matmul(pR[P_, 0:128], X[:, 0:64], PTl[h2][:], start=True, stop=True), reads=[X, PTl[h2]], writes=[pR])
                    kb.op("pe", lambda e, P_=P_, X=X, h2=h2: e.matmul(pG[P_, 0:64], X[:, 0:64], tokm[:, 1, h2 * 64:(h2 + 1) * 64], start=True, stop=True), reads=[X, tokm], writes=[pG])
                kb.op("dve", lambda e: e.tensor_tensor(RhT[:], pR[:, 0:128], ARt[:, hp, 1, cs], ALU.add), reads=[pR, ARt], writes=[RhT])
                if sample:
                    kb.op("dve", lambda e: e.tensor_tensor(GTt[:], pG[:, 0:64], i2, ALU.add), reads=[pG, self.consts], writes=[GTt])
                else:
                    kb.op("dve", lambda e: e.tensor_copy(GTt[:], pG[:, 0:64]), reads=[pG], writes=[GTt])
                if not sample:
                    for h2 in range(2):
                        P_ = slice(h2 * 64, (h2 + 1) * 64)
                        X = XF[h2]
                        vt = tokm[:, 3, h2 * 64:(h2 + 1) * 64]
                        kb.op("pe", lambda e, P_=P_, X=X, h2=h2: e.matmul(pYT[P_, 128:256], X[:, 64:128], PTl[h2][:], start=True, stop=False), reads=[X, PTl[h2]], writes=[pYT])
                        kb.op("pe", lambda e, P_=P_, vt=vt, h2=h2: e.matmul(pYT[P_, 128:256], vt, QTl[h2][:], start=False, stop=False), reads=[tokm, QTl[h2]], writes=[pYT])
                        kb.op("pe", lambda e, P_=P_: e.matmul(pYT[P_, 128:256], self.Zb[P_, hp, :], RhT[P_, :], start=False, stop=True), reads=[self.Zb, RhT], writes=[pYT])
                        kb.op("pe", lambda e, P_=P_, X=X, h2=h2: e.matmul(pH[P_, 0:64], tokm[:, 1, h2 * 64:(h2 + 1) * 64], X[:, 64:128], start=True, stop=False), reads=[tokm, X], writes=[pH])
                        kb.op("pe", lambda e, P_=P_, vt=vt, h2=h2: e.matmul(pH[P_, 0:64], tokm[:, 2, h2 * 64:(h2 + 1) * 64], vt, start=False, stop=False), reads=[tokm], writes=[pH])
                        kb.op("pe", lambda e, P_=P_: e.matmul(pH[P_, 0:64], GTt[P_, :], self.Zb[P_, hp, :], start=False, stop=True), reads=[GTt, self.Zb], writes=[pH])
                    kb.op("dve", lambda e: e.tensor_tensor(self.Zf[:, hp, :], pH[:, 0:64], self.Zf[:, hp, :], ALU.add), reads=[pH, self.Zf], writes=[self.Zf])
                    kb.op("dve", lambda e, c=c: e.tensor_scalar(self.Zf[:, hp, :], self.Zf[:, hp, :], gam[:, hp, c:c + 1], None, ALU.mult), reads=[self.Zf, gam], writes=[self.Zf])
                    kb.op("pool", lambda e: e.tensor_copy(self.Zb[:, hp, :], self.Zf[:, hp, :]), reads=[self.Zf], writes=[self.Zb])
                else:
                    sel = self.C("sel16")
                    for h2 in range(2):
                        P_ = slice(h2 * 64, (h2 + 1) * 64)
                        X = XF[h2]
                        vt = tokm[:, 3, h2 * 64:(h2 + 1) * 64]
                        kb.op("pe", lambda e, P_=P_, X=X, h2=h2: e.matmul(pYT[P_, 128:256], X[:, 64:128], PTl[h2][:], start=True, stop=False), reads=[X, PTl[h2]], writes=[pYT])
                        kb.op("pe", lambda e, P_=P_, vt=vt, h2=h2: e.matmul(pYT[P_, 128:256], vt, QTl[h2][:], start=False, stop=True), reads=[tokm, QTl[h2]], writes=[pYT])
                        for b in range(16):
                            kb.op("pe", lambda e, P_=P_, b=b: e.matmul(pb[0][P_, b * 8:(b + 1) * 8], Zsb[P_, hp, b, :], RhT[P_, b * 8:(b + 1) * 8], start=True, stop=True), reads=[Zsb, RhT], writes=[pb[0]])
                        bx = lambda ap: ap.unsqueeze(1).to_broadcast([128, 16, 64])
                        sx = sel.unsqueeze(2).to_broadcast([128, 16, 64])
                        kb.op("dve", lambda e, h2=h2: e.tensor_tensor(Bex[:], bx(tokm[:, 1, h2 * 64:(h2 + 1) * 64]), sx, ALU.mult), reads=[tokm, self.consts], writes=[Bex])
                        kb.op("dve", lambda e, X=X: e.tensor_tensor(Uex[:], bx(X[:, 64:128]), sx, ALU.mult), reads=[X, self.consts], writes=[Uex])
                        kb.op("pool", lambda e, vt=vt: e.tensor_tensor(Vex[:], bx(vt), sx, ALU.mult), reads=[tokm, self.consts], writes=[Vex])
                        for hf in range(2):
                            bsl = slice(hf * 8, (hf + 1) * 8)
                            pg, ph = pb[3], pb[4]
                            kb.op("pe", lambda e, P_=P_, X=X, bsl=bsl, pg=pg: e.matmul(pg[P_, :], X[:, 0:64], Bex[:, bsl, :], start=True, stop=True), reads=[X, Bex], writes=[pg])
                            kb.op("pe", lambda e, P_=P_, bsl=bsl, ph=ph, h2=h2: e.matmul(ph[P_, :], tokm[:, 1, h2 * 64:(h2 + 1) * 64], Uex[:, bsl, :], start=True, stop=False), reads=[tokm, Uex], writes=[ph])
                            kb.op("pe", lambda e, P_=P_, bsl=bsl, ph=ph, h2=h2: e.matmul(ph[P_, :], tokm[:, 2, h2 * 64:(h2 + 1) * 64], Vex[:, bsl, :], start=False, stop=True), reads=[tokm, Vex], writes=[ph])
                            kb.op("dve", lambda e, P_=P_, bsl=bsl, pg=pg: e.tensor_tensor(GTs[P_, bsl, :], pg[P_, :].rearrange("p (b k) -> p b k", b=8), i2[P_, :].unsqueeze(1).to_broadcast([64, 8, 64]), ALU.add), reads=[pg, self.consts], writes=[GTs])
                            kb.op("act", lambda e, P_=P_, bsl=bsl, ph=ph: e.copy(Hs[P_, bsl, :], ph[P_, :].rearrange("p (b k) -> p b k", b=8)), reads=[ph], writes=[Hs])
                        for b in range(16):
                            kb.op("pe", lambda e, P_=P_, b=b: e.matmul(pb[1][P_, (b % 8) * 64:(b % 8 + 1) * 64], GTs[P_, b, :], Zsb[P_, hp, b, :], start=True, stop=True), reads=[GTs, Zsb], writes=[pb[1]])
                            if b % 8 == 7:
                                bsl = slice(b - 7, b + 1)
                                kb.op("dve", lambda e, P_=P_, bsl=bsl: e.tensor_tensor(Hs[P_, bsl, :], Hs[P_, bsl, :], pb[1][P_, :].rearrange("p (b k) -> p b k", b=8), ALU.add), reads=[Hs, pb[1]], writes=[Hs])
                    kb.op("dve", lambda e: e.tensor_tensor(Znh[:], Hs[:], gam[:, hp, :].unsqueeze(2).to_broadcast([128, 16, 64]), ALU.mult), reads=[Hs, gam], writes=[Znh])
                    kb.dma("sp", self.O["wkvs"], self.O["wkvs"][:, hp * 1024:(hp + 1) * 1024], Znh, Znh[:].rearrange("p b v -> p (b v)"), grp=self.g_out)
                    kb.op("dve", lambda e: e.tensor_copy(ysb[:], pb[0][:, 0:128]), reads=[pb[0]], writes=[ysb])
                if sample:
                    kb.op("dve", lambda e: e.tensor_tensor(ysb[:], ysb[:], pYT[:, 128:256], ALU.add), reads=[ysb, pYT], writes=[ysb])
                else:
                    kb.op("act", lambda e: e.copy(ysb[:], pYT[:, 128:256]), reads=[pYT], writes=[ysb])
                pm = pb[3]
                kb.op("pe", lambda e: e.matmul(pm[:, 0:128], bones, ysb[:], start=True, stop=True), reads=[self.consts, ysb], writes=[pm])
                kb.op("dve", lambda e: e.scalar_tensor_tensor(dd[:], pm[:, 0:128], -1.0 / 64, ysb[:], ALU.mult, ALU.add), reads=[pm, ysb], writes=[dd])
                kb.op("dve", lambda e: e.tensor_tensor(d2[:], dd[:], dd[:], ALU.mult), reads=[dd], writes=[d2])
                kb.op("pe", lambda e: e.matmul(pm[:, 128:256], bones, d2[:], start=True, stop=True), reads=[self.consts, d2], writes=[pm])
                kb.op("act", lambda e: e.activation(rs_[:], pm[:, 128:256], AF.Sqrt, bias=self.gneps[:], scale=1.0 / 64), reads=[pm, self.gneps], writes=[rs_])
                kb.op("dve", lambda e: e.reciprocal(rs_[:], rs_[:]), reads=[rs_], writes=[rs_])
                kb.op("dve", lambda e: e.tensor_tensor(dd[:], dd[:], rs_[:], ALU.mult), reads=[dd, rs_], writes=[dd])
                kb.op("dve", lambda e: e.tensor_scalar(dd[:], dd[:], CP(64, hp), CP(68, hp), ALU.mult, ALU.add), reads=[dd, colp], writes=[dd])
                kb.op("dve", lambda e: e.tensor_tensor(dd[:], dd[:], bv[:, hp, cs], ALU.add), reads=[dd, bv], writes=[dd])
                kb.op("dve", lambda e: e.tensor_tensor(self.ycat[:, 4 + hp, cs], dd[:], gT[:, hp, cs], ALU.mult), reads=[dd, gT], writes=[self.ycat])
        if blk == 0:
            self.dbg_dump("yrw0", self.ycat, self.ycat[:], [128, 8, NB], BF16)
        if sample:
            self.dbg_dump("yrwS", self.ycat, self.ycat[:], [128, 8, 128], BF16)
            shso = Hs
            shsov = Hs[0:16, 0:8, :].rearrange("p a b -> p (a b)")
            for r in range(14):
                kb.op("pe", lambda e, r=r: e.transpose(pb[4][0:16, (r % 4) * 128:(r % 4 + 1) * 128], self.shsT[:, r, :], ident), reads=[self.shsT, self.consts], writes=[pb[4]])
                if r % 4 == 3 or r == 13:
                    r0 = r - (r % 4)
                    n = r - r0 + 1
                    kb.op("dve", lambda e, n=n: e.tensor_copy(shsov[:, 0:n * 128], pb[4][0:16, 0:n * 128]), reads=[pb[4]], writes=[shso])
                    kb.dma("sp", self.O["shs"], self.O["shs"][:, r0 * 128:(r + 1) * 128], shso, shsov[:, 0:n * 128], grp=self.g_out)
        if blk == NPB - 1 and "no_tail" not in self.dbg:
            wv = self.O["wkvp"][:].rearrange("(a h2) k v -> h2 k a v", h2=2)
            for h2 in range(2):
                kb.dma("sp", self.O["wkvp"], wv[h2], self.Zf, self.Zf[h2 * 64:(h2 + 1) * 64, :, :], grp=self.g_out)
            kb.op("pe", lambda e: e.transpose(pb[4][0:14, 0:128], self.carry[:], ident), reads=[self.carry, self.consts], writes=[pb[4]])
            kb.op("dve", lambda e: e.tensor_copy(ysb[0:14, :], pb[4][0:14, 0:128]), reads=[pb[4]], writes=[ysb])
            kb.dma("sp", self.O["shp"], self.O["shp"][:], ysb, ysb[0:14, :], grp=self.g_out)

    def outproj(self, blk, sample, nb, ntile, row0, GA1):
        kb = self.kb
        self.x1t = kb.sb("x1t", [128, D])
        xt = kb.sb("xt", [128, D])
        for j in range(ntile):
            kb.dma("sp", xt, xt[:], self.I["xall"], self.I["xall"][row0 + j * 128:row0 + (j + 1) * 128, :])
            for hf in range(2):
                pk = self.pb[hf]
                for c in range(8):
                    kb.op("pe", lambda e, c=c, hf=hf, pk=pk: e.matmul(pk[:], self.ycat[:, c, j * 128:(j + 1) * 128], self.woutb[:, c, hf * 512:(hf + 1) * 512], start=(c == 0), stop=(c == 7)),
                          reads=[self.ycat, self.woutb], writes=[pk])
                sl = slice(hf * 512, (hf + 1) * 512)
                kb.op("dve", lambda e, pk=pk, sl=sl: e.tensor_tensor(self.x1t[:, sl], pk[:], GA1[:, sl], ALU.mult), reads=[pk, GA1], writes=[self.x1t])
                kb.op("dve", lambda e, sl=sl: e.tensor_tensor(self.x1t[:, sl], self.x1t[:, sl], xt[:, sl], ALU.add), reads=[self.x1t, xt], writes=[self.x1t])
            kb.dma("sp", self.x1s, self.x1s[row0 + j * 128:row0 + (j + 1) * 128, :], self.x1t, self.x1t[:], grp=self.g_x1)

    def tab_alloc(self):
        kb = self.kb
        self.UTs = kb.dram("UTs", [128, 128, 8, 128], BF16)
        self.Vs = kb.dram("Vs", [128, 128, D], BF16)
        self.g_tab = [[kb.group(f"tabu{i}"), kb.group(f"tabv{i}")] for i in range(2)]
        self.tb_uf = [kb.sb(f"uf{i}", [128, D]) for i in range(2)]
        self.tb_vf = [kb.sb(f"vf{i}", [128, D]) for i in range(2)]
        self.tb_utb = [kb.sb(f"utb{i}", [128, 8, 128], BF16) for i in range(2)]
        self.tb_vbb = [kb.sb(f"vbb{i}", [128, D], BF16) for i in range(2)]

    def tab_prep(self):
        kb = self.kb
        I = self.I
        pb = self.pb
        ident = self.C("ident")
        uf, vf, utb, vbb = self.tb_uf, self.tb_vf, self.tb_utb, self.tb_vbb
        pu = [pb[5], pb[6]]

        def tload(k):
            kb.dma("sp", uf[k % 2], uf[k % 2][:], I["peer_u"], I["peer_u"][k * 128:(k + 1) * 128, :])
            kb.dma("sp", vf[k % 2], vf[k % 2][:], I["peer_v"], I["peer_v"][k * 128:(k + 1) * 128, :])
        tload(0)
        for k in range(128):
            u, v, ub, vb = uf[k % 2], vf[k % 2], utb[k % 2], vbb[k % 2]
            if k + 1 < 128:
                tload(k + 1)
            for c in range(8):
                kb.op("pe", lambda e: e.transpose(pu[c // 4][:, (c % 4) * 128:(c % 4 + 1) * 128], u[:, c * 128:(c + 1) * 128], ident), reads=[u, self.consts], writes=[pu[c // 4]])
            kb.op("act", lambda e: e.copy(ub[:, 0:4, :].rearrange("p a n -> p (a n)"), pu[0][:]), reads=[pu[0]], writes=[ub])
            kb.op("dve", lambda e: e.tensor_copy(ub[:, 4:8, :].rearrange("p a n -> p (a n)"), pu[1][:]), reads=[pu[1]], writes=[ub])
            self.cast("pool", vb, vb[:, 0:512], v, v[:, 0:512])
            self.cast("act" if k % 2 else "dve", vb, vb[:, 512:1024], v, v[:, 512:1024])
            kb.dma("pool", self.UTs, self.UTs[k], ub, ub[:], grp=self.g_tab[k % 2][0])
            kb.dma("pool", self.Vs, self.Vs[k], vb, vb[:], grp=self.g_tab[k % 2][1])
        for gg in self.g_tab:
            for g_ in gg:
                g_.close()

    def phase2(self):
        kb = self.kb
        I = self.I
        pb = self.pb
        ident = self.C("ident")
        NEG = -1.0e30
        g = kb.group("p2par")
        fng = kb.sb("fng", [128, D])
        kb.dma("sp", fng, fng[:], I["fng"], I["fng"][:].partition_broadcast(128), grp=g)
        g.close()
        keysT = kb.sb("keysT", [128, 8, 128], BF16)
        wqb = kb.sb("wqb", [128, 8, D], BF16)
        mod2 = [kb.sb(f"m2{n}", [128, D]) for n in range(3)]
        kb.scope_enter()
        g = kb.group("p2k")
        K12 = kb.sb("K12", [128, 8, 128])
        kb.dma("sp", K12, K12[:, :, 0:64], I["keys1"], I["keys1"][:].rearrange("h n d -> n h d"), grp=g)
        kb.dma("sp", K12, K12[:, :, 64:128], I["keys2"], I["keys2"][:].rearrange("h n d -> n h d"), grp=g)
        g.close()
        for h in range(8):
            kb.op("pe", lambda e, h=h: e.transpose(pb[h // 4][:, (h % 4) * 128:(h % 4 + 1) * 128], K12[:, h, :], ident), reads=[K12, self.consts], writes=[pb[h // 4]])
        for q in range(2):
            kb.op("dve", lambda e, q=q: e.tensor_copy(keysT[:, q * 4:(q + 1) * 4, :].rearrange("p a n -> p (a n)"), pb[q][:]), reads=[pb[q]], writes=[keysT])
        if "p2a" in self.dbg:
            kb.scope_exit()
            return
        wst = [kb.sb(f"wq{i}", [128, D]) for i in range(2)]
        for kc in range(8):
            s_ = kc % 2
            kb.dma("sp", wst[s_], wst[s_][:], I["w_q"], I["w_q"][kc * 128:(kc + 1) * 128, :])
            self.cast("act" if kc % 2 else "dve", wqb, wqb[:, kc, :], wst[s_], wst[s_][:])
        UTs, Vs = self.UTs, self.Vs
        kb.scope_exit()
        if "p2b" in self.dbg:
            return
        x1g = [kb.sb(f"x1g{i}", [128, D]) for i in range(2)]
        x1r = kb.sb("x1r", [128, D])
        h2Ts = [kb.sb(f"h2T{i}", [128, 8, NB], BF16) for i in range(2)]
        qT = kb.sb("qT", [128, 8, 128], BF16)
        sqj = kb.sb("sqj2", [128, D], BF16)
        t1k = kb.sb("t1k2", [128, D])
        hb = kb.sb("hb2", [128, D], BF16)
        ss = kb.sb("ss2", [128, 4])
        rstd = kb.sb("rstd2", [128, 4])
        sc = kb.sb("sc", [128, 8, 2, 128])
        v16 = kb.sb("v16", [128, 8, 2, 16])
        v16b = kb.sb("v16b", [128, 8, 2, 8])
        s16b = kb.sb("s16b", [128, 8, 8])
        i16 = kb.sb("i16", [128, 8, 2, 16], U32)
        i16f = kb.sb("i16f", [128, 8, 2, 16])
        cand = kb.sb("cand", [128, 8, 16, 16])
        s16 = kb.sb("s16", [128, 8, 16])
        p16 = kb.sb("p16", [128, 8, 16], U32)
        pa_i = kb.sb("pa_i", [128, 8, 16], U32)
        paf = kb.sb("paf", [128, 2, 8, 16])
        oh = kb.sb("oh", [128, 8, 16, 16])
        sm = kb.sb("sm", [128, 8])
        jt = kb.sb("jt", [128, 3, 128])
        jTs = [[kb.sb(f"jT{p_}{j_}", [128, 3, 128], BF16) for j_ in range(2)] for p_ in range(2)]
        iotab = kb.sb("iotab", [128, 128], BF16)
        kb.op("dve", lambda e: e.tensor_copy(iotab[:], self.C("iota")), reads=[self.consts], writes=[iotab])
        OH1gs = [kb.sb(f"OH1g{i}", [128, 16, 128], BF16) for i in range(2)]
        OH2s = [kb.sb(f"OH2{i}", [128, 16, 128], BF16) for i in range(2)]
        Wall = kb.sb("Wall", [128, NB, 128], BF16)
        ubs = [kb.sb(f"ubs{i}", [128, 8, 128], BF16) for i in range(4)]
        vbs = [kb.sb(f"vbs{i}", [128, D], BF16) for i in range(4)]
        gsb = kb.sb("gsb", [128, NB], BF16)
        WA = [kb.sb(f"WA{i}", [128, NB], BF16) for i in range(2)]
        x2 = kb.sb("x2", [128, D])
        iota = self.C("iota")
        SH2, A2, GA2 = mod2
        pq = pb[6]

        def ginfo(gi):
            sample = gi == NPB
            nt = TS if sample else NB
            return sample, nt, nt // 128, (TP if sample else gi * NB), (1 if sample else 0)

        def front(gi):
            sample, nt, ntile, row0, typ = ginfo(gi)
            h2T = h2Ts[gi % 2]
            if gi == 0 or sample:
                for n in range(2):
                    kb.dma("sp", mod2[n], mod2[n][:], self.mod2s, self.mod2s[typ, 3 + n])
            for j in range(ntile):
                ts_ = slice(j * 128, (j + 1) * 128)
                kb.dma("sp", x1g[j], x1g[j][:], self.x1s, self.x1s[row0 + j * 128:row0 + (j + 1) * 128, :])
                kb.op("act", lambda e: e.activation(sqj[:], x1g[j][:], AF.Square, accum_out=ss[:, j:j + 1]), reads=[x1g[j]], writes=[sqj, ss])
                kb.op("act", lambda e: e.activation(rstd[:, j:j + 1], ss[:, j:j + 1], AF.Sqrt, bias=self.eps6[:], scale=1.0 / D), reads=[ss, self.eps6], writes=[rstd])
                kb.op("dve", lambda e: e.reciprocal(rstd[:, j:j + 1], rstd[:, j:j + 1]), reads=[rstd], writes=[rstd])
                kb.op("dve", lambda e: e.scalar_tensor_tensor(t1k[:], x1g[j][:], rstd[:, j:j + 1], A2[:], ALU.mult, ALU.mult), reads=[x1g[j], rstd, A2], writes=[t1k])
                kb.op("pool", lambda e: e.tensor_tensor(hb[:], t1k[:], SH2[:], ALU.add), reads=[t1k, SH2], writes=[hb])
                for c in range(8):
                    kb.op("pe", lambda e: e.transpose(self.pt[:, c * 128:(c + 1) * 128], hb[:, c * 128:(c + 1) * 128], self.identb[:]), reads=[hb, self.identb], writes=[self.pt])
                kb.op("act", lambda e: e.copy(h2T[:, :, ts_], self.pt[:].rearrange("p (c n) -> p c n", c=8)), reads=[self.pt], writes=[h2T])
                for hg in range(2):
                    for hh in range(4):
                        h = hg * 4 + hh
                        for k in range(8):
                            kb.op("pe", lambda e: e.matmul(pq[:, hh * 128:(hh + 1) * 128], wqb[:, k, h * 128:(h + 1) * 128], h2T[:, k, ts_], start=(k == 0), stop=(k == 7)), reads=[wqb, h2T], writes=[pq])
                    kb.op("act", lambda e: e.copy(qT[:, hg * 4:(hg + 1) * 4, :], pq[:].rearrange("p (a n) -> p a n", a=4)), reads=[pq], writes=[qT])
                for f in range(2):
                    P_ = slice(f * 64, (f + 1) * 64)
                    for hg in range(2):
                        for hh in range(4):
                            h = hg * 4 + hh
                            kb.op("pe", lambda e: e.matmul(pq[:, hh * 128:(hh + 1) * 128], qT[P_, h, :], keysT[P_, h, :], start=True, stop=True), reads=[qT, keysT], writes=[pq])
                        kb.op("act", lambda e: e.copy(sc[:, hg * 4:(hg + 1) * 4, f, :], pq[:].rearrange("p (a n) -> p a n", a=4)), reads=[pq], writes=[sc])
                HF = [(h, f) for h in range(8) for f in range(2)]
                sctb = oh[:].rearrange("p a b c -> p (a b c)").rearrange("p (q n) -> p q n", q=16)
                for h, f in HF:
                    kb.op("dve", lambda e, h=h, f=f: e.max(v16[:, h, f, 0:8], sc[:, h, f, :]), reads=[sc], writes=[v16])
                for h, f in HF:
                    kb.op("dve", lambda e, h=h, f=f: e.max_index(i16[:, h, f, 0:8], v16[:, h, f, 0:8], sc[:, h, f, :]), reads=[sc, v16], writes=[i16])
                for q, (h, f) in enumerate(HF):
                    kb.op("dve", lambda e, h=h, f=f, q=q: e.match_replace(sctb[:, q, :], v16[:, h, f, 0:8], sc[:, h, f, :], NEG), reads=[sc, v16], writes=[oh])
                for q, (h, f) in enumerate(HF):
                    kb.op("dve", lambda e, h=h, f=f, q=q: e.max(v16b[:, h, f, :], sctb[:, q, :]), reads=[oh], writes=[v16b])
                for q, (h, f) in enumerate(HF):
                    kb.op("dve", lambda e, h=h, f=f, q=q: e.max_index(i16[:, h, f, 8:16], v16b[:, h, f, :], sctb[:, q, :]), reads=[oh, v16b], writes=[i16])
                kb.op("dve", lambda e: e.tensor_copy(v16[:, :, :, 8:16], v16b[:]), reads=[v16b], writes=[v16])
                kb.op("dve", lambda e: e.tensor_copy(i16f[:], i16[:]), reads=[i16], writes=[i16f])
                kb.op("dve", lambda e: e.tensor_tensor(cand[:], v16[:, :, 0, :].unsqueeze(3).to_broadcast([128, 8, 16, 16]), v16[:, :, 1, :].unsqueeze(2).to_broadcast([128, 8, 16, 16]), ALU.add), reads=[v16], writes=[cand])
                candf = lambda h: cand[:, h, :, :].rearrange("p a b -> p (a b)")
                sc2 = sc[:].rearrange("p h f n -> p h (f n)")
                for h in range(8):
                    kb.op("dve", lambda e, h=h: e.max(s16[:, h, 0:8], candf(h)), reads=[cand], writes=[s16])
                for h in range(8):
                    kb.op("dve", lambda e, h=h: e.max_index(p16[:, h, 0:8], s16[:, h, 0:8], candf(h)), reads=[cand, s16], writes=[p16])
                for h in range(8):
                    kb.op("dve", lambda e, h=h: e.match_replace(sc2[:, h, :], s16[:, h, 0:8], candf(h), NEG), reads=[cand, s16], writes=[sc])
                for h in range(8):
                    kb.op("dve", lambda e, h=h: e.max(s16b[:, h, :], sc2[:, h, :]), reads=[sc], writes=[s16b])
                for h in range(8):
                    kb.op("dve", lambda e, h=h: e.max_index(p16[:, h, 8:16], s16b[:, h, :], sc2[:, h, :]), reads=[sc, s16b], writes=[p16])
                kb.op("dve", lambda e: e.tensor_copy(s16[:, :, 8:16], s16b[:]), reads=[s16b], writes=[s16])
                gate = jt[:, 2, :].rearrange("p (h k) -> p h k", h=8)
                kb.op("dve", lambda e: e.tensor_tensor(gate, s16[:], s16[:, :, 0:1].to_broadcast([128, 8, 16]), ALU.subtract), reads=[s16], writes=[jt])
                kb.op("act", lambda e: e.activation(gate, gate, AF.Exp), reads=[jt], writes=[jt])
                kb.op("dve", lambda e: e.tensor_reduce(sm[:], gate, AX.X, ALU.add), reads=[jt], writes=[sm])
                kb.op("dve", lambda e: e.reciprocal(sm[:], sm[:]), reads=[sm], writes=[sm])
                kb.op("dve", lambda e: e.tensor_tensor(gate, gate, sm[:].unsqueeze(2).to_broadcast([128, 8, 16]), ALU.mult), reads=[jt, sm], writes=[jt])
                kb.op("dve", lambda e: e.tensor_single_scalar(pa_i[:], p16[:], 4, ALU.logical_shift_right), reads=[p16], writes=[pa_i])
                kb.op("dve", lambda e: e.tensor_copy(paf[:, 0], pa_i[:]), reads=[pa_i], writes=[paf])
                kb.op("dve", lambda e: e.tensor_single_scalar(pa_i[:], p16[:], 15, ALU.bitwise_and), reads=[p16], writes=[pa_i])
                kb.op("dve", lambda e: e.tensor_copy(paf[:, 1], pa_i[:]), reads=[pa_i], writes=[paf])
                io16 = iota[:, 0:16].unsqueeze(1).unsqueeze(1).to_broadcast([128, 8, 16, 16])
                for f in range(2):
                    kb.op("dve", lambda e, f=f: e.tensor_tensor(oh[:], io16, paf[:, f].unsqueeze(3).to_broadcast([128, 8, 16, 16]), ALU.is_equal), reads=[paf, self.consts], writes=[oh])
                    kb.op("dve", lambda e, f=f: e.tensor_tensor(oh[:], oh[:], i16f[:, :, f, :].unsqueeze(2).to_broadcast([128, 8, 16, 16]), ALU.mult), reads=[oh, i16f], writes=[oh])
                    kb.op("dve", lambda e, f=f: e.tensor_reduce(jt[:, f, :].rearrange("p (h k) -> p h k", h=8), oh[:], AX.X, ALU.add), reads=[oh], writes=[jt])
                jTc = jTs[gi % 2][j]
                for q in range(3):
                    kb.op("pe", lambda e: e.transpose(pq[:, q * 128:(q + 1) * 128], jt[:, q, :], ident), reads=[jt, self.consts], writes=[pq])
                kb.op("act", lambda e: e.copy(jTc[:].rearrange("p a n -> p (a n)"), pq[:, 0:384]), reads=[pq], writes=[jTc])

        def mainA(gi):
            sample, nt, ntile, row0, typ = ginfo(gi)
            if gi == 0 or sample:
                kb.dma("sp", mod2[2], mod2[2][:], self.mod2s, self.mod2s[typ, 5])
            for j in range(ntile):
                jTc = jTs[gi % 2][j]
                for qq in range(8):
                    t0 = qq * 16
                    OH1g, OH2 = OH1gs[qq % 2], OH2s[qq % 2]
                    io3 = iotab[:].unsqueeze(1).to_broadcast([128, 16, 128])
                    bc = lambda q, t0=t0: jTc[:, q, t0:t0 + 16].unsqueeze(2).to_broadcast([128, 16, 128])
                    kb.op("dve", lambda e: e.tensor_tensor(OH2[:], io3, bc(1), ALU.is_equal), reads=[jTc, iotab], writes=[OH2])
                    kb.op("dve", lambda e: e.tensor_tensor(OH1g[:], io3, bc(0), ALU.is_equal), reads=[jTc, iotab], writes=[OH1g])
                    kb.op("pool" if qq % 2 else "dve", lambda e: e.tensor_tensor(OH1g[:], OH1g[:], bc(2), ALU.mult), reads=[OH1g, jTc], writes=[OH1g])
                    for t4 in range(4):
                        pk = pb[4 + t4 % 2]
                        for u in range(4):
                            t = t4 * 4 + u
                            kb.op("pe", lambda e, t=t, u=u, pk=pk: e.matmul(pk[:, u * 128:(u + 1) * 128], OH2[:, t, :], OH1g[:, t, :], start=True, stop=True), reads=[OH2, OH1g], writes=[pk])
                        tb = j * 128 + t0 + t4 * 4
                        kb.op("act", lambda e, tb=tb, pk=pk: e.copy(Wall[:, tb:tb + 4, :].rearrange("p a n -> p (a n)"), pk[:]), reads=[pk], writes=[Wall])

        def mainB(gi):
            sample, nt, ntile, row0, typ = ginfo(gi)
            h2T = h2Ts[gi % 2]

            def sload(k):
                kb.dma("sp", ubs[k % 4], ubs[k % 4][:], UTs, UTs[k])
                kb.dma("sp", vbs[k % 4], vbs[k % 4][:], Vs, Vs[k])
            sload(0)
            sload(1)
            sload(2)

            def Dmm(k):
                ub, pD = ubs[k % 4], pb[4 + k % 2]
                for c in range(8):
                    kb.op("pe", lambda e: e.matmul(pD[:, 0:nt], ub[:, c, :], h2T[:, c, 0:nt], start=(c == 0), stop=(c == 7)), reads=[ub, h2T], writes=[pD])
            Dmm(0)
            for k in range(128):
                vb = vbs[k % 4]
                pD = pb[4 + k % 2]
                wa = WA[k % 2]
                if k + 1 < 128:
                    Dmm(k + 1)
                kb.op("act", lambda e: e.activation(gsb[:, 0:nt], pD[:, 0:nt], AF.Gelu), reads=[pD], writes=[gsb])
                kb.op("dve", lambda e: e.tensor_tensor(wa[:, 0:nt], gsb[:, 0:nt], Wall[:, 0:nt, k], ALU.mult), reads=[gsb, Wall], writes=[wa])
                for a in range(ntile):
                    for hf in range(2):
                        po = pb[a * 2 + hf]
                        kb.op("pe", lambda e: e.matmul(po[:], wa[:, a * 128:(a + 1) * 128], vb[:, hf * 512:(hf + 1) * 512], start=(k == 0), stop=(k == 127)), reads=[wa, vb], writes=[po])
                if k + 3 < 128:
                    sload(k + 3)
            for a in range(ntile):
                kb.dma("sp", x1r, x1r[:], self.x1s, self.x1s[row0 + a * 128:row0 + (a + 1) * 128, :])
                for hf in range(2):
                    sl = slice(hf * 512, (hf + 1) * 512)
                    po = pb[a * 2 + hf]
                    kb.op("dve", lambda e: e.tensor_tensor(x2[:, sl], po[:], GA2[:, sl], ALU.mult), reads=[po, GA2], writes=[x2])
                    kb.op("dve", lambda e: e.tensor_tensor(x2[:, sl], x2[:, sl], x1r[:, sl], ALU.add), reads=[x2, x1r], writes=[x2])
                kb.op("act", lambda e: e.activation(sqj[:], x2[:], AF.Square, accum_out=ss[:, 2 + a:3 + a]), reads=[x2], writes=[sqj, ss])
                kb.op("act", lambda e: e.activation(rstd[:, 2 + a:3 + a], ss[:, 2 + a:3 + a], AF.Sqrt, bias=self.eps6[:], scale=1.0 / D), reads=[ss, self.eps6], writes=[rstd])
                kb.op("dve", lambda e: e.reciprocal(rstd[:, 2 + a:3 + a], rstd[:, 2 + a:3 + a]), reads=[rstd], writes=[rstd])
                kb.op("dve", lambda e: e.scalar_tensor_tensor(x2[:], x2[:], rstd[:, 2 + a:3 + a], fng[:], ALU.mult, ALU.mult), reads=[x2, rstd, fng], writes=[x2])
                kb.dma("sp", self.Oy, self.Oy[row0 + a * 128:row0 + (a + 1) * 128, :], x2, x2[:], grp=self.g_out)

        ngr = NPB + 1
        for f_ in self.dbg:
            if f_.startswith("ngr"):
                ngr = int(f_[3:])
        wts = [9, 2]
        for f_ in self.dbg:
            if f_.startswith("wt"):
                wts = [int(x) for x in f_[2:].split("_")]
        front(0)
        def main(gi):
            mainA(gi)
            mainB(gi)
        for gi in range(ngr):
            if gi + 1 < ngr and "p2serial" not in self.dbg:
                kb.interleave([lambda gi=gi: main(gi), lambda gi=gi: front(gi + 1)], weights=wts)
            else:
                main(gi)
                if gi + 1 < ngr:
                    front(gi + 1)


def host_inputs(inp, core):
    f = lambda a: np.ascontiguousarray(a, dtype=np.float32)
    m = {}
    xs = inp["x_sample"][16 * core:16 * core + 16].reshape(128, D)
    m["xall"] = f(np.concatenate([inp["x_prompt"][core], xs], 0))
    cp = np.broadcast_to(inp["c_prompt"][core][None, :], (128, D))
    cs = np.repeat(inp["c_sample"][16 * core:16 * core + 16], 8, axis=0)
    m["crep"] = f(np.stack([cp, cs], 0))
    m["consts"] = CONSTS
    for k in ("w_ada", "b_ada", "norm1_g", "norm2_g", "w_in", "w_out", "w_glu"):
        m[k] = f(inp[k][0])
    are, aim, ldt = inp["s5_a_re"][0], inp["s5_a_im"][0], inp["s5_log_dt"][0]
    bre, bim, cre, cim = inp["s5_b_re"][0], inp["s5_b_im"][0], inp["s5_c_re"][0], inp["s5_c_im"][0]
    R = np.zeros((4, 128, 5, 64), np.float32)
    gidx = (np.arange(512) // 16).reshape(4, 128)
    hidx = (np.arange(512) % 16).reshape(4, 128)
    R[:, :, 0, :] = are[gidx]
    R[:, :, 1, :] = aim[gidx]
    R[:, :, 2, :] = ldt[gidx][..., None]
    R[:, :, 3, :] = bre[gidx, :, hidx]
    R[:, :, 4, :] = bim[gidx, :, hidx]
    m["s5R"] = f(R.transpose(1, 0, 2, 3))
    Pm = np.stack([are.T, aim.T, np.broadcast_to(ldt[None, :], (64, 32))], 1)
    m["s5P"] = f(Pm)
    PB = np.stack([bre.transpose(1, 0, 2).reshape(64, 512), bim.transpose(1, 0, 2).reshape(64, 512),
                   cre.transpose(2, 0, 1).reshape(64, 512), cim.transpose(2, 0, 1).reshape(64, 512)], 1)
    m["s5PB"] = f(PB)
    def qlay(a_gp):
        return a_gp.reshape(16, 2, 64).transpose(1, 2, 0).reshape(128, 16)
    m["s5Q"] = f(np.stack([qlay(are), qlay(aim), qlay(np.broadcast_to(ldt[:, None], (32, 64)))], 1))
    def qlay3(c_ghp):
        return c_ghp.reshape(16, 2, 16, 64).transpose(1, 3, 0, 2).reshape(128, 16, 16)
    m["s5QC"] = f(np.stack([qlay3(cre), qlay3(cim)], 1))
    sr = inp["state_s5_re"][0, 16 * core:16 * core + 16]
    si = inp["state_s5_im"][0, 16 * core:16 * core + 16]
    def qst(s):
        return s.reshape(16, 16, 2, 64).transpose(2, 3, 1, 0).reshape(128, 16, 16)
    m["s5st"] = f(np.stack([qst(sr), qst(si)], 1))
    colp = np.zeros((128, 80), np.float32)
    mu = inp["rwkv_mu"][0]
    colp[:, 0:14] = mu.reshape(14, 128).T
    colp[:, 14:28] = 0.0
    colp[:, 36:40] = inp["s5_d"][0].reshape(4, 128).T
    colp[:, 32:36] = inp["b_glu"][0].reshape(4, 128).T
    c4 = lambda a: np.asarray(a).reshape(4, 128).T
    colp[:, 40:44] = c4(inp["rwkv_w0"][0])
    colp[:, 44:48] = c4(inp["rwkv_a0"][0])
    colp[:, 48:52] = c4(inp["rwkv_k_k"][0])
    colp[:, 52:56] = c4(inp["rwkv_k_a"][0])
    colp[:, 60:64] = c4(inp["rwkv_r_k"][0].reshape(512))
    colp[:, 64:68] = c4(inp["rwkv_gn_w"][0])
    colp[:, 68:72] = c4(inp["rwkv_gn_b"][0])
    m["colp"] = colp
    wk = inp["state_wkv"][0, 16 * core:16 * core + 16]
    m["wkv0"] = f(wk.reshape(16, 4, 2, 64, 64).transpose(2, 4, 1, 0, 3).reshape(128, 4096))
    m["rowp"] = f(np.stack([inp["rwkv_gn_w"][0], inp["rwkv_gn_b"][0], inp["rwkv_gn_b"][0]], 0))
    m["w2"] = f(inp["rwkv_w2"][0])
    m["a2"] = f(inp["rwkv_a2"][0])
    m["g2"] = f(inp["rwkv_g2"][0])
    m["shift0"] = f(inp["state_shift"][0, 16 * core:16 * core + 16])
    m["w_q"] = f(inp["peer_w_q"][0])
    m["keys1"] = f(inp["peer_keys1"][0])
    m["keys2"] = f(inp["peer_keys2"][0])
    m["peer_u"] = f(inp["peer_u"][0])
    m["peer_v"] = f(inp["peer_v"][0])
    m["fng"] = f(inp["final_norm_g"])
    return m


_CACHE = {}


def kernel(**inputs):
    inp = {k: np.asarray(v) for k, v in inputs.items()}
    if "nc" not in _CACHE:
        _CACHE["nc"] = Builder().build()
    nc = _CACHE["nc"]
    in_maps = [host_inputs(inp, c) for c in range(NCORES)]
    res = run_bass_kernel_spmd(nc, in_maps, core_ids=list(range(NCORES)))
    R = res.results
    nc_ = NCORES
    y_p = np.stack([R[c]["y"][:TP] for c in range(nc_)], 0)
    y_s = np.concatenate([R[c]["y"][TP:].reshape(16, 8, D) for c in range(nc_)], 0)
    s5p = np.stack([R[c]["s5p"].reshape(2, 32, 64) for c in range(nc_)], 1)
    s5s = np.concatenate([R[c]["s5s"].reshape(2, 16, 32, 64) for c in range(nc_)], 1)
    wkvp = np.stack([R[c]["wkvp"].transpose(0, 2, 1) for c in range(nc_)], 0)
    shp = np.stack([R[c]["shp"].reshape(1792) for c in range(nc_)], 0)
    wkvs = np.concatenate([R[c]["wkvs"].reshape(2, 64, 4, 16, 64).transpose(3, 2, 0, 4, 1).reshape(16, 8, 64, 64) for c in range(nc_)], 0)
    shs = np.concatenate([R[c]["shs"] for c in range(nc_)], 0)
    f = lambda a: np.ascontiguousarray(a, dtype=np.float32)
    return (f(y_p), f(y_s), f(s5p[0][None]), f(s5p[1][None]), f(wkvp[None]), f(shp[None]),
            f(s5s[0][None]), f(s5s[1][None]), f(wkvs[None]), f(shs[None]))
```

```python
from contextlib import ExitStack
import numpy as np
import concourse.bass as bass
import concourse.mybir as mybir
from concourse.bass_utils import run_bass_kernel_spmd

F32 = mybir.dt.float32
BF16 = mybir.dt.bfloat16
I32 = mybir.dt.int32
U32 = mybir.dt.uint32
ALU = mybir.AluOpType
AF = mybir.ActivationFunctionType
AX = mybir.AxisListType

NCORES = 8
D = 1024
NB = 256
NPB = 8
TP = 2048
TS = 128
TT_ = TP + TS
INC = 2304


class TT:
    __slots__ = ("h", "w", "r", "dsem", "dcnt", "name", "psum")

    def __init__(self, h, name):
        self.h = h
        self.psum = False
        self.w = {}
        self.r = {}
        self.dsem = None
        self.dcnt = 0
        self.name = name

    def __getitem__(self, k):
        return self.h[k]


class DmaGroup:
    def __init__(self, kb, name):
        self.sem = kb.newsem("g_" + name)
        self.cnt = 0
        self.outs = []
        self.ins = []

    def close(self):
        for t in self.outs:
            t.w[self.sem] = self.cnt
        for t in self.ins:
            t.r[self.sem] = self.cnt
        self.outs = []
        self.ins = []


class KB:
    def __init__(self):
        self.nc = bass.Bass("TRN2", target_bir_lowering=False)
        nc = self.nc
        self.es = ExitStack()
        self.eng = {"pe": nc.tensor, "act": nc.scalar, "dve": nc.vector, "pool": nc.gpsimd, "sp": nc.sync}
        self.sem = {e: self.es.enter_context(nc.semaphore("s_" + e)) for e in self.eng}
        self.cnt = {e: 0 for e in self.eng}
        self.known = {e: {} for e in self.eng}
        self.nsem = len(self.eng)
        self.n_ins = 0
        self.n_wait = 0
        self.uid = 0
        self.root_es = self.es
        self.scopes = []
        self.free_ev = {}

    def scope_enter(self):
        self.scopes.append((self.es, []))
        self.es = ExitStack()

    def scope_exit(self):
        outer, tts = self.scopes.pop()
        for t in tts:
            self._merge(self.free_ev, t.w)
            self._merge(self.free_ev, t.r)
        self.es.close()
        self.es = outer

    def sb(self, name, shape, dtype=F32):
        self.uid += 1
        t = TT(self.es.enter_context(self.nc.sbuf_tensor(f"{name}_{self.uid}", list(shape), dtype)), name)
        t.w = dict(self.free_ev)
        t.r = dict(self.free_ev)
        if self.scopes:
            self.scopes[-1][1].append(t)
        return t

    def ps(self, name, shape, dtype=F32):
        self.uid += 1
        t = TT(self.root_es.enter_context(self.nc.psum_tensor(f"{name}_{self.uid}", list(shape), dtype)), name)
        t.psum = True
        return t

    def dram(self, name, shape, dtype=F32, kind="Internal"):
        return TT(self.nc.dram_tensor(name, list(shape), dtype, kind=kind).ap(), name)

    def newsem(self, name):
        self.nsem += 1
        return self.root_es.enter_context(self.nc.semaphore(f"{name}_{self.nsem}"))

    def _wait(self, e, waits):
        eng = self.eng[e]
        own = self.sem[e]
        kn = self.known[e]
        for s, v in waits.items():
            if e == "pe" and s is own:
                continue
            if kn.get(s, 0) >= v:
                continue
            eng.wait_ge(s, v)
            kn[s] = v
            self.n_wait += 1

    @staticmethod
    def _merge(d, src):
        for s, v in src.items():
            if d.get(s, 0) < v:
                d[s] = v

    def op(self, e, fn, reads=(), writes=()):
        waits = {}
        own = self.sem[e]
        for t in reads:
            self._merge(waits, t.w)
            if t.psum:
                for s_, v_ in t.r.items():
                    if s_ is not own and waits.get(s_, 0) < v_:
                        waits[s_] = v_
        for t in writes:
            for s_, v_ in t.w.items():
                if s_ is not own and waits.get(s_, 0) < v_:
                    waits[s_] = v_
            for s_, v_ in t.r.items():
                if s_ is not own and waits.get(s_, 0) < v_:
                    waits[s_] = v_
        self._wait(e, waits)
        ins = fn(self.eng[e])
        self.cnt[e] += 1
        c = self.cnt[e]
        ins.then_inc(own, 1)
        self.n_ins += 1
        for t in writes:
            t.w[own] = c
        for t in reads:
            t.r[own] = c
        self._co_yield()
        return ins

    def group(self, name):
        return DmaGroup(self, name)

    def interleave(self, fns, weights=None):
        import threading
        n = len(fns)
        weights = weights or [1] * n
        st = {"turn": 0, "left": weights[0], "alive": [True] * n, "cv": threading.Condition(), "ids": {}, "w": weights, "err": None}
        self._co = st

        def advance():
            k = st["turn"]
            for _ in range(n):
                k = (k + 1) % n
                if st["alive"][k]:
                    break
            st["turn"] = k
            st["left"] = st["w"][k]

        st["advance"] = advance

        def runner(i):
            st["ids"][threading.get_ident()] = i
            with st["cv"]:
                while st["turn"] != i:
                    st["cv"].wait()
            try:
                fns[i]()
            except BaseException as ex:
                st["err"] = ex
            finally:
                with st["cv"]:
                    st["alive"][i] = False
                    if any(st["alive"]):
                        advance()
                    st["cv"].notify_all()

        ths = [threading.Thread(target=runner, args=(i,)) for i in range(n)]
        for t in ths:
            t.start()
        for t in ths:
            t.join()
        self._co = None
        if st["err"] is not None:
            raise st["err"]

    def _co_yield(self):
        st = getattr(self, "_co", None)
        if st is None:
            return
        import threading
        i = st["ids"].get(threading.get_ident())
        if i is None:
            return
        with st["cv"]:
            st["left"] -= 1
            if st["left"] > 0:
                return
            st["advance"]()
            st["cv"].notify_all()
            while st["turn"] != i:
                st["cv"].wait()

    def dma(self, q, out_t, out_ap, in_t, in_ap, grp=None, **kw):
        if grp is not None:
            waits = {}
            self._merge(waits, in_t.w)
            self._merge(waits, out_t.w)
            self._merge(waits, out_t.r)
            self._wait(q, waits)
            ins = self.eng[q].dma_start(out=out_ap, in_=in_ap, **kw)
            ins.then_inc(grp.sem, 16)
            grp.cnt += 16
            grp.outs.append(out_t)
            in_t.r[grp.sem] = grp.cnt
            out_t.w[grp.sem] = grp.cnt
            self.n_ins += 1
            return ins
        if out_t.dsem is None:
            out_t.dsem = self.newsem("d_" + out_t.name)
        waits = {}
        self._merge(waits, in_t.w)
        for s, v in out_t.w.items():
            if s is out_t.dsem:
                continue
            if waits.get(s, 0) < v:
                waits[s] = v
        self._merge(waits, out_t.r)
        self._wait(q, waits)
        ins = self.eng[q].dma_start(out=out_ap, in_=in_ap, **kw)
        ins.then_inc(out_t.dsem, 16)
        out_t.dcnt += 16
        out_t.w[out_t.dsem] = out_t.dcnt
        in_t.r[out_t.dsem] = out_t.dcnt
        self.n_ins += 1
        return ins

    def finish(self, outs, e="sp"):
        waits = {}
        for t in outs:
            self._merge(waits, t.w)
        self._wait(e, waits)


CONST_LAYOUT = {}


def _make_consts():
    parts = []
    off = 0

    def add(name, arr):
        nonlocal off
        arr = np.asarray(arr, np.float32)
        assert arr.shape[0] == 128
        CONST_LAYOUT[name] = (off, arr.shape[1])
        parts.append(arr)
        off += arr.shape[1]

    q = np.arange(128)
    add("ident", np.eye(128))
    add("mu_s", (q[:, None] < q[None, :]))
    add("mu_i", (q[:, None] <= q[None, :]))
    add("ml_s", (q[None, :] < q[:, None]))
    add("bones", (q[:, None] // 64 == q[None, :] // 64))
    add("hsel", (q[:, None] // 64 == np.arange(2)[None, :]))
    add("bd16", (q[:, None] // 16 == q[None, :] // 16))
    add("eo", np.stack([(q // 16) % 2 == 0, (q // 16) % 2 == 1], 1))
    add("i2", np.concatenate([np.eye(64), np.eye(64)], 0))
    add("iota", np.broadcast_to(np.arange(128)[None, :], (128, 128)))
    add("ones", np.ones((128, 128)))
    add("m96", (q[:, None] >= 96))
    add("rst8", np.broadcast_to((q % 8 != 0)[None, :], (128, 128)))
    add("sel16", (q[:, None] // 8 == np.arange(16)[None, :]))
    same = (q[:, None] // 8 == q[None, :] // 8)
    add("smu_s", same & (q[:, None] < q[None, :]))
    add("smu_i", same & (q[:, None] <= q[None, :]))
    add("sml_s", same & (q[None, :] < q[:, None]))
    return np.concatenate(parts, 1)


CONSTS = _make_consts()
NCONST = CONSTS.shape[1]


class Builder:
    def __init__(self, dbg=()):
        self.kb = KB()
        self.dbg = set(dbg)
        self.dbg_out = {}

    def C(self, name, rows=slice(0, 128)):
        o, n = CONST_LAYOUT[name]
        return self.consts[rows, o:o + n]

    def cast(self, eng, out_t, out_ap, in_t, in_ap):
        if eng == "act":
            return self.kb.op("act", lambda e: e.copy(out_ap, in_ap), reads=[in_t], writes=[out_t])
        return self.kb.op(eng, lambda e: e.tensor_copy(out_ap, in_ap), reads=[in_t], writes=[out_t])

    def dbg_dump(self, name, t, ap, shape, dtype=F32):
        if name not in self.dbg:
            return
        kb = self.kb
        o = kb.dram("dbg_" + name, shape, dtype, "ExternalOutput")
        kb.dma("sp", o, o[:], t, ap, grp=self.g_out)
        self.dbg_out[name] = o

    def build(self):
        kb = self.kb
        X = lambda n, s, dt=F32: kb.dram(n, s, dt, "ExternalInput")
        O = lambda n, s, dt=F32: kb.dram(n, s, dt, "ExternalOutput")
        self.I = I = {}
        I["xall"] = X("xall", [TT_, D])
        I["crep"] = X("crep", [2, 128, D])
        I["consts"] = X("consts", [128, NCONST])
        I["w_ada"] = X("w_ada", [D, 6 * D])
        I["b_ada"] = X("b_ada", [6 * D])
        I["norm1_g"] = X("norm1_g", [D])
        I["norm2_g"] = X("norm2_g", [D])
        I["w_in"] = X("w_in", [D, INC])
        I["w_out"] = X("w_out", [D, D])
        I["w_glu"] = X("w_glu", [512, 512])
        I["s5R"] = X("s5R", [128, 4, 5, 64])
        I["s5P"] = X("s5P", [64, 3, 32])
        I["s5PB"] = X("s5PB", [64, 4, 512])
        I["s5Q"] = X("s5Q", [128, 3, 16])
        I["s5QC"] = X("s5QC", [128, 2, 16, 16])
        I["s5st"] = X("s5st", [128, 2, 16, 16])
        I["colp"] = X("colp", [128, 80])
        I["rowp"] = X("rowp", [3, 512])
        I["w2"] = X("w2", [64, 512])
        I["a2"] = X("a2", [64, 512])
        I["g2"] = X("g2", [128, 512])
        I["shift0"] = X("shift0", [16, 1792])
        I["wkv0"] = X("wkv0", [128, 4096])
        I["w_q"] = X("w_q", [D, D])
        I["keys1"] = X("keys1", [8, 128, 64])
        I["keys2"] = X("keys2", [8, 128, 64])
        I["peer_u"] = X("peer_u", [16384, D])
        I["peer_v"] = X("peer_v", [16384, D])
        I["fng"] = X("fng", [D])
        self.Oy = O("y", [TT_, D])
        self.O = {}
        self.O["s5p"] = O("s5p", [2, 16, 128])
        self.O["s5s"] = O("s5s", [2, 16, 16, 128])
        self.O["wkvp"] = O("wkvp", [8, 64, 64])
        self.O["wkvs"] = O("wkvs", [128, 4096])
        self.O["shp"] = O("shp", [14, 128])
        self.O["shs"] = O("shs", [16, 1792])
        self.x1s = kb.dram("x1s", [TT_, D], F32)
        self.g_out = kb.group("out")
        self.g_x1 = kb.group("x1")

        self.setup_common()
        kb.scope_enter()
        self.tab_alloc()
        kb.scope_enter()
        self.setup()
        nblk = NPB + 1
        for f_ in self.dbg:
            if f_.startswith("nblk"):
                nblk = int(f_[4:])

        def blocks():
            for blk in range(nblk):
                self.block(blk)
        tw = [12, 1]
        for f_ in self.dbg:
            if f_.startswith("tw"):
                tw = [int(x) for x in f_[2:].split("_")]
        if "no_p2" in self.dbg:
            blocks()
        elif "tabserial" in self.dbg or nblk == 0:
            blocks()
            self.tab_prep()
        else:
            kb.interleave([blocks, self.tab_prep], weights=tw)
        kb.scope_exit()
        kb.scope_exit()
        self.g_x1.close()
        self.dbg_dump("x1", self.x1s, self.x1s[:], [TT_, D])
        kb.scope_enter()
        if "no_p2" not in self.dbg:
            self.phase2()
        kb.scope_exit()
        self.g_out.close()
        kb.finish([self.Oy] + list(self.O.values()) + list(self.dbg_out.values()))
        return kb.nc

    def setup_common(self):
        kb = self.kb
        I = self.I
        g = kb.group("par")
        self.consts = kb.sb("consts", [128, NCONST])
        kb.dma("sp", self.consts, self.consts[:], I["consts"], I["consts"][:], grp=g)
        self.colp = kb.sb("colp", [128, 80])
        kb.dma("sp", self.colp, self.colp[:], I["colp"], I["colp"][:], grp=g)
        g.close()
        kb.op("dve", lambda e: e.tensor_scalar(self.colp[:, 14:28], self.colp[:, 0:14], -1.0, 1.0, ALU.mult, ALU.add), reads=[self.colp], writes=[self.colp])
        kb.op("dve", lambda e: e.tensor_scalar(self.colp[:, 56:60], self.colp[:, 52:56], -1.0, 1.0, ALU.mult, ALU.add), reads=[self.colp], writes=[self.colp])
        kb.op("dve", lambda e: e.tensor_scalar_mul(self.colp[:, 72:76], self.colp[:, 40:44], -1.0), reads=[self.colp], writes=[self.colp])
        self.gneps = kb.sb("gneps", [128, 1])
        kb.op("dve", lambda e: e.memset(self.gneps[:], 64e-5), writes=[self.gneps])
        self.identb = kb.sb("identb", [128, 128], BF16)
        kb.op("dve", lambda e: e.tensor_copy(self.identb[:], self.C("ident")), reads=[self.consts], writes=[self.identb])
        self.eps6 = kb.sb("eps6", [128, 1])
        kb.op("dve", lambda e: e.memset(self.eps6[:], 1e-6), writes=[self.eps6])
        self.pt = kb.ps("pt", [128, 1024], BF16)
        self.pb = [kb.ps(f"pb{i}", [128, 512], F32) for i in range(7)]


    def setup(self):
        kb = self.kb
        I = self.I
        self.wins = kb.dram("wins", [18, 128, 8, 128], BF16)
        self.g_win = kb.group("win")
        self.wcs = [kb.sb(f"wcs{i}", [128, 8, 128], BF16) for i in range(4)]
        self.woutb = kb.sb("woutb", [128, 8, D], BF16)
        self.wglub = kb.sb("wglub", [128, 4, 512], BF16)
        self.EW = kb.sb("EW", [128, 4, 2, 8, 128], BF16)
        self.KT = kb.sb("KT", [128, 4, 8, 128], BF16)
        self.pwr = kb.sb("pwr", [128, 13, 16])
        self.pwi = kb.sb("pwi", [128, 13, 16])
        self.CW = kb.sb("CW", [128, 16, 8, 2, 32], BF16)
        self.CWz = kb.sb("CWz", [128, 4, 8, 2, 64], BF16)
        self.s5c = kb.sb("s5c", [128, 2, 16])
        self.s5st = kb.sb("s5st", [128, 2, 16, 16])
        self.w2b = kb.sb("w2b", [128, 512], BF16)
        self.a2b = kb.sb("a2b", [128, 512], BF16)
        self.g2b = kb.sb("g2b", [128, 512], BF16)
        self.Zf = kb.sb("Zf", [128, 4, 64])
        self.Zb = kb.sb("Zb", [128, 4, 64], BF16)
        self.carry = kb.sb("carry", [128, 14])
        self.shiftT = kb.sb("shiftT", [128, 14, 16])
        self.shsT = kb.sb("shsT", [128, 14, 16])
        self.mod = [kb.sb(f"mod{n}", [128, D]) for n in range(3)]
        self.mod2s = kb.dram("mod2s", [2, 6, 128, D], F32)
        self.g_mod2 = [kb.group("mod2a"), kb.group("mod2b")]
        kb.scope_enter()
        g = kb.group("par2")
        g1 = kb.sb("g1", [128, D])
        kb.dma("sp", g1, g1[:], I["norm1_g"], I["norm1_g"][:].partition_broadcast(128), grp=g)
        g2n = kb.sb("g2n", [128, D])
        kb.dma("sp", g2n, g2n[:], I["norm2_g"], I["norm2_g"][:].partition_broadcast(128), grp=g)
        crep = kb.sb("crep", [128, 2, D])
        kb.dma("sp", crep, crep[:], I["crep"], I["crep"][:].rearrange("t p d -> p t d"), grp=g)
        g.close()
        bada = kb.sb("bada", [128, D])
        mtmp = [kb.sb(f"mtmp{t}", [128, D]) for t in range(2)]
        csil = kb.sb("csil", [128, 2, D], BF16)
        kb.op("act", lambda e: e.activation(csil[:], crep[:], AF.Silu), reads=[crep], writes=[csil])
        cT = kb.sb("cT", [128, 2, 8, 128], BF16)
        for typ in range(2):
            for c in range(8):
                kb.op("pe", lambda e, c=c, typ=typ: e.transpose(self.pt[:, c * 128:(c + 1) * 128], csil[:, typ, c * 128:(c + 1) * 128], self.identb[:]),
                      reads=[csil, self.identb], writes=[self.pt])
            kb.op("dve", lambda e, typ=typ: e.tensor_copy(cT[:, typ, :, :].rearrange("p c n -> p (c n)"), self.pt[:]), reads=[self.pt], writes=[cT])
        wst = [kb.sb(f"wst{i}", [128, 1024]) for i in range(3)]
        wsb = [kb.sb(f"wsb{i}", [128, 1024], BF16) for i in range(3)]
        it = 0
        for n in range(6):
            kb.dma("sp", bada, bada[:], I["b_ada"], I["b_ada"][n * 1024:(n + 1) * 1024].partition_broadcast(128))
            for kc in range(8):
                s = it % 3
                it += 1
                kb.dma("sp", wst[s], wst[s][:], I["w_ada"], I["w_ada"][kc * 128:(kc + 1) * 128, n * 1024:(n + 1) * 1024])
                self.cast("act" if it % 2 else "dve", wsb[s], wsb[s][:], wst[s], wst[s][:])
                for typ in range(2):
                    for hf in range(2):
                        kb.op("pe", lambda e, s=s, typ=typ, hf=hf, kc=kc: e.matmul(self.pb[typ * 2 + hf][:], cT[:, typ, kc, :], wsb[s][:, hf * 512:(hf + 1) * 512], start=(kc == 0), stop=(kc == 7)),
                              reads=[cT, wsb[s]], writes=[self.pb[typ * 2 + hf]])
            for typ in range(2):
                res = (typ == 0 and n < 3)
                m = self.mod[n] if res else mtmp[typ]
                for hf in range(2):
                    sl = slice(hf * 512, (hf + 1) * 512)
                    kb.op("dve", lambda e, m=m, typ=typ, hf=hf, sl=sl, n=n: e.tensor_tensor(m[:, sl], self.pb[typ * 2 + hf][:], bada[:, sl], ALU.add),
                          reads=[self.pb[typ * 2 + hf], bada], writes=[m])
                if n in (1, 4):
                    gg = g1 if n == 1 else g2n
                    kb.op("pool", lambda e, m=m: e.tensor_scalar_add(m[:], m[:], 1.0), reads=[m], writes=[m])
                    kb.op("pool", lambda e, m=m, gg=gg: e.tensor_mul(m[:], m[:], gg[:]), reads=[m, gg], writes=[m])
                if not res:
                    kb.dma("sp", self.mod2s, self.mod2s[typ, n], m, m[:], grp=self.g_mod2[typ])
        for g_ in self.g_mod2:
            g_.close()
        kb.scope_exit()
        kb.scope_enter()
        if "su1" in self.dbg:
            kb.scope_exit()
            return
        wst2 = [kb.sb(f"wst2{i}", [128, INC]) for i in range(2)]
        wsb2 = [kb.sb(f"wsb2{i}", [128, INC], BF16) for i in range(2)]
        for kc in range(8):
            s = kc % 2
            kb.dma("sp", wst2[s], wst2[s][:], I["w_in"], I["w_in"][kc * 128:(kc + 1) * 128, :])
            self.cast("act" if kc % 2 else "dve", wsb2[s], wsb2[s][:], wst2[s], wst2[s][:])
            kb.dma("sp", self.wins, self.wins[:, :, kc, :].rearrange("c p n -> p c n"), wsb2[s], wsb2[s][:].rearrange("p (c n) -> p c n", c=18), grp=self.g_win)
        for kc in range(8):
            s = kc % 2
            kb.dma("sp", wst2[s], wst2[s][:, 0:D], I["w_out"], I["w_out"][kc * 128:(kc + 1) * 128, :])
            self.cast("act" if kc % 2 else "dve", self.woutb, self.woutb[:, kc, :], wst2[s], wst2[s][:, 0:D])
        for kc in range(4):
            s = kc % 2
            kb.dma("sp", wst2[s], wst2[s][:, 0:512], I["w_glu"], I["w_glu"][kc * 128:(kc + 1) * 128, :])
            self.cast("act" if kc % 2 else "dve", self.wglub, self.wglub[:, kc, :], wst2[s], wst2[s][:, 0:512])
        kb.scope_exit()
        self.g_win.close()
        if "su2" in self.dbg:
            return
        kb.scope_enter()
        self.setup_s5()
        kb.scope_exit()
        if "su3" in self.dbg:
            return
        self.setup_rwkv()
        kb.op("dve", lambda e: e.memset(self.carry[:], 0.0), writes=[self.carry])

    def s5_lambda(self, name, are, aim, ldt, shape, np_, srcs):
        kb = self.kb
        F = shape[1]
        mk = lambda n: kb.sb(f"{name}_{n}", [128, F])
        dt, mag, ang, t1, t2, fi = mk("dt"), mk("mag"), mk("ang"), mk("t1"), mk("t2"), kb.sb(f"{name}_fi", [128, F], I32)
        lbr, lbi, cfr, cfi = mk("lbr"), mk("lbi"), mk("cfr"), mk("cfi")
        P = slice(0, np_)
        kb.op("act", lambda e: e.activation(dt[P], ldt, AF.Exp), reads=srcs, writes=[dt])
        kb.op("dve", lambda e: e.tensor_tensor(mag[P], are, dt[P], ALU.mult), reads=srcs + [dt], writes=[mag])
        kb.op("act", lambda e: e.activation(mag[P], mag[P], AF.Exp), reads=[mag], writes=[mag])
        kb.op("dve", lambda e: e.tensor_tensor(ang[P], aim, dt[P], ALU.mult), reads=srcs + [dt], writes=[ang])

        def sin_of(dst, shift):
            kb.op("dve", lambda e: e.tensor_scalar(t1[P], ang[P], 1.0 / (2 * np.pi), 0.5 + shift / (2 * np.pi), ALU.mult, ALU.add), reads=[ang], writes=[t1])
            kb.op("dve", lambda e: e.tensor_copy(fi[P], t1[P]), reads=[t1], writes=[fi])
            kb.op("dve", lambda e: e.tensor_copy(t2[P], fi[P]), reads=[fi], writes=[t2])
            kb.op("dve", lambda e: e.tensor_tensor(t1[P], t1[P], t2[P], ALU.subtract), reads=[t1, t2], writes=[t1])
            kb.op("dve", lambda e: e.tensor_single_scalar(t2[P], t1[P], 0.0, ALU.is_lt), reads=[t1], writes=[t2])
            kb.op("dve", lambda e: e.tensor_tensor(t1[P], t1[P], t2[P], ALU.add), reads=[t1, t2], writes=[t1])
            kb.op("dve", lambda e: e.tensor_scalar(t1[P], t1[P], 2 * np.pi, -np.pi, ALU.mult, ALU.add), reads=[t1], writes=[t1])
            kb.op("dve", lambda e: e.tensor_scalar(t1[P], t1[P], -np.pi, np.pi, ALU.max, ALU.min), reads=[t1], writes=[t1])
            kb.op("act", lambda e: e.activation(dst[P], t1[P], AF.Sin), reads=[t1], writes=[dst])

        sin_of(lbi, 0.0)
        sin_of(lbr, np.pi / 2)
        kb.op("dve", lambda e: e.tensor_tensor(lbr[P], lbr[P], mag[P], ALU.mult), reads=[lbr, mag], writes=[lbr])
        kb.op("dve", lambda e: e.tensor_tensor(lbi[P], lbi[P], mag[P], ALU.mult), reads=[lbi, mag], writes=[lbi])
        den, nre = mk("den"), mk("nre")
        kb.op("dve", lambda e: e.tensor_tensor(den[P], are, are, ALU.mult), reads=srcs, writes=[den])
        kb.op("dve", lambda e: e.tensor_tensor(t1[P], aim, aim, ALU.mult), reads=srcs, writes=[t1])
        kb.op("dve", lambda e: e.tensor_tensor(den[P], den[P], t1[P], ALU.add), reads=[den, t1], writes=[den])
        kb.op("dve", lambda e: e.reciprocal(den[P], den[P]), reads=[den], writes=[den])
        kb.op("dve", lambda e: e.tensor_scalar_add(nre[P], lbr[P], -1.0), reads=[lbr], writes=[nre])
        kb.op("dve", lambda e: e.tensor_tensor(t1[P], nre[P], are, ALU.mult), reads=srcs + [nre], writes=[t1])
        kb.op("dve", lambda e: e.tensor_tensor(t2[P], lbi[P], aim, ALU.mult), reads=srcs + [lbi], writes=[t2])
        kb.op("dve", lambda e: e.tensor_tensor(t1[P], t1[P], t2[P], ALU.add), reads=[t1, t2], writes=[t1])
        kb.op("dve", lambda e: e.tensor_tensor(cfr[P], t1[P], den[P], ALU.mult), reads=[t1, den], writes=[cfr])
        kb.op("dve", lambda e: e.tensor_tensor(t1[P], lbi[P], are, ALU.mult), reads=srcs + [lbi], writes=[t1])
        kb.op("dve", lambda e: e.tensor_tensor(t2[P], nre[P], aim, ALU.mult), reads=srcs + [nre], writes=[t2])
        kb.op("dve", lambda e: e.tensor_tensor(t1[P], t1[P], t2[P], ALU.subtract), reads=[t1, t2], writes=[t1])
        kb.op("dve", lambda e: e.tensor_tensor(cfi[P], t1[P], den[P], ALU.mult), reads=[t1, den], writes=[cfi])
        return lbr, lbi, cfr, cfi

    def cmul(self, P, outr, outi, ar, ai, br, bi, tmp, rd, wr, eng="dve"):
        kb = self.kb
        t1, t2 = tmp
        kb.op(eng, lambda e: e.tensor_tensor(t1, ar, br, ALU.mult), reads=rd, writes=wr)
        kb.op(eng, lambda e: e.tensor_tensor(t2, ai, bi, ALU.mult), reads=rd, writes=wr)
        kb.op(eng, lambda e: e.tensor_tensor(t1, t1, t2, ALU.subtract), reads=rd, writes=wr)
        kb.op(eng, lambda e: e.tensor_tensor(t2, ar, bi, ALU.mult), reads=rd, writes=wr)
        kb.op(eng, lambda e: e.tensor_tensor(outi, ai, br, ALU.mult), reads=rd, writes=wr)
        kb.op(eng, lambda e: e.tensor_tensor(outi, outi, t2, ALU.add), reads=rd, writes=wr)
        kb.op(eng, lambda e: e.tensor_copy(outr, t1), reads=rd, writes=wr)

    def setup_s5(self):
        kb = self.kb
        I = self.I
        g = kb.group("s5par")
        sR = kb.sb("sR", [128, 4, 5, 64])
        kb.dma("sp", sR, sR[:], I["s5R"], I["s5R"][:], grp=g)
        sP = kb.sb("sP", [128, 3, 32])
        kb.dma("sp", sP, sP[0:64], I["s5P"], I["s5P"][:], grp=g)
        sPB = kb.sb("sPB", [128, 4, 512])
        kb.dma("sp", sPB, sPB[0:64], I["s5PB"], I["s5PB"][:], grp=g)
        sQ = kb.sb("sQ", [128, 3, 16])
        kb.dma("sp", sQ, sQ[:], I["s5Q"], I["s5Q"][:], grp=g)
        sQC = kb.sb("sQC", [128, 2, 16, 16])
        kb.dma("sp", sQC, sQC[:], I["s5QC"], I["s5QC"][:], grp=g)
        kb.dma("sp", self.s5st, self.s5st[:], I["s5st"], I["s5st"][:], grp=g)
        g.close()
        def chainR():
            rr = kb.sb("rr", [128, 5, 256])
            for k in range(5):
                kb.op("dve", lambda e, k=k: e.tensor_copy(rr[:, k, :].rearrange("p (t q) -> p t q", t=4), sR[:, :, k, :]), reads=[sR], writes=[rr])
            lbr, lbi, cfr, cfi = self.s5_lambda("R", rr[:, 0, :], rr[:, 1, :], rr[:, 2, :], [128, 256], 128, [rr])
            cur_r, cur_i = kb.sb("curRr", [128, 256]), kb.sb("curRi", [128, 256])
            ta, tb = kb.sb("taR", [128, 256]), kb.sb("tbR", [128, 256])
            allr = [rr, lbr, lbi, cfr, cfi, cur_r, cur_i, ta, tb]
            self.cmul(None, cur_r[:], cur_i[:], cfr[:], cfi[:], rr[:, 3, :], rr[:, 4, :], (ta[:], tb[:]), allr, [cur_r, cur_i, ta, tb])
            eo = self.C("eo")
            for d in range(8):
                i = 7 - d
                for c, cur in enumerate((cur_r, cur_i)):
                    for g2 in range(2):
                        kb.op("pool", lambda e, c=c, cur=cur, g2=g2, i=i: e.tensor_scalar(
                            self.EW[:, :, c, i, g2 * 64:(g2 + 1) * 64], cur[:].rearrange("p (t q) -> p t q", t=4), eo[:, g2:g2 + 1], None, ALU.mult),
                            reads=[cur, self.consts], writes=[self.EW])
                if d < 7:
                    self.cmul(None, cur_r[:], cur_i[:], cur_r[:], cur_i[:], lbr[:], lbi[:], (ta[:], tb[:]), allr, [cur_r, cur_i, ta, tb])

        def chainP():
            P64 = slice(0, 64)
            lbrP, lbiP, cfrP, cfiP = self.s5_lambda("P", sP[P64, 0, :], sP[P64, 1, :], sP[P64, 2, :], [64, 32], 64, [sP])
            cpr, cpi = kb.sb("cpr", [128, 512]), kb.sb("cpi", [128, 512])
            tpa, tpb = kb.sb("tpa", [128, 512]), kb.sb("tpb", [128, 512])
            nci = kb.sb("nci", [128, 512])
            allp = [sPB, lbrP, lbiP, cfrP, cfiP, cpr, cpi, tpa, tpb]
            bc = lambda t: t[P64, :].to_broadcast([64, 32, 16]) if False else t[P64, :].unsqueeze(2).to_broadcast([64, 32, 16])
            v3 = lambda ap: ap.rearrange("p (g h) -> p g h", h=16)
            self.cmul(None, v3(cpr[P64]), v3(cpi[P64]), bc(cfrP), bc(cfiP), v3(sPB[P64, 0, :]), v3(sPB[P64, 1, :]), (v3(tpa[P64]), v3(tpb[P64])), allp, [cpr, cpi, tpa, tpb])
            kb.op("dve", lambda e: e.tensor_scalar_mul(nci[P64], sPB[P64, 3, :], -1.0), reads=[sPB], writes=[nci])
            dsk = self.colp[:, 36:40]
            for d in range(8):
                for t in range(4):
                    sl = slice(t * 128, (t + 1) * 128)
                    pk = self.pb[4 + (t % 2)]
                    kb.op("pe", lambda e, sl=sl, pk=pk: e.matmul(pk[:, 0:128], cpr[P64, sl], sPB[P64, 2, sl], start=True, stop=False), reads=[cpr, sPB], writes=[pk])
                    kb.op("pe", lambda e, sl=sl, pk=pk: e.matmul(pk[:, 0:128], cpi[P64, sl], nci[P64, sl], start=False, stop=True), reads=[cpi, nci], writes=[pk])
                    if d == 0:
                        kb.op("dve", lambda e, pk=pk: e.tensor_tensor(tpa[:, 0:128], pk[:, 0:128], self.C("bd16"), ALU.mult), reads=[pk, self.consts], writes=[tpa])
                        kb.op("dve", lambda e, t=t: e.scalar_tensor_tensor(self.KT[:, t, 0, :], self.C("ident"), dsk[:, t:t + 1], tpa[:, 0:128], ALU.mult, ALU.add),
                              reads=[tpa, self.consts, self.colp], writes=[self.KT])
                    else:
                        kb.op("dve", lambda e, pk=pk, t=t, d=d: e.tensor_tensor(self.KT[:, t, d, :], pk[:, 0:128], self.C("bd16"), ALU.mult), reads=[pk, self.consts], writes=[self.KT])
                if d < 7:
                    self.cmul(None, v3(cpr[P64]), v3(cpi[P64]), v3(cpr[P64]), v3(cpi[P64]), bc(lbrP), bc(lbiP), (v3(tpa[P64]), v3(tpb[P64])), allp, [cpr, cpi, tpa, tpb])

        def chainQ():
            lbrQ, lbiQ, _, _ = self.s5_lambda("Q", sQ[:, 0, :], sQ[:, 1, :], sQ[:, 2, :], [128, 16], 128, [sQ])
            tq = kb.sb("tq", [128, 2, 16])
            allq = [lbrQ, lbiQ, self.pwr, self.pwi, tq]
            kb.op("dve", lambda e: e.tensor_copy(self.pwr[:, 0, :], lbrQ[:]), reads=[lbrQ], writes=[self.pwr])
            kb.op("dve", lambda e: e.tensor_copy(self.pwi[:, 0, :], lbiQ[:]), reads=[lbiQ], writes=[self.pwi])
            for j in range(1, 8):
                self.cmul(None, self.pwr[:, j, :], self.pwi[:, j, :], self.pwr[:, j - 1, :], self.pwi[:, j - 1, :], lbrQ[:], lbiQ[:], (tq[:, 0, :], tq[:, 1, :]), allq, [self.pwr, self.pwi, tq])
            for j in range(8, 12):
                self.cmul(None, self.pwr[:, j, :], self.pwi[:, j, :], self.pwr[:, j - 1, :], self.pwi[:, j - 1, :], self.pwr[:, j - 1, :], self.pwi[:, j - 1, :], (tq[:, 0, :], tq[:, 1, :]), allq, [self.pwr, self.pwi, tq])
            kb.op("pool", lambda e: e.memset(self.CW[:], 0.0), writes=[self.CW])
            qa, qb, qc = kb.sb("qa", [128, 16, 16]), kb.sb("qb", [128, 16, 16]), kb.sb("qc", [128, 16, 16])
            pb_ = lambda t, j: t[:, j, :].unsqueeze(2).to_broadcast([128, 16, 16])
            for j in range(8):
                kb.op("dve", lambda e, j=j: e.tensor_tensor(qa[:], sQC[:, 0, :, :], pb_(self.pwr, j), ALU.mult), reads=[sQC, self.pwr], writes=[qa])
                kb.op("dve", lambda e, j=j: e.tensor_tensor(qb[:], sQC[:, 1, :, :], pb_(self.pwi, j), ALU.mult), reads=[sQC, self.pwi], writes=[qb])
                kb.op("dve", lambda e: e.tensor_tensor(qc[:], qa[:], qb[:], ALU.subtract), reads=[qa, qb], writes=[qc])
                for g2 in range(2):
                    Pq = slice(g2 * 64, (g2 + 1) * 64)
                    kb.op("dve", lambda e, j=j, g2=g2, Pq=Pq: e.tensor_copy(self.CW[Pq, :, j, 0, g2 * 16:(g2 + 1) * 16], qc[Pq]), reads=[qc], writes=[self.CW])
                kb.op("dve", lambda e, j=j: e.tensor_tensor(qa[:], sQC[:, 0, :, :], pb_(self.pwi, j), ALU.mult), reads=[sQC, self.pwi], writes=[qa])
                kb.op("dve", lambda e, j=j: e.tensor_tensor(qb[:], sQC[:, 1, :, :], pb_(self.pwr, j), ALU.mult), reads=[sQC, self.pwr], writes=[qb])
                kb.op("dve", lambda e: e.scalar_tensor_tensor(qc[:], qa[:], -1.0, qb[:], ALU.mult, ALU.subtract), reads=[qa, qb], writes=[qc])
                for g2 in range(2):
                    Pq = slice(g2 * 64, (g2 + 1) * 64)
                    kb.op("dve", lambda e, j=j, g2=g2, Pq=Pq: e.tensor_copy(self.CW[Pq, :, j, 1, g2 * 16:(g2 + 1) * 16], qc[Pq]), reads=[qc], writes=[self.CW])

        kb.interleave([chainR, chainP, chainQ])
        kb.op("pool", lambda e: e.memset(self.CWz[:], 0.0), writes=[self.CWz])
        for t in range(4):
            kb.op("pool", lambda e, t=t: e.tensor_copy(self.CWz[:, t, :, :, 32:64], self.CW[:, 4 * t + 3, :, :, :]), reads=[self.CW], writes=[self.CWz])
        kb.op("dve", lambda e: e.memset(self.s5c[:], 0.0), writes=[self.s5c])

    def setup_rwkv(self):
        kb = self.kb
        I = self.I
        kb.scope_enter()
        g = kb.group("rwpar")
        sh0 = kb.sb("sh0", [128, 1792])
        kb.dma("sp", sh0, sh0[0:16, :], I["shift0"], I["shift0"][:], grp=g)
        wl = kb.sb("wl", [128, 3, 512])
        kb.dma("sp", wl, wl[0:64, 0, :], I["w2"], I["w2"][:], grp=g)
        kb.dma("sp", wl, wl[64:128, 1, :], I["a2"], I["a2"][:], grp=g)
        kb.dma("sp", wl, wl[:, 2, :], I["g2"], I["g2"][:], grp=g)
        g.close()
        kb.op("dve", lambda e: e.tensor_copy(self.w2b[0:64, :], wl[0:64, 0, :]), reads=[wl], writes=[self.w2b])
        kb.op("dve", lambda e: e.tensor_copy(self.a2b[64:128, :], wl[64:128, 1, :]), reads=[wl], writes=[self.a2b])
        kb.op("dve", lambda e: e.tensor_copy(self.g2b[:], wl[:, 2, :]), reads=[wl], writes=[self.g2b])
        kb.op("dve", lambda e: e.memset(self.Zf[:], 0.0), writes=[self.Zf])
        kb.op("dve", lambda e: e.memset(self.Zb[:], 0.0), writes=[self.Zb])
        pk = self.pb[6]
        for r in range(14):
            kb.op("pe", lambda e, r=r: e.transpose(pk[:, r * 16:(r + 1) * 16], sh0[0:16, r * 128:(r + 1) * 128], self.C("ident", slice(0, 16))[:, 0:16]),
                  reads=[sh0, self.consts], writes=[pk])
        kb.op("dve", lambda e: e.tensor_copy(self.shiftT[:].rearrange("p r b -> p (r b)"), pk[:, 0:224]), reads=[pk], writes=[self.shiftT])
        kb.scope_exit()

    def block(self, blk):
        kb = self.kb
        I = self.I
        sample = blk == NPB
        nb = TS if sample else NB
        ntile = nb // 128
        typ = 1 if sample else 0
        row0 = TP if sample else blk * NB
        kb.scope_enter()
        self.uT = kb.sb("uT", [128, 4, nb], BF16)
        self.psT = kb.sb("psT", [128, 14, nb])
        self.ycat = kb.sb("ycat", [128, 8, nb], BF16)
        kb.scope_enter()
        self.xblk = kb.sb("xblk", [128, 1, D])
        self.sqj = kb.sb("sqj", [128, D], BF16)
        self.ss = kb.sb("ss", [128, 2])
        self.rstd = kb.sb("rstd", [128, 2])
        self.t1k = kb.sb("t1k", [128, D])
        self.hb = kb.sb("hb", [128, D], BF16)
        self.hT = kb.sb("hT", [128, 8, nb], BF16)
        self.tsh = kb.sb("tsh", [128, nb])
        A1, SH1, GA1 = self.mod[1], self.mod[0], self.mod[2]
        if sample:
            for n in range(3):
                kb.dma("sp", self.mod[n], self.mod[n][:], self.mod2s, self.mod2s[1, n])
        for j in range(ntile):
            kb.dma("sp", self.xblk, self.xblk[:, 0, :], I["xall"], I["xall"][row0 + j * 128:row0 + (j + 1) * 128, :])
            kb.op("act", lambda e, j=j: e.activation(self.sqj[:], self.xblk[:, 0, :], AF.Square, accum_out=self.ss[:, j:j + 1]), reads=[self.xblk], writes=[self.sqj, self.ss])
            kb.op("act", lambda e, j=j: e.activation(self.rstd[:, j:j + 1], self.ss[:, j:j + 1], AF.Sqrt, bias=self.eps6[:], scale=1.0 / D), reads=[self.ss, self.eps6], writes=[self.rstd])
            kb.op("dve", lambda e, j=j: e.reciprocal(self.rstd[:, j:j + 1], self.rstd[:, j:j + 1]), reads=[self.rstd], writes=[self.rstd])
            kb.op("dve", lambda e, j=j: e.scalar_tensor_tensor(self.t1k[:], self.xblk[:, 0, :], self.rstd[:, j:j + 1], A1[:], ALU.mult, ALU.mult), reads=[self.xblk, self.rstd, A1], writes=[self.t1k])
            kb.op("dve", lambda e: e.tensor_tensor(self.hb[:], self.t1k[:], SH1[:], ALU.add), reads=[self.t1k, SH1], writes=[self.hb])
            for c in range(8):
                kb.op("pe", lambda e, c=c: e.transpose(self.pt[:, c * 128:(c + 1) * 128], self.hb[:, c * 128:(c + 1) * 128], self.identb[:]), reads=[self.hb, self.identb], writes=[self.pt])
            kb.op("act", lambda e, j=j: e.copy(self.hT[:, :, j * 128:(j + 1) * 128], self.pt[:].rearrange("p (c n) -> p c n", c=8)), reads=[self.pt], writes=[self.hT])
        wcs = self.wcs
        for cc in range(3):
            kb.dma("sp", wcs[cc % 4], wcs[cc % 4][:], self.wins, self.wins[cc])
        for cc in range(18):
            pk = self.pb[cc % 2]
            wc = wcs[cc % 4]
            if cc + 3 < 18:
                kb.dma("sp", wcs[(cc + 3) % 4], wcs[(cc + 3) % 4][:], self.wins, self.wins[cc + 3])
            for k in range(8):
                kb.op("pe", lambda e, cc=cc, k=k, pk=pk: e.matmul(pk[:, 0:nb], wc[:, k, :], self.hT[:, k, 0:nb], start=(k == 0), stop=(k == 7)),
                      reads=[wc, self.hT], writes=[pk])
            if cc < 4:
                kb.op("act", lambda e, cc=cc, pk=pk: e.copy(self.uT[:, cc, 0:nb], pk[:, 0:nb]), reads=[pk], writes=[self.uT])
            else:
                r = cc - 4
                mu = self.colp[:, r:r + 1]
                omm = self.colp[:, 14 + r:15 + r]
                kb.op("act", lambda e, pk=pk, omm=omm: e.activation(self.tsh[:, 0:nb], pk[:, 0:nb], AF.Identity, scale=omm), reads=[pk, self.colp], writes=[self.tsh])
                kb.op("dve", lambda e, pk=pk, r=r, mu=mu: e.scalar_tensor_tensor(self.psT[:, r, 1:nb], pk[:, 0:nb - 1], mu, self.tsh[:, 1:nb], ALU.mult, ALU.add),
                      reads=[pk, self.colp, self.tsh], writes=[self.psT])
                if not sample:
                    kb.op("dve", lambda e, r=r, mu=mu: e.scalar_tensor_tensor(self.psT[:, r, 0:1], self.carry[:, r:r + 1], mu, self.tsh[:, 0:1], ALU.mult, ALU.add),
                          reads=[self.carry, self.colp, self.tsh], writes=[self.psT])
                    kb.op("dve", lambda e, r=r, pk=pk: e.tensor_copy(self.carry[:, r:r + 1], pk[:, nb - 1:nb]), reads=[pk], writes=[self.carry])
                else:
                    kb.op("dve", lambda e, r=r, mu=mu: e.scalar_tensor_tensor(self.psT[:, r, 0:nb:8], self.shiftT[:, r, :], mu, self.tsh[:, 0:nb:8], ALU.mult, ALU.add),
                          reads=[self.shiftT, self.colp, self.tsh], writes=[self.psT])
                    kb.op("dve", lambda e, r=r, pk=pk: e.tensor_copy(self.shsT[:, r, :], pk[:, 7:nb:8]), reads=[pk], writes=[self.shsT])
        kb.scope_exit()
        if blk == 0:
            self.dbg_dump("uT0", self.uT, self.uT[:], [128, 4, NB], BF16)
            self.dbg_dump("psT0", self.psT, self.psT[:], [128, 14, NB])
        kb.scope_enter()
        if "skip_s5" not in self.dbg:
            self.s5_block(blk, sample, nb)
        kb.scope_exit()
        kb.scope_enter()
        if "skip_rw" not in self.dbg:
            self.rwkv_block(blk, sample, nb)
        kb.scope_exit()
        kb.scope_enter()
        self.outproj(blk, sample, nb, ntile, row0, GA1)
        kb.scope_exit()
        kb.scope_exit()

    def s5_block(self, blk, sample, nb):
        kb = self.kb
        nsb = nb // 8
        if True:
            self.Ea = kb.sb("Ea", [128, 2, 16, 32])
            self.Eb = kb.sb("Eb", [128, 2, 16, 32])
            self.hst = kb.sb("hst", [128, 2, 16, 32])
            self.Cin = kb.sb("Cin", [128, 2, 16, 32], BF16)
            self.ypre = kb.sb("ypre", [128, 4, NB])
            self.ygb = kb.sb("ygb", [128, 4, NB], BF16)
            self.sig = kb.sb("sig", [128, NB])
            self.s5tmp = kb.sb("s5tmp", [128, 4, 16])
        uT, EW, KT, CW, CWz = self.uT, self.EW, self.KT, self.CW, self.CWz
        if True:
            self.uTz = kb.sb("uTz", [128, 4, NB], BF16)
        uTz = self.uTz
        kb.op("pool", lambda e: e.tensor_scalar(uTz[64:128, :, 0:nb], uT[64:128, :, 0:nb], self.C("m96", slice(64, 128)), None, ALU.mult), reads=[uT, self.consts], writes=[uTz])
        pE = [self.pb[2], self.pb[3]]
        for pair in range(16):
            tile, r0 = pair // 4, 32 * (pair % 4)
            kk_ = 32
            src = uT
            if pair % 4 == 3:
                r0, kk_, src = 64, 64, uTz
            for c in range(2):
                for i in range(8):
                    kb.op("pe", lambda e, pair=pair, tile=tile, r0=r0, c=c, i=i, kk_=kk_, src=src: e.matmul(
                        pE[c][:, pair * nsb:(pair + 1) * nsb], EW[r0:r0 + kk_, tile, c, i, :], src[r0:r0 + kk_, tile, i:nb:8], start=(i == 0), stop=(i == 7)),
                        reads=[EW, src], writes=[pE[c]])
        A, B = self.Ea, self.Eb
        for c in range(2):
            kb.op("act", lambda e, c=c: e.copy(A[:, c, :, 0:nsb], pE[c][:, 0:16 * nsb].rearrange("p (a b) -> p a b", a=16)), reads=[pE[c]], writes=[A])
        tt = lambda out, a, b, op, rd, wr: kb.op("dve", lambda e: e.tensor_tensor(out, a, b, op), reads=rd, writes=wr)
        pw, pwi = self.pwr, self.pwi
        tmp = self.s5tmp
        if not sample:
            cr, ci = self.s5c[:, 0, :], self.s5c[:, 1, :]
            rd = [pw, pwi, self.s5c, tmp, A]
            tt(tmp[:, 0, :], pw[:, 7, :], cr, ALU.mult, rd, [tmp])
            tt(tmp[:, 1, :], pwi[:, 7, :], ci, ALU.mult, rd, [tmp])
            tt(tmp[:, 2, :], pw[:, 7, :], ci, ALU.mult, rd, [tmp])
            tt(tmp[:, 3, :], pwi[:, 7, :], cr, ALU.mult, rd, [tmp])
            tt(tmp[:, 0, :], tmp[:, 0, :], tmp[:, 1, :], ALU.subtract, rd, [tmp])
            tt(tmp[:, 2, :], tmp[:, 2, :], tmp[:, 3, :], ALU.add, rd, [tmp])
            tt(A[:, 0, :, 0], A[:, 0, :, 0], tmp[:, 0, :], ALU.add, rd, [A])
            tt(A[:, 1, :, 0], A[:, 1, :, 0], tmp[:, 2, :], ALU.add, rd, [A])
            sft, k = 1, 0
            while sft < nsb:
                n = nsb - sft
                bcr = pw[:, 7 + k, :].unsqueeze(2).to_broadcast([128, 16, n])
                bci = pwi[:, 7 + k, :].unsqueeze(2).to_broadcast([128, 16, n])
                T = self.hst
                rd = [A, pw, pwi, T]
                tt(T[:, 0, :, 0:n], A[:, 0, :, 0:n], bcr, ALU.mult, rd, [T])
                tt(T[:, 1, :, 0:n], A[:, 1, :, 0:n], bci, ALU.mult, rd, [T])
                tt(T[:, 0, :, 0:n], T[:, 0, :, 0:n], T[:, 1, :, 0:n], ALU.subtract, rd, [T])
                tt(B[:, 0, :, sft:nsb], A[:, 0, :, sft:nsb], T[:, 0, :, 0:n], ALU.add, rd, [B])
                tt(T[:, 0, :, 0:n], A[:, 0, :, 0:n], bci, ALU.mult, rd, [T])
                tt(T[:, 1, :, 0:n], A[:, 1, :, 0:n], bcr, ALU.mult, rd, [T])
                tt(T[:, 0, :, 0:n], T[:, 0, :, 0:n], T[:, 1, :, 0:n], ALU.add, rd, [T])
                tt(B[:, 1, :, sft:nsb], A[:, 1, :, sft:nsb], T[:, 0, :, 0:n], ALU.add, rd, [B])
                kb.op("dve", lambda e, A=A, B=B, sft=sft: e.tensor_copy(B[:, :, :, 0:sft], A[:, :, :, 0:sft]), reads=[A], writes=[B])
                A, B = B, A
                sft *= 2
                k += 1
            kb.op("dve", lambda e: e.tensor_copy(self.Cin[:, :, :, 0], self.s5c[:]), reads=[self.s5c], writes=[self.Cin])
            kb.op("dve", lambda e, A=A: e.tensor_copy(self.Cin[:, :, :, 1:nsb], A[:, :, :, 0:nsb - 1]), reads=[A], writes=[self.Cin])
            kb.op("dve", lambda e, A=A: e.tensor_copy(self.s5c[:], A[:, :, :, nsb - 1]), reads=[A, self.Cin], writes=[self.s5c])
            if blk == NPB - 1:
                pk = self.pb[4]
                for c in range(2):
                    kb.op("pe", lambda e, c=c: e.transpose(pk[0:16, c * 128:(c + 1) * 128], self.s5c[:, c, :], self.C("ident")), reads=[self.s5c, self.consts], writes=[pk])
                kb.op("dve", lambda e: e.tensor_copy(self.hst[0:16, 0, 0, 0:256] if False else self.sig[0:16, 0:256], pk[0:16, 0:256]), reads=[pk], writes=[self.sig])
                kb.dma("sp", self.O["s5p"], self.O["s5p"][:].rearrange("c a q -> a c q"), self.sig, self.sig[0:16, 0:256].rearrange("a (c q) -> a c q", c=2), grp=self.g_out)
        else:
            st = self.s5st
            kb.op("pool", lambda e: e.tensor_copy(self.Cin[:, :, :, 0:16], st[:]), reads=[st], writes=[self.Cin])
            F_ = B
            bcr = pw[:, 7, :].unsqueeze(2).to_broadcast([128, 16, 16])
            bci = pwi[:, 7, :].unsqueeze(2).to_broadcast([128, 16, 16])
            T = self.hst
            rd = [A, pw, pwi, T, st]
            tt(T[:, 0, :, 0:16], st[:, 0], bcr, ALU.mult, rd, [T])
            tt(T[:, 1, :, 0:16], st[:, 1], bci, ALU.mult, rd, [T])
            tt(T[:, 0, :, 0:16], T[:, 0, :, 0:16], T[:, 1, :, 0:16], ALU.subtract, rd, [T])
            tt(F_[:, 0, :, 0:16], A[:, 0, :, 0:16], T[:, 0, :, 0:16], ALU.add, rd, [F_])
            tt(T[:, 0, :, 0:16], st[:, 0], bci, ALU.mult, rd, [T])
            tt(T[:, 1, :, 0:16], st[:, 1], bcr, ALU.mult, rd, [T])
            tt(T[:, 0, :, 0:16], T[:, 0, :, 0:16], T[:, 1, :, 0:16], ALU.add, rd, [T])
            tt(F_[:, 1, :, 0:16], A[:, 1, :, 0:16], T[:, 0, :, 0:16], ALU.add, rd, [F_])
            pk = self.pb[4]
            for c in range(2):
                for q4 in range(4):
                    for a in range(4):
                        kb.op("pe", lambda e, c=c, q4=q4, a=a: e.transpose(pk[0:16, a * 128:(a + 1) * 128], F_[:, c, q4 * 4 + a, 0:16], self.C("ident")), reads=[F_, self.consts], writes=[pk])
                    kb.op("dve", lambda e: e.tensor_copy(self.ypre[0:16, 0, 0:512] if False else self.ypre[0:16, 0:2, :].rearrange("p a b -> p (a b)"), pk[0:16, 0:512]), reads=[pk], writes=[self.ypre])
                    kb.dma("sp", self.O["s5s"], self.O["s5s"][c, :, q4 * 4:(q4 + 1) * 4, :], self.ypre, self.ypre[0:16, 0:2, :].rearrange("p a (b q) -> p (a b) q", q=128), grp=self.g_out)
        Cin = self.Cin
        for tile in range(4):
            pY = [self.pb[4], self.pb[1]][tile % 2]
            for j in range(8):
                osl = slice(j * nsb, (j + 1) * nsb)
                for i in range(j + 1):
                    kb.op("pe", lambda e, tile=tile, j=j, i=i, osl=osl, pY=pY: e.matmul(pY[:, osl], KT[:, tile, j - i, :], uT[:, tile, i:nb:8], start=(i == 0), stop=False),
                          reads=[KT, uT], writes=[pY])
                for pl in range(4):
                    pair = tile * 4 + pl
                    for c in range(2):
                        last = (pl == 3 and c == 1)
                        if pl < 3:
                            kb.op("pe", lambda e, pair=pair, pl=pl, j=j, c=c, osl=osl, pY=pY, last=last: e.matmul(
                                pY[32 * pl:32 * pl + 32, osl], CW[:, pair, j, c, :], Cin[:, c, pair, 0:nsb], start=False, stop=last),
                                reads=[CW, Cin], writes=[pY])
                        else:
                            kb.op("pe", lambda e, pair=pair, tile=tile, j=j, c=c, osl=osl, pY=pY, last=last: e.matmul(
                                pY[64:128, osl], CWz[:, tile, j, c, :], Cin[:, c, pair, 0:nsb], start=False, stop=last),
                                reads=[CWz, Cin], writes=[pY])
            kb.op("act", lambda e, tile=tile, pY=pY: e.copy(self.ypre[:, tile, 0:nb].rearrange("p (b j) -> p j b", j=8), pY[:, 0:8 * nsb].rearrange("p (j b) -> p j b", j=8)),
                  reads=[pY], writes=[self.ypre])
        kb.op("act", lambda e: e.activation(self.ypre[:, :, 0:nb], self.ypre[:, :, 0:nb], AF.Gelu), reads=[self.ypre], writes=[self.ypre])
        kb.op("dve", lambda e: e.tensor_copy(self.ygb[:, :, 0:nb], self.ypre[:, :, 0:nb]), reads=[self.ypre], writes=[self.ygb])
        if blk == 0:
            self.dbg_dump("yg0", self.ypre, self.ypre[:], [128, 4, NB])
        for oc in range(4):
            pk = self.pb[oc % 2]
            for c in range(4):
                kb.op("pe", lambda e, oc=oc, c=c, pk=pk: e.matmul(pk[:, 0:nb], self.wglub[:, c, oc * 128:(oc + 1) * 128], self.ygb[:, c, 0:nb], start=(c == 0), stop=(c == 3)),
                      reads=[self.wglub, self.ygb], writes=[pk])
            kb.op("act", lambda e, oc=oc, pk=pk: e.activation(self.sig[:, 0:nb], pk[:, 0:nb], AF.Sigmoid, bias=self.colp[:, 32 + oc:33 + oc]), reads=[pk, self.colp], writes=[self.sig])
            kb.op("dve", lambda e, oc=oc: e.tensor_tensor(self.ycat[:, oc, 0:nb], self.ypre[:, oc, 0:nb], self.sig[:, 0:nb], ALU.mult), reads=[self.ypre, self.sig], writes=[self.ycat])
        if blk == 0:
            self.dbg_dump("ycat0", self.ycat, self.ycat[:], [128, 8, NB], BF16)
        if sample:
            self.dbg_dump("ycatS", self.ycat, self.ycat[:], [128, 8, 128], BF16)

    def rwkv_block(self, blk, sample, nb):
        kb = self.kb
        psT, colp = self.psT, self.colp
        CP = lambda c0, hp: colp[:, c0 + hp:c0 + hp + 1]
        f32t = lambda n, sh=None: kb.sb(n, sh or [128, nb])
        elw, cum, E1, E2, E3 = f32t("elw"), f32t("cum"), f32t("E1"), f32t("E2"), f32t("E3")
        av, kk, sq, kp, tm = f32t("av"), f32t("kk"), f32t("sq"), f32t("kp"), f32t("tm")
        tanhw = kb.sb("tanhw", [128, nb], BF16)
        alob = kb.sb("alob", [128, nb], BF16)
        sigg = kb.sb("sigg", [128, nb], BF16)
        ARt = kb.sb("ARt", [128, 4, 2, nb], BF16)
        BKt = kb.sb("BKt", [128, 4, 2, nb], BF16)
        VbT = kb.sb("VbT", [128, 4, nb], BF16)
        bv = kb.sb("bv", [128, 4, nb])
        gT = kb.sb("gT", [128, 4, nb])
        gam = kb.sb("gam", [128, 4, 16])
        S = slice(0, nb)
        nch = nb // 128
        kb.op("act", lambda e: e.activation(tanhw[0:64, S], psT[0:64, 12, S], AF.Tanh), reads=[psT], writes=[tanhw])
        kb.op("pool", lambda e: e.tensor_copy(alob[64:128, S], psT[64:128, 12, S]), reads=[psT], writes=[alob])
        kb.op("act", lambda e: e.activation(sigg[:, S], psT[:, 13, S], AF.Sigmoid), reads=[psT], writes=[sigg])
        bones = self.C("bones")
        for hp in range(4):
            hs = slice(hp * 128, (hp + 1) * 128)
            p0, p1, p2 = self.pb[0], self.pb[1], self.pb[2]
            kb.op("pe", lambda e: e.matmul(p0[:, S], self.w2b[0:64, hs], tanhw[0:64, S], start=True, stop=True), reads=[self.w2b, tanhw], writes=[p0])
            kb.op("pe", lambda e: e.matmul(p1[:, S], self.a2b[64:128, hs], alob[64:128, S], start=True, stop=True), reads=[self.a2b, alob], writes=[p1])
            kb.op("pe", lambda e: e.matmul(p2[:, S], self.g2b[:, hs], sigg[:, S], start=True, stop=True), reads=[self.g2b, sigg], writes=[p2])
            kb.op("act", lambda e: e.activation(elw[:, S], p0[:, S], AF.Exp, bias=CP(72, hp), scale=-1.0), reads=[p0, colp], writes=[elw])
            kb.op("act", lambda e: e.activation(elw[:, S], elw[:, S], AF.Ln, bias=1.0), reads=[elw], writes=[elw])
            kb.op("act", lambda e: e.activation(elw[:, S], elw[:, S], AF.Exp, bias=-0.5, scale=-1.0), reads=[elw], writes=[elw])
            kb.op("act", lambda e: e.activation(av[:, S], p1[:, S], AF.Sigmoid, bias=CP(44, hp)), reads=[p1, colp], writes=[av])
            kb.op("act", lambda e: e.copy(gT[:, hp, S], p2[:, S]), reads=[p2], writes=[gT])
            for c in range(nch):
                cs = slice(c * 128, (c + 1) * 128)
                d0 = self.C("rst8") if sample else self.C("ones")
                kb.op("dve", lambda e, cs=cs, d0=d0: e.tensor_tensor_scan(cum[:, cs], d0, elw[:, cs], 0.0, ALU.mult, ALU.add), reads=[elw, self.consts], writes=[cum])
            kb.op("act", lambda e: e.activation(E1[:, S], cum[:, S], AF.Exp, scale=-1.0), reads=[cum], writes=[E1])
            kb.op("act", lambda e: e.activation(E2[:, S], cum[:, S], AF.Exp), reads=[cum], writes=[E2])
            kb.op("act", lambda e: e.activation(E3[:, S], elw[:, S], AF.Exp), reads=[elw], writes=[E3])
            kb.op("dve", lambda e: e.tensor_tensor(E3[:, S], E3[:, S], E1[:, S], ALU.mult), reads=[E3, E1], writes=[E3])
            if sample:
                kb.op("dve", lambda e, hp=hp: e.tensor_copy(gam[:, hp, :], E1[:, 7:nb:8]), reads=[E1], writes=[gam])
            else:
                kb.op("dve", lambda e, hp=hp: e.tensor_copy(gam[:, hp, 0:nch], E1[:, 127:nb:128]), reads=[E1], writes=[gam])
            kb.op("dve", lambda e, hp=hp: e.tensor_scalar(kk[:, S], psT[:, 4 + hp, S], CP(48, hp), None, ALU.mult), reads=[psT, colp], writes=[kk])
            kb.op("dve", lambda e: e.tensor_tensor(sq[:, S], kk[:, S], kk[:, S], ALU.mult), reads=[kk], writes=[sq])
            kb.op("pe", lambda e: e.matmul(p0[:, S], bones, sq[:, S], start=True, stop=True), reads=[self.consts, sq], writes=[p0])
            kb.op("act", lambda e: e.activation(sq[:, S], p0[:, S], AF.Sqrt), reads=[p0], writes=[sq])
            kb.op("dve", lambda e: e.tensor_scalar_max(sq[:, S], sq[:, S], 1e-12), reads=[sq], writes=[sq])
            kb.op("dve", lambda e: e.reciprocal(sq[:, S], sq[:, S]), reads=[sq], writes=[sq])
            kb.op("dve", lambda e: e.tensor_tensor(kk[:, S], kk[:, S], sq[:, S], ALU.mult), reads=[kk, sq], writes=[kk])
            kb.op("dve", lambda e, hp=hp: e.tensor_scalar(tm[:, S], av[:, S], CP(52, hp), CP(56, hp), ALU.mult, ALU.add), reads=[av, colp], writes=[tm])
            kb.op("dve", lambda e, hp=hp: e.tensor_tensor(kp[:, S], psT[:, 4 + hp, S], tm[:, S], ALU.mult), reads=[psT, tm], writes=[kp])
            kb.op("dve", lambda e, hp=hp: e.scalar_tensor_tensor(ARt[:, hp, 0, S], kk[:, S], -1.0, E3[:, S], ALU.mult, ALU.mult), reads=[kk, E3], writes=[ARt])
            kb.op("dve", lambda e, hp=hp: e.tensor_tensor(ARt[:, hp, 1, S], psT[:, hp, S], E1[:, S], ALU.mult), reads=[psT, E1], writes=[ARt])
            kb.op("dve", lambda e: e.tensor_tensor(tm[:, S], kk[:, S], av[:, S], ALU.mult), reads=[kk, av], writes=[tm])
            kb.op("dve", lambda e, hp=hp: e.tensor_tensor(BKt[:, hp, 0, S], tm[:, S], E2[:, S], ALU.mult), reads=[tm, E2], writes=[BKt])
            kb.op("dve", lambda e, hp=hp: e.tensor_tensor(BKt[:, hp, 1, S], kp[:, S], E2[:, S], ALU.mult), reads=[kp, E2], writes=[BKt])
            kb.op("pool", lambda e, hp=hp: e.tensor_copy(VbT[:, hp, S], psT[:, 8 + hp, S]), reads=[psT], writes=[VbT])
            kb.op("dve", lambda e, hp=hp: e.scalar_tensor_tensor(tm[:, S], psT[:, hp, S], CP(60, hp), kp[:, S], ALU.mult, ALU.mult), reads=[psT, colp, kp], writes=[tm])
            kb.op("pe", lambda e: e.matmul(p1[:, S], bones, tm[:, S], start=True, stop=True), reads=[self.consts, tm], writes=[p1])
            kb.op("dve", lambda e, hp=hp: e.tensor_tensor(bv[:, hp, S], p1[:, S], psT[:, 8 + hp, S], ALU.mult), reads=[p1, psT], writes=[bv])
        if blk == 0:
            self.dbg_dump("ARt0", ARt, ARt[:], [128, 4, 2, NB], BF16)
            self.dbg_dump("BKt0", BKt, BKt[:], [128, 4, 2, NB], BF16)
        if ("b0only" in self.dbg and blk > 0) or "stop_prep" in self.dbg or ("no_sample" in self.dbg and sample) or ("no_prompt" in self.dbg and not sample):
            return
        nsq = 3 if sample else 6
        mk = lambda n: self.C(("s" if sample else "") + n)
        MUS, MUI, MLS = mk("mu_s"), mk("mu_i"), mk("ml_s")
        ident = self.C("ident")
        tokm2 = [kb.sb(f"tokm{i}", [128, 4, 128], BF16) for i in range(2)]
        UD = BF16 if sample else F32
        Wt = [[kb.sb(f"W{h}{i}", [128, 128], UD) for i in range(2)] for h in range(4)]
        At = [[kb.sb(f"A{h}{i}", [128, 128], UD) for i in range(2)] for h in range(4)]
        IW = [[kb.sb(f"IW{h}{i}", [128, 128], UD) for i in range(2)] for h in range(4)]
        Xfin = [kb.sb(f"Xfin{h}", [128, 128], BF16) for h in range(4)]
        PT = [kb.sb(f"PT{h}", [128, 128], BF16) for h in range(4)]
        MT = [kb.sb(f"MT{h}", [128, 128], BF16) for h in range(4)]
        QT = [kb.sb(f"QT{h}", [128, 128], BF16) for h in range(4)]
        Xt = [[kb.sb(f"X{h}{i}", [128, 128], UD) for i in range(2)] for h in range(4)]
        RhT = kb.sb("RhT", [128, 128], BF16)
        GTt = kb.sb("GTt", [128, 64], BF16)
        ysb = kb.sb("ysb", [128, 128])
        dd = kb.sb("dd", [128, 128])
        d2 = kb.sb("d2", [128, 128])
        rs_ = kb.sb("rs_", [128, 128])
        if sample:
            Zsb = kb.sb("Zsb", [128, 4, 16, 64], BF16)
            Znh = kb.sb("Znh", [128, 16, 64])
            kb.dma("pool", Zsb, Zsb[:].rearrange("p a b v -> p (a b v)"), self.I["wkv0"], self.I["wkv0"][:])
            Bex = kb.sb("Bex", [128, 16, 64], BF16)
            Uex = kb.sb("Uex", [128, 16, 64], BF16)
            Vex = kb.sb("Vex", [128, 16, 64], BF16)
            GTs = kb.sb("GTs", [128, 16, 64], BF16)
            Hs = kb.sb("Hs", [128, 16, 64])
        pb = self.pb
        i2 = self.C("i2")
        for c in range(nch):
            cs = slice(c * 128, (c + 1) * 128)
            for q2 in range(2):
              hps = [2 * q2, 2 * q2 + 1]
              for i, hp in enumerate(hps):
                tokm = tokm2[i]
                srcs = [ARt[:, hp, 0, cs], BKt[:, hp, 0, cs], BKt[:, hp, 1, cs], VbT[:, hp, cs]]
                for ii, sap in enumerate(srcs):
                    kb.op("pe", lambda e, ii=ii, sap=sap: e.transpose(self.pt[:, ii * 128:(ii + 1) * 128], sap, self.identb[:]), reads=[ARt, BKt, VbT, self.identb], writes=[self.pt])
                kb.op("act", lambda e: e.copy(tokm[:].rearrange("p a n -> p (a n)"), self.pt[:, 0:512]), reads=[self.pt], writes=[tokm])
              for i, hp in enumerate(hps):
                tokm = tokm2[i]
                for h2 in range(2):
                    hq = 2 * i + h2
                    P_ = slice(h2 * 64, (h2 + 1) * 64)
                    W, A, Iw, X = Wt[hq], At[hq], IW[hq], Xt[hq]
                    pa, pbk = pb[hq], pb[4]
                    kb.op("pe", lambda e: e.matmul(pa[:, 0:256].rearrange("p (a n) -> p a n", a=2), BKt[P_, hp, 0, cs], ARt[P_, hp, :, cs], start=True, stop=True), reads=[BKt, ARt], writes=[pa])
                    kb.op("pe", lambda e: e.matmul(pa[:, 256:512].rearrange("p (a n) -> p a n", a=2), BKt[P_, hp, 1, cs], ARt[P_, hp, :, cs], start=True, stop=True), reads=[BKt, ARt], writes=[pa])
                    kb.op("pe", lambda e: e.matmul(pbk[:, 0:128], ARt[P_, hp, 0, cs], BKt[P_, hp, 0, cs], start=True, stop=True), reads=[BKt, ARt], writes=[pbk])
                    kb.op("dve", lambda e: e.tensor_tensor(W[0][:], pa[:, 0:128], MUS, ALU.mult), reads=[pa, self.consts], writes=[W[0]])
                    kb.op("dve", lambda e: e.tensor_tensor(PT[hq][:], pa[:, 128:256], MUI, ALU.mult), reads=[pa, self.consts], writes=[PT[hq]])
                    kb.op("dve", lambda e: e.tensor_tensor(MT[hq][:], pa[:, 256:384], MUS, ALU.mult), reads=[pa, self.consts], writes=[MT[hq]])
                    kb.op("dve", lambda e: e.tensor_tensor(QT[hq][:], pa[:, 384:512], MUI, ALU.mult), reads=[pa, self.consts], writes=[QT[hq]])
                    kb.op("dve", lambda e: e.tensor_tensor(A[0][:], pbk[:, 0:128], MLS, ALU.mult), reads=[pbk, self.consts], writes=[A[0]])
                    kb.op("dve", lambda e: e.tensor_tensor(Iw[0][:], W[0][:], ident, ALU.add), reads=[W[0], self.consts], writes=[Iw[0]])
                    kb.op("pe", lambda e: e.matmul(pbk[:, 128:192], MT[hq][:], tokm[:, 3, h2 * 64:(h2 + 1) * 64], start=True, stop=True), reads=[MT[hq], tokm], writes=[pbk])
                    kb.op("dve", lambda e: e.tensor_copy(X[0][:, 0:64], tokm[:, 0, h2 * 64:(h2 + 1) * 64]), reads=[tokm], writes=[X[0]])
                    kb.op("act", lambda e: e.copy(X[0][:, 64:128], pbk[:, 128:192]), reads=[pbk], writes=[X[0]])
              for j in range(nsq + 1):
                a, b = j % 2, (j + 1) % 2
                for hq in range(4):
                    W, A, Iw, X = Wt[hq], At[hq], IW[hq], Xt[hq]
                    pk = pb[hq]
                    kb.op("pe", lambda e: e.matmul(pk[:, 256:384], Iw[a][:], X[a][:], start=True, stop=True), reads=[Iw[a], X[a]], writes=[pk])
                    if j < nsq:
                        kb.op("pe", lambda e: e.matmul(pk[:, 0:128], A[a][:], W[a][:], start=True, stop=True), reads=[A[a], W[a]], writes=[pk])
                    if j < nsq - 1:
                        kb.op("pe", lambda e: e.matmul(pk[:, 128:256], W[a][:], A[a][:], start=True, stop=True), reads=[A[a], W[a]], writes=[pk])
                    dst = Xfin[hq] if j == nsq else X[b]
                    if hq % 2 == 0:
                        kb.op("dve", lambda e: e.tensor_copy(dst[:], pk[:, 256:384]), reads=[pk], writes=[dst])
                    else:
                        kb.op("act", lambda e: e.copy(dst[:], pk[:, 256:384]), reads=[pk], writes=[dst])
                    if j < nsq:
                        kb.op("dve", lambda e: e.tensor_tensor(Iw[b][:], pk[:, 0:128], ident, ALU.add), reads=[pk, self.consts], writes=[Iw[b]])
                    if j < nsq - 1:
                        kb.op("act", lambda e: e.copy(W[b][:], pk[:, 0:128]), reads=[pk], writes=[W[b]])
                        kb.op("act", lambda e: e.copy(A[b][:], pk[:, 128:256]), reads=[pk], writes=[A[b]])
              for i, hp in enumerate(hps):
                tokm = tokm2[i]
                XF = [Xfin[2 * i], Xfin[2 * i + 1]]
                PTl = [PT[2 * i], PT[2 * i + 1]]
                QTl = [QT[2 * i], QT[2 * i + 1]]
                pR, pG, pH, pYT = pb[2], pb[3], pb[4], pb[2]
                for h2 in range(2):
                    P_ = slice(h2 * 64, (h2 + 1) * 64)
                    X = XF[h2]
                    kb.op("pe", lambda e, P_=P_, X=X, h2=h2: e.matmul(pR[P_, 0:128], X[:, 0:64], PTl[h2][:], start=True, stop=True), reads=[X, PTl[h2]], writes=[pR])
                    kb.op("pe", lambda e, P_=P_, X=X, h2=h2: e.matmul(pG[P_, 0:64], X[:, 0:64], tokm[:, 1, h2 * 64:(h2 + 1) * 64], start=True, stop=True), reads=[X, tokm], writes=[pG])
                kb.op("dve", lambda e: e.tensor_tensor(RhT[:], pR[:, 0:128], ARt[:, hp, 1, cs], ALU.add), reads=[pR, ARt], writes=[RhT])
                if sample:
                    kb.op("dve", lambda e: e.tensor_tensor(GTt[:], pG[:, 0:64], i2, ALU.add), reads=[pG, self.consts], writes=[GTt])
                else:
                    kb.op("dve", lambda e: e.tensor_copy(GTt[:], pG[:, 0:64]), reads=[pG], writes=[GTt])
                if not sample:
                    for h2 in range(2):
                        P_ = slice(h2 * 64, (h2 + 1) * 64)
                        X = XF[h2]
                        vt = tokm[:, 3, h2 * 64:(h2 + 1) * 64]
                        kb.op("pe", lambda e, P_=P_, X=X, h2=h2: e.matmul(pYT[P_, 128:256], X[:, 64:128], PTl[h2][:], start=True, stop=False), reads=[X, PTl[h2]], writes=[pYT])
                        kb.op("pe", lambda e, P_=P_, vt=vt, h2=h2: e.matmul(pYT[P_, 128:256], vt, QTl[h2][:], start=False, stop=False), reads=[tokm, QTl[h2]], writes=[pYT])
                        kb.op("pe", lambda e, P_=P_: e.matmul(pYT[P_, 128:256], self.Zb[P_, hp, :], RhT[P_, :], start=False, stop=True), reads=[self.Zb, RhT], writes=[pYT])
                        kb.op("pe", lambda e, P_=P_, X=X, h2=h2: e.matmul(pH[P_, 0:64], tokm[:, 1, h2 * 64:(h2 + 1) * 64], X[:, 64:128], start=True, stop=False), reads=[tokm, X], writes=[pH])
                        kb.op("pe", lambda e, P_=P_, vt=vt, h2=h2: e.matmul(pH[P_, 0:64], tokm[:, 2, h2 * 64:(h2 + 1) * 64], vt, start=False, stop=False), reads=[tokm], writes=[pH])
                        kb.op("pe", lambda e, P_=P_: e.matmul(pH[P_, 0:64], GTt[P_, :], self.Zb[P_, hp, :], start=False, stop=True), reads=[GTt, self.Zb], writes=[pH])
                    kb.op("dve", lambda e: e.tensor_tensor(self.Zf[:, hp, :], pH[:, 0:64], self.Zf[:, hp, :], ALU.add), reads=[pH, self.Zf], writes=[self.Zf])
                    kb.op("dve", lambda e, c=c: e.tensor_scalar(self.Zf[:, hp, :], self.Zf[:, hp, :], gam[:, hp, c:c + 1], None, ALU.mult), reads=[self.Zf, gam], writes=[self.Zf])
                    kb.op("dve", lambda e: e.tensor_copy(self.Zb[:, hp, :], self.Zf[:, hp, :]), reads=[self.Zf], writes=[self.Zb])
                else:
                    sel = self.C("sel16")
                    for h2 in range(2):
                        P_ = slice(h2 * 64, (h2 + 1) * 64)
                        X = XF[h2]
                        vt = tokm[:, 3, h2 * 64:(h2 + 1) * 64]
                        kb.op("pe", lambda e, P_=P_, X=X, h2=h2: e.matmul(pYT[P_, 128:256], X[:, 64:128], PTl[h2][:], start=True, stop=False), reads=[X, PTl[h2]], writes=[pYT])
                        kb.op("pe", lambda e, P_=P_, vt=vt, h2=h2: e.matmul(pYT[P_, 128:256], vt, QTl[h2][:], start=False, stop=True), reads=[tokm, QTl[h2]], writes=[pYT])
                        for b in range(16):
                            kb.op("pe", lambda e, P_=P_, b=b: e.matmul(pb[0][P_, b * 8:(b + 1) * 8], Zsb[P_, hp, b, :], RhT[P_, b * 8:(b + 1) * 8], start=True, stop=True), reads=[Zsb, RhT], writes=[pb[0]])
                        bx = lambda ap: ap.unsqueeze(1).to_broadcast([128, 16, 64])
                        sx = sel.unsqueeze(2).to_broadcast([128, 16, 64])
                        kb.op("dve", lambda e, h2=h2: e.tensor_tensor(Bex[:], bx(tokm[:, 1, h2 * 64:(h2 + 1) * 64]), sx, ALU.mult), reads=[tokm, self.consts], writes=[Bex])
                        kb.op("dve", lambda e, X=X: e.tensor_tensor(Uex[:], bx(X[:, 64:128]), sx, ALU.mult), reads=[X, self.consts], writes=[Uex])
                        kb.op("pool", lambda e, vt=vt: e.tensor_tensor(Vex[:], bx(vt), sx, ALU.mult), reads=[tokm, self.consts], writes=[Vex])
                        for hf in range(2):
                            bsl = slice(hf * 8, (hf + 1) * 8)
                            pg, ph = pb[3], pb[4]
                            kb.op("pe", lambda e, P_=P_, X=X, bsl=bsl, pg=pg: e.matmul(pg[P_, :], X[:, 0:64], Bex[:, bsl, :], start=True, stop=True), reads=[X, Bex], writes=[pg])
                            kb.op("pe", lambda e, P_=P_, bsl=bsl, ph=ph, h2=h2: e.matmul(ph[P_, :], tokm[:, 1, h2 * 64:(h2 + 1) * 64], Uex[:, bsl, :], start=True, stop=False), reads=[tokm, Uex], writes=[ph])
                            kb.op("pe", lambda e, P_=P_, bsl=bsl, ph=ph, h2=h2: e.matmul(ph[P_, :], tokm[:, 2, h2 * 64:(h2 + 1) * 64], Vex[:, bsl, :], start=False, stop=True), reads=[tokm, Vex], writes=[ph])
                            kb.op("dve", lambda e, P_=P_, bsl=bsl, pg=pg: e.tensor_tensor(GTs[P_, bsl, :], pg[P_, :].rearrange("p (b k) -> p b k", b=8), i2[P_, :].unsqueeze(1).to_broadcast([64, 8, 64]), ALU.add), reads=[pg, self.consts], writes=[GTs])
                            kb.op("act", lambda e, P_=P_, bsl=bsl, ph=ph: e.copy(Hs[P_, bsl, :], ph[P_, :].rearrange("p (b k) -> p b k", b=8)), reads=[ph], writes=[Hs])
                        for b in range(16):
                            kb.op("pe", lambda e, P_=P_, b=b: e.matmul(pb[1][P_, (b % 8) * 64:(b % 8 + 1) * 64], GTs[P_, b, :], Zsb[P_, hp, b, :], start=True, stop=True), reads=[GTs, Zsb], writes=[pb[1]])
                            if b % 8 == 7:
                                bsl = slice(b - 7, b + 1)
                                kb.op("dve", lambda e, P_=P_, bsl=bsl: e.tensor_tensor(Hs[P_, bsl, :], Hs[P_, bsl, :], pb[1][P_, :].rearrange("p (b k) -> p b k", b=8), ALU.add), reads=[Hs, pb[1]], writes=[Hs])
                    kb.op("dve", lambda e: e.tensor_tensor(Znh[:], Hs[:], gam[:, hp, :].unsqueeze(2).to_broadcast([128, 16, 64]), ALU.mult), reads=[Hs, gam], writes=[Znh])
                    kb.dma("sp", self.O["wkvs"], self.O["wkvs"][:, hp * 1024:(hp + 1) * 1024], Znh, Znh[:].rearrange("p b v -> p (b v)"), grp=self.g_out)
                    kb.op("dve", lambda e: e.tensor_copy(ysb[:], pb[0][:, 0:128]), reads=[pb[0]], writes=[ysb])
                if sample:
                    kb.op("dve", lambda e: e.tensor_tensor(ysb[:], ysb[:], pYT[:, 128:256], ALU.add), reads=[ysb, pYT], writes=[ysb])
                else:
                    kb.op("act", lambda e: e.copy(ysb[:], pYT[:, 128:256]), reads=[pYT], writes=[ysb])
                pm = pb[3]
                kb.op("pe", lambda e: e.matmul(pm[:, 0:128], bones, ysb[:], start=True, stop=True), reads=[self.consts, ysb], writes=[pm])
                kb.op("dve", lambda e: e.scalar_tensor_tensor(dd[:], pm[:, 0:128], -1.0 / 64, ysb[:], ALU.mult, ALU.add), reads=[pm, ysb], writes=[dd])
                kb.op("dve", lambda e: e.tensor_tensor(d2[:], dd[:], dd[:], ALU.mult), reads=[dd], writes=[d2])
                kb.op("pe", lambda e: e.matmul(pm[:, 128:256], bones, d2[:], start=True, stop=True), reads=[self.consts, d2], writes=[pm])
                kb.op("act", lambda e: e.activation(rs_[:], pm[:, 128:256], AF.Sqrt, bias=self.gneps[:], scale=1.0 / 64), reads=[pm, self.gneps], writes=[rs_])
                kb.op("dve", lambda e: e.reciprocal(rs_[:], rs_[:]), reads=[rs_], writes=[rs_])
                kb.op("dve", lambda e: e.tensor_tensor(dd[:], dd[:], rs_[:], ALU.mult), reads=[dd, rs_], writes=[dd])
                kb.op("dve", lambda e: e.tensor_scalar(dd[:], dd[:], CP(64, hp), CP(68, hp), ALU.mult, ALU.add), reads=[dd, colp], writes=[dd])
                kb.op("dve", lambda e: e.tensor_tensor(dd[:], dd[:], bv[:, hp, cs], ALU.add), reads=[dd, bv], writes=[dd])
                kb.op("dve", lambda e: e.tensor_tensor(self.ycat[:, 4 + hp, cs], dd[:], gT[:, hp, cs], ALU.mult), reads=[dd, gT], writes=[self.ycat])
        if blk == 0:
            self.dbg_dump("yrw0", self.ycat, self.ycat[:], [128, 8, NB], BF16)
        if sample:
            self.dbg_dump("yrwS", self.ycat, self.ycat[:], [128, 8, 128], BF16)
            shso = Hs
            shsov = Hs[0:16, 0:8, :].rearrange("p a b -> p (a b)")
            for r in range(14):
                kb.op("pe", lambda e, r=r: e.transpose(pb[4][0:16, (r % 4) * 128:(r % 4 + 1) * 128], self.shsT[:, r, :], ident), reads=[self.shsT, self.consts], writes=[pb[4]])
                if r % 4 == 3 or r == 13:
                    r0 = r - (r % 4)
                    n = r - r0 + 1
                    kb.op("dve", lambda e, n=n: e.tensor_copy(shsov[:, 0:n * 128], pb[4][0:16, 0:n * 128]), reads=[pb[4]], writes=[shso])
                    kb.dma("sp", self.O["shs"], self.O["shs"][:, r0 * 128:(r + 1) * 128], shso, shsov[:, 0:n * 128], grp=self.g_out)
        if blk == NPB - 1 and "no_tail" not in self.dbg:
            wv = self.O["wkvp"][:].rearrange("(a h2) k v -> h2 k a v", h2=2)
            for h2 in range(2):
                kb.dma("sp", self.O["wkvp"], wv[h2], self.Zf, self.Zf[h2 * 64:(h2 + 1) * 64, :, :], grp=self.g_out)
            kb.op("pe", lambda e: e.transpose(pb[4][0:14, 0:128], self.carry[:], ident), reads=[self.carry, self.consts], writes=[pb[4]])
            kb.op("dve", lambda e: e.tensor_copy(ysb[0:14, :], pb[4][0:14, 0:128]), reads=[pb[4]], writes=[ysb])
            kb.dma("sp", self.O["shp"], self.O["shp"][:], ysb, ysb[0:14, :], grp=self.g_out)

    def outproj(self, blk, sample, nb, ntile, row0, GA1):
        kb = self.kb
        self.x1t = kb.sb("x1t", [128, D])
        xt = kb.sb("xt", [128, D])
        for j in range(ntile):
            kb.dma("sp", xt, xt[:], self.I["xall"], self.I["xall"][row0 + j * 128:row0 + (j + 1) * 128, :])
            for hf in range(2):
                pk = self.pb[hf]
                for c in range(8):
                    kb.op("pe", lambda e, c=c, hf=hf, pk=pk: e.matmul(pk[:], self.ycat[:, c, j * 128:(j + 1) * 128], self.woutb[:, c, hf * 512:(hf + 1) * 512], start=(c == 0), stop=(c == 7)),
                          reads=[self.ycat, self.woutb], writes=[pk])
                sl = slice(hf * 512, (hf + 1) * 512)
                kb.op("dve", lambda e, pk=pk, sl=sl: e.tensor_tensor(self.x1t[:, sl], pk[:], GA1[:, sl], ALU.mult), reads=[pk, GA1], writes=[self.x1t])
                kb.op("dve", lambda e, sl=sl: e.tensor_tensor(self.x1t[:, sl], self.x1t[:, sl], xt[:, sl], ALU.add), reads=[self.x1t, xt], writes=[self.x1t])
            kb.dma("sp", self.x1s, self.x1s[row0 + j * 128:row0 + (j + 1) * 128, :], self.x1t, self.x1t[:], grp=self.g_x1)

    def tab_alloc(self):
        kb = self.kb
        self.UTs = kb.dram("UTs", [128, 128, 8, 128], BF16)
        self.Vs = kb.dram("Vs", [128, 128, D], BF16)
        self.g_tab = [[kb.group(f"tabu{i}"), kb.group(f"tabv{i}")] for i in range(2)]
        self.tb_uf = [kb.sb(f"uf{i}", [128, D]) for i in range(2)]
        self.tb_vf = [kb.sb(f"vf{i}", [128, D]) for i in range(2)]
        self.tb_utb = [kb.sb(f"utb{i}", [128, 8, 128], BF16) for i in range(2)]
        self.tb_vbb = [kb.sb(f"vbb{i}", [128, D], BF16) for i in range(2)]

    def tab_prep(self):
        kb = self.kb
        I = self.I
        pb = self.pb
        ident = self.C("ident")
        uf, vf, utb, vbb = self.tb_uf, self.tb_vf, self.tb_utb, self.tb_vbb
        pu = [pb[5], pb[6]]

        def tload(k):
            kb.dma("sp", uf[k % 2], uf[k % 2][:], I["peer_u"], I["peer_u"][k * 128:(k + 1) * 128, :])
            kb.dma("sp", vf[k % 2], vf[k % 2][:], I["peer_v"], I["peer_v"][k * 128:(k + 1) * 128, :])
        tload(0)
        for k in range(128):
            u, v, ub, vb = uf[k % 2], vf[k % 2], utb[k % 2], vbb[k % 2]
            if k + 1 < 128:
                tload(k + 1)
            for c in range(8):
                kb.op("pe", lambda e: e.transpose(pu[c // 4][:, (c % 4) * 128:(c % 4 + 1) * 128], u[:, c * 128:(c + 1) * 128], ident), reads=[u, self.consts], writes=[pu[c // 4]])
            kb.op("act", lambda e: e.copy(ub[:, 0:4, :].rearrange("p a n -> p (a n)"), pu[0][:]), reads=[pu[0]], writes=[ub])
            kb.op("dve", lambda e: e.tensor_copy(ub[:, 4:8, :].rearrange("p a n -> p (a n)"), pu[1][:]), reads=[pu[1]], writes=[ub])
            self.cast("pool", vb, vb[:, 0:512], v, v[:, 0:512])
            self.cast("act" if k % 2 else "dve", vb, vb[:, 512:1024], v, v[:, 512:1024])
            kb.dma("pool", self.UTs, self.UTs[k], ub, ub[:], grp=self.g_tab[k % 2][0])
            kb.dma("pool", self.Vs, self.Vs[k], vb, vb[:], grp=self.g_tab[k % 2][1])
        for gg in self.g_tab:
            for g_ in gg:
                g_.close()

    def phase2(self):
        kb = self.kb
        I = self.I
        pb = self.pb
        ident = self.C("ident")
        NEG = -1.0e30
        g = kb.group("p2par")
        fng = kb.sb("fng", [128, D])
        kb.dma("sp", fng, fng[:], I["fng"], I["fng"][:].partition_broadcast(128), grp=g)
        g.close()
        keysT = kb.sb("keysT", [128, 8, 128], BF16)
        wqb = kb.sb("wqb", [128, 8, D], BF16)
        mod2 = [kb.sb(f"m2{n}", [128, D]) for n in range(3)]
        kb.scope_enter()
        g = kb.group("p2k")
        K12 = kb.sb("K12", [128, 8, 128])
        kb.dma("sp", K12, K12[:, :, 0:64], I["keys1"], I["keys1"][:].rearrange("h n d -> n h d"), grp=g)
        kb.dma("sp", K12, K12[:, :, 64:128], I["keys2"], I["keys2"][:].rearrange("h n d -> n h d"), grp=g)
        g.close()
        for h in range(8):
            kb.op("pe", lambda e, h=h: e.transpose(pb[h // 4][:, (h % 4) * 128:(h % 4 + 1) * 128], K12[:, h, :], ident), reads=[K12, self.consts], writes=[pb[h // 4]])
        for q in range(2):
            kb.op("dve", lambda e, q=q: e.tensor_copy(keysT[:, q * 4:(q + 1) * 4, :].rearrange("p a n -> p (a n)"), pb[q][:]), reads=[pb[q]], writes=[keysT])
        if "p2a" in self.dbg:
            kb.scope_exit()
            return
        wst = [kb.sb(f"wq{i}", [128, D]) for i in range(2)]
        for kc in range(8):
            s_ = kc % 2
            kb.dma("sp", wst[s_], wst[s_][:], I["w_q"], I["w_q"][kc * 128:(kc + 1) * 128, :])
            self.cast("act" if kc % 2 else "dve", wqb, wqb[:, kc, :], wst[s_], wst[s_][:])
        UTs, Vs = self.UTs, self.Vs
        kb.scope_exit()
        if "p2b" in self.dbg:
            return
        x1g = [kb.sb(f"x1g{i}", [128, D]) for i in range(2)]
        x1r = kb.sb("x1r", [128, D])
        h2Ts = [kb.sb(f"h2T{i}", [128, 8, NB], BF16) for i in range(2)]
        qT = kb.sb("qT", [128, 8, 128], BF16)
        sqj = kb.sb("sqj2", [128, D], BF16)
        t1k = kb.sb("t1k2", [128, D])
        hb = kb.sb("hb2", [128, D], BF16)
        ss = kb.sb("ss2", [128, 4])
        rstd = kb.sb("rstd2", [128, 4])
        sc = kb.sb("sc", [128, 8, 2, 128])
        v16 = kb.sb("v16", [128, 8, 2, 16])
        v16b = kb.sb("v16b", [128, 8, 2, 8])
        s16b = kb.sb("s16b", [128, 8, 8])
        i16 = kb.sb("i16", [128, 8, 2, 16], U32)
        i16f = kb.sb("i16f", [128, 8, 2, 16])
        cand = kb.sb("cand", [128, 8, 16, 16])
        s16 = kb.sb("s16", [128, 8, 16])
        p16 = kb.sb("p16", [128, 8, 16], U32)
        pa_i = kb.sb("pa_i", [128, 8, 16], U32)
        paf = kb.sb("paf", [128, 2, 8, 16])
        oh = kb.sb("oh", [128, 8, 16, 16])
        sm = kb.sb("sm", [128, 8])
        jt = kb.sb("jt", [128, 3, 128])
        jTs = [[kb.sb(f"jT{p_}{j_}", [128, 3, 128], BF16) for j_ in range(2)] for p_ in range(2)]
        iotab = kb.sb("iotab", [128, 128], BF16)
        kb.op("dve", lambda e: e.tensor_copy(iotab[:], self.C("iota")), reads=[self.consts], writes=[iotab])
        OH1gs = [kb.sb(f"OH1g{i}", [128, 16, 128], BF16) for i in range(2)]
        OH2s = [kb.sb(f"OH2{i}", [128, 16, 128], BF16) for i in range(2)]
        Wall = kb.sb("Wall", [128, NB, 128], BF16)
        ubs = [kb.sb(f"ubs{i}", [128, 8, 128], BF16) for i in range(4)]
        vbs = [kb.sb(f"vbs{i}", [128, D], BF16) for i in range(4)]
        gsb = kb.sb("gsb", [128, NB], BF16)
        WA = [kb.sb(f"WA{i}", [128, NB], BF16) for i in range(2)]
        x2 = kb.sb("x2", [128, D])
        iota = self.C("iota")
        SH2, A2, GA2 = mod2
        pq = pb[6]

        def ginfo(gi):
            sample = gi == NPB
            nt = TS if sample else NB
            return sample, nt, nt // 128, (TP if sample else gi * NB), (1 if sample else 0)

        def front(gi):
            sample, nt, ntile, row0, typ = ginfo(gi)
            h2T = h2Ts[gi % 2]
            if gi == 0 or sample:
                for n in range(2):
                    kb.dma("sp", mod2[n], mod2[n][:], self.mod2s, self.mod2s[typ, 3 + n])
            for j in range(ntile):
                ts_ = slice(j * 128, (j + 1) * 128)
                kb.dma("sp", x1g[j], x1g[j][:], self.x1s, self.x1s[row0 + j * 128:row0 + (j + 1) * 128, :])
                kb.op("act", lambda e: e.activation(sqj[:], x1g[j][:], AF.Square, accum_out=ss[:, j:j + 1]), reads=[x1g[j]], writes=[sqj, ss])
                kb.op("act", lambda e: e.activation(rstd[:, j:j + 1], ss[:, j:j + 1], AF.Sqrt, bias=self.eps6[:], scale=1.0 / D), reads=[ss, self.eps6], writes=[rstd])
                kb.op("dve", lambda e: e.reciprocal(rstd[:, j:j + 1], rstd[:, j:j + 1]), reads=[rstd], writes=[rstd])
                kb.op("dve", lambda e: e.scalar_tensor_tensor(t1k[:], x1g[j][:], rstd[:, j:j + 1], A2[:], ALU.mult, ALU.mult), reads=[x1g[j], rstd, A2], writes=[t1k])
                kb.op("pool", lambda e: e.tensor_tensor(hb[:], t1k[:], SH2[:], ALU.add), reads=[t1k, SH2], writes=[hb])
                for c in range(8):
                    kb.op("pe", lambda e: e.transpose(self.pt[:, c * 128:(c + 1) * 128], hb[:, c * 128:(c + 1) * 128], self.identb[:]), reads=[hb, self.identb], writes=[self.pt])
                kb.op("act", lambda e: e.copy(h2T[:, :, ts_], self.pt[:].rearrange("p (c n) -> p c n", c=8)), reads=[self.pt], writes=[h2T])
                for hg in range(2):
                    for hh in range(4):
                        h = hg * 4 + hh
                        for k in range(8):
                            kb.op("pe", lambda e: e.matmul(pq[:, hh * 128:(hh + 1) * 128], wqb[:, k, h * 128:(h + 1) * 128], h2T[:, k, ts_], start=(k == 0), stop=(k == 7)), reads=[wqb, h2T], writes=[pq])
                    kb.op("act", lambda e: e.copy(qT[:, hg * 4:(hg + 1) * 4, :], pq[:].rearrange("p (a n) -> p a n", a=4)), reads=[pq], writes=[qT])
                for f in range(2):
                    P_ = slice(f * 64, (f + 1) * 64)
                    for hg in range(2):
                        for hh in range(4):
                            h = hg * 4 + hh
                            kb.op("pe", lambda e: e.matmul(pq[:, hh * 128:(hh + 1) * 128], qT[P_, h, :], keysT[P_, h, :], start=True, stop=True), reads=[qT, keysT], writes=[pq])
                        kb.op("act", lambda e: e.copy(sc[:, hg * 4:(hg + 1) * 4, f, :], pq[:].rearrange("p (a n) -> p a n", a=4)), reads=[pq], writes=[sc])
                HF = [(h, f) for h in range(8) for f in range(2)]
                sctb = oh[:].rearrange("p a b c -> p (a b c)").rearrange("p (q n) -> p q n", q=16)
                for h, f in HF:
                    kb.op("dve", lambda e, h=h, f=f: e.max(v16[:, h, f, 0:8], sc[:, h, f, :]), reads=[sc], writes=[v16])
                for h, f in HF:
                    kb.op("dve", lambda e, h=h, f=f: e.max_index(i16[:, h, f, 0:8], v16[:, h, f, 0:8], sc[:, h, f, :]), reads=[sc, v16], writes=[i16])
                for q, (h, f) in enumerate(HF):
                    kb.op("dve", lambda e, h=h, f=f, q=q: e.match_replace(sctb[:, q, :], v16[:, h, f, 0:8], sc[:, h, f, :], NEG), reads=[sc, v16], writes=[oh])
                for q, (h, f) in enumerate(HF):
                    kb.op("dve", lambda e, h=h, f=f, q=q: e.max(v16b[:, h, f, :], sctb[:, q, :]), reads=[oh], writes=[v16b])
                for q, (h, f) in enumerate(HF):
                    kb.op("dve", lambda e, h=h, f=f, q=q: e.max_index(i16[:, h, f, 8:16], v16b[:, h, f, :], sctb[:, q, :]), reads=[oh, v16b], writes=[i16])
                kb.op("dve", lambda e: e.tensor_copy(v16[:, :, :, 8:16], v16b[:]), reads=[v16b], writes=[v16])
                kb.op("dve", lambda e: e.tensor_copy(i16f[:], i16[:]), reads=[i16], writes=[i16f])
                kb.op("dve", lambda e: e.tensor_tensor(cand[:], v16[:, :, 0, :].unsqueeze(3).to_broadcast([128, 8, 16, 16]), v16[:, :, 1, :].unsqueeze(2).to_broadcast([128, 8, 16, 16]), ALU.add), reads=[v16], writes=[cand])
                candf = lambda h: cand[:, h, :, :].rearrange("p a b -> p (a b)")
                sc2 = sc[:].rearrange("p h f n -> p h (f n)")
                for h in range(8):
                    kb.op("dve", lambda e, h=h: e.max(s16[:, h, 0:8], candf(h)), reads=[cand], writes=[s16])
                for h in range(8):
                    kb.op("dve", lambda e, h=h: e.max_index(p16[:, h, 0:8], s16[:, h, 0:8], candf(h)), reads=[cand, s16], writes=[p16])
                for h in range(8):
                    kb.op("dve", lambda e, h=h: e.match_replace(sc2[:, h, :], s16[:, h, 0:8], candf(h), NEG), reads=[cand, s16], writes=[sc])
                for h in range(8):
                    kb.op("dve", lambda e, h=h: e.max(s16b[:, h, :], sc2[:, h, :]), reads=[sc], writes=[s16b])
                for h in range(8):
                    kb.op("dve", lambda e, h=h: e.max_index(p16[:, h, 8:16], s16b[:, h, :], sc2[:, h, :]), reads=[sc, s16b], writes=[p16])
                kb.op("dve", lambda e: e.tensor_copy(s16[:, :, 8:16], s16b[:]), reads=[s16b], writes=[s16])
                gate = jt[:, 2, :].rearrange("p (h k) -> p h k", h=8)
                kb.op("dve", lambda e: e.tensor_tensor(gate, s16[:], s16[:, :, 0:1].to_broadcast([128, 8, 16]), ALU.subtract), reads=[s16], writes=[jt])
                kb.op("act", lambda e: e.activation(gate, gate, AF.Exp), reads=[jt], writes=[jt])
                kb.op("dve", lambda e: e.tensor_reduce(sm[:], gate, AX.X, ALU.add), reads=[jt], writes=[sm])
                kb.op("dve", lambda e: e.reciprocal(sm[:], sm[:]), reads=[sm], writes=[sm])
                kb.op("dve", lambda e: e.tensor_tensor(gate, gate, sm[:].unsqueeze(2).to_broadcast([128, 8, 16]), ALU.mult), reads=[jt, sm], writes=[jt])
                kb.op("dve", lambda e: e.tensor_single_scalar(pa_i[:], p16[:], 4, ALU.logical_shift_right), reads=[p16], writes=[pa_i])
                kb.op("dve", lambda e: e.tensor_copy(paf[:, 0], pa_i[:]), reads=[pa_i], writes=[paf])
                kb.op("dve", lambda e: e.tensor_single_scalar(pa_i[:], p16[:], 15, ALU.bitwise_and), reads=[p16], writes=[pa_i])
                kb.op("dve", lambda e: e.tensor_copy(paf[:, 1], pa_i[:]), reads=[pa_i], writes=[paf])
                io16 = iota[:, 0:16].unsqueeze(1).unsqueeze(1).to_broadcast([128, 8, 16, 16])
                for f in range(2):
                    kb.op("dve", lambda e, f=f: e.tensor_tensor(oh[:], io16, paf[:, f].unsqueeze(3).to_broadcast([128, 8, 16, 16]), ALU.is_equal), reads=[paf, self.consts], writes=[oh])
                    kb.op("dve", lambda e, f=f: e.tensor_tensor(oh[:], oh[:], i16f[:, :, f, :].unsqueeze(2).to_broadcast([128, 8, 16, 16]), ALU.mult), reads=[oh, i16f], writes=[oh])
                    kb.op("dve", lambda e, f=f: e.tensor_reduce(jt[:, f, :].rearrange("p (h k) -> p h k", h=8), oh[:], AX.X, ALU.add), reads=[oh], writes=[jt])
                jTc = jTs[gi % 2][j]
                for q in range(3):
                    kb.op("pe", lambda e: e.transpose(pq[:, q * 128:(q + 1) * 128], jt[:, q, :], ident), reads=[jt, self.consts], writes=[pq])
                kb.op("act", lambda e: e.copy(jTc[:].rearrange("p a n -> p (a n)"), pq[:, 0:384]), reads=[pq], writes=[jTc])

        def mainA(gi):
            sample, nt, ntile, row0, typ = ginfo(gi)
            if gi == 0 or sample:
                kb.dma("sp", mod2[2], mod2[2][:], self.mod2s, self.mod2s[typ, 5])
            for j in range(ntile):
                jTc = jTs[gi % 2][j]
                for qq in range(8):
                    t0 = qq * 16
                    OH1g, OH2 = OH1gs[qq % 2], OH2s[qq % 2]
                    io3 = iotab[:].unsqueeze(1).to_broadcast([128, 16, 128])
                    bc = lambda q, t0=t0: jTc[:, q, t0:t0 + 16].unsqueeze(2).to_broadcast([128, 16, 128])
                    kb.op("dve", lambda e: e.tensor_tensor(OH2[:], io3, bc(1), ALU.is_equal), reads=[jTc, iotab], writes=[OH2])
                    kb.op("dve", lambda e: e.tensor_tensor(OH1g[:], io3, bc(0), ALU.is_equal), reads=[jTc, iotab], writes=[OH1g])
                    kb.op("pool" if qq % 2 else "dve", lambda e: e.tensor_tensor(OH1g[:], OH1g[:], bc(2), ALU.mult), reads=[OH1g, jTc], writes=[OH1g])
                    for t4 in range(4):
                        pk = pb[4 + t4 % 2]
                        for u in range(4):
                            t = t4 * 4 + u
                            kb.op("pe", lambda e, t=t, u=u, pk=pk: e.matmul(pk[:, u * 128:(u + 1) * 128], OH2[:, t, :], OH1g[:, t, :], start=True, stop=True), reads=[OH2, OH1g], writes=[pk])
                        tb = j * 128 + t0 + t4 * 4
                        kb.op("act", lambda e, tb=tb, pk=pk: e.copy(Wall[:, tb:tb + 4, :].rearrange("p a n -> p (a n)"), pk[:]), reads=[pk], writes=[Wall])

        def mainB(gi):
            sample, nt, ntile, row0, typ = ginfo(gi)
            h2T = h2Ts[gi % 2]

            def sload(k):
                kb.dma("sp", ubs[k % 4], ubs[k % 4][:], UTs, UTs[k])
                kb.dma("sp", vbs[k % 4], vbs[k % 4][:], Vs, Vs[k])
            sload(0)
            sload(1)
            sload(2)

            def Dmm(k):
                ub, pD = ubs[k % 4], pb[4 + k % 2]
                for c in range(8):
                    kb.op("pe", lambda e: e.matmul(pD[:, 0:nt], ub[:, c, :], h2T[:, c, 0:nt], start=(c == 0), stop=(c == 7)), reads=[ub, h2T], writes=[pD])
            Dmm(0)
            for k in range(128):
                vb = vbs[k % 4]
                pD = pb[4 + k % 2]
                wa = WA[k % 2]
                if k + 1 < 128:
                    Dmm(k + 1)
                kb.op("act", lambda e: e.activation(gsb[:, 0:nt], pD[:, 0:nt], AF.Gelu), reads=[pD], writes=[gsb])
                kb.op("dve", lambda e: e.tensor_tensor(wa[:, 0:nt], gsb[:, 0:nt], Wall[:, 0:nt, k], ALU.mult), reads=[gsb, Wall], writes=[wa])
                for a in range(ntile):
                    for hf in range(2):
                        po = pb[a * 2 + hf]
                        kb.op("pe", lambda e: e.matmul(po[:], wa[:, a * 128:(a + 1) * 128], vb[:, hf * 512:(hf + 1) * 512], start=(k == 0), stop=(k == 127)), reads=[wa, vb], writes=[po])
                if k + 3 < 128:
                    sload(k + 3)
            for a in range(ntile):
                kb.dma("sp", x1r, x1r[:], self.x1s, self.x1s[row0 + a * 128:row0 + (a + 1) * 128, :])
                for hf in range(2):
                    sl = slice(hf * 512, (hf + 1) * 512)
                    po = pb[a * 2 + hf]
                    kb.op("dve", lambda e: e.tensor_tensor(x2[:, sl], po[:], GA2[:, sl], ALU.mult), reads=[po, GA2], writes=[x2])
                    kb.op("dve", lambda e: e.tensor_tensor(x2[:, sl], x2[:, sl], x1r[:, sl], ALU.add), reads=[x2, x1r], writes=[x2])
                kb.op("act", lambda e: e.activation(sqj[:], x2[:], AF.Square, accum_out=ss[:, 2 + a:3 + a]), reads=[x2], writes=[sqj, ss])
                kb.op("act", lambda e: e.activation(rstd[:, 2 + a:3 + a], ss[:, 2 + a:3 + a], AF.Sqrt, bias=self.eps6[:], scale=1.0 / D), reads=[ss, self.eps6], writes=[rstd])
                kb.op("dve", lambda e: e.reciprocal(rstd[:, 2 + a:3 + a], rstd[:, 2 + a:3 + a]), reads=[rstd], writes=[rstd])
                kb.op("dve", lambda e: e.scalar_tensor_tensor(x2[:], x2[:], rstd[:, 2 + a:3 + a], fng[:], ALU.mult, ALU.mult), reads=[x2, rstd, fng], writes=[x2])
                kb.dma("sp", self.Oy, self.Oy[row0 + a * 128:row0 + (a + 1) * 128, :], x2, x2[:], grp=self.g_out)

        ngr = NPB + 1
        for f_ in self.dbg:
            if f_.startswith("ngr"):
                ngr = int(f_[3:])
        wts = [9, 2]
        for f_ in self.dbg:
            if f_.startswith("wt"):
                wts = [int(x) for x in f_[2:].split("_")]
        front(0)
        def main(gi):
            mainA(gi)
            mainB(gi)
        for gi in range(ngr):
            if gi + 1 < ngr and "p2serial" not in self.dbg:
                kb.interleave([lambda gi=gi: main(gi), lambda gi=gi: front(gi + 1)], weights=wts)
            else:
                main(gi)
                if gi + 1 < ngr:
                    front(gi + 1)


def host_inputs(inp, core):
    f = lambda a: np.ascontiguousarray(a, dtype=np.float32)
    m = {}
    xs = inp["x_sample"][16 * core:16 * core + 16].reshape(128, D)
    m["xall"] = f(np.concatenate([inp["x_prompt"][core], xs], 0))
    cp = np.broadcast_to(inp["c_prompt"][core][None, :], (128, D))
    cs = np.repeat(inp["c_sample"][16 * core:16 * core + 16], 8, axis=0)
    m["crep"] = f(np.stack([cp, cs], 0))
    m["consts"] = CONSTS
    for k in ("w_ada", "b_ada", "norm1_g", "norm2_g", "w_in", "w_out", "w_glu"):
        m[k] = f(inp[k][0])
    are, aim, ldt = inp["s5_a_re"][0], inp["s5_a_im"][0], inp["s5_log_dt"][0]
    bre, bim, cre, cim = inp["s5_b_re"][0], inp["s5_b_im"][0], inp["s5_c_re"][0], inp["s5_c_im"][0]
    R = np.zeros((4, 128, 5, 64), np.float32)
    gidx = (np.arange(512) // 16).reshape(4, 128)
    hidx = (np.arange(512) % 16).reshape(4, 128)
    R[:, :, 0, :] = are[gidx]
    R[:, :, 1, :] = aim[gidx]
    R[:, :, 2, :] = ldt[gidx][..., None]
    R[:, :, 3, :] = bre[gidx, :, hidx]
    R[:, :, 4, :] = bim[gidx, :, hidx]
    m["s5R"] = f(R.transpose(1, 0, 2, 3))
    Pm = np.stack([are.T, aim.T, np.broadcast_to(ldt[None, :], (64, 32))], 1)
    m["s5P"] = f(Pm)
    PB = np.stack([bre.transpose(1, 0, 2).reshape(64, 512), bim.transpose(1, 0, 2).reshape(64, 512),
                   cre.transpose(2, 0, 1).reshape(64, 512), cim.transpose(2, 0, 1).reshape(64, 512)], 1)
    m["s5PB"] = f(PB)
    def qlay(a_gp):
        return a_gp.reshape(16, 2, 64).transpose(1, 2, 0).reshape(128, 16)
    m["s5Q"] = f(np.stack([qlay(are), qlay(aim), qlay(np.broadcast_to(ldt[:, None], (32, 64)))], 1))
    def qlay3(c_ghp):
        return c_ghp.reshape(16, 2, 16, 64).transpose(1, 3, 0, 2).reshape(128, 16, 16)
    m["s5QC"] = f(np.stack([qlay3(cre), qlay3(cim)], 1))
    sr = inp["state_s5_re"][0, 16 * core:16 * core + 16]
    si = inp["state_s5_im"][0, 16 * core:16 * core + 16]
    def qst(s):
        return s.reshape(16, 16, 2, 64).transpose(2, 3, 1, 0).reshape(128, 16, 16)
    m["s5st"] = f(np.stack([qst(sr), qst(si)], 1))
    colp = np.zeros((128, 80), np.float32)
    mu = inp["rwkv_mu"][0]
    colp[:, 0:14] = mu.reshape(14, 128).T
    colp[:, 14:28] = 0.0
    colp[:, 36:40] = inp["s5_d"][0].reshape(4, 128).T
    colp[:, 32:36] = inp["b_glu"][0].reshape(4, 128).T
    c4 = lambda a: np.asarray(a).reshape(4, 128).T
    colp[:, 40:44] = c4(inp["rwkv_w0"][0])
    colp[:, 44:48] = c4(inp["rwkv_a0"][0])
    colp[:, 48:52] = c4(inp["rwkv_k_k"][0])
    colp[:, 52:56] = c4(inp["rwkv_k_a"][0])
    colp[:, 60:64] = c4(inp["rwkv_r_k"][0].reshape(512))
    colp[:, 64:68] = c4(inp["rwkv_gn_w"][0])
    colp[:, 68:72] = c4(inp["rwkv_gn_b"][0])
    m["colp"] = colp
    wk = inp["state_wkv"][0, 16 * core:16 * core + 16]
    m["wkv0"] = f(wk.reshape(16, 4, 2, 64, 64).transpose(2, 4, 1, 0, 3).reshape(128, 4096))
    m["rowp"] = f(np.stack([inp["rwkv_gn_w"][0], inp["rwkv_gn_b"][0], inp["rwkv_gn_b"][0]], 0))
    m["w2"] = f(inp["rwkv_w2"][0])
    m["a2"] = f(inp["rwkv_a2"][0])
    m["g2"] = f(inp["rwkv_g2"][0])
    m["shift0"] = f(inp["state_shift"][0, 16 * core:16 * core + 16])
    m["w_q"] = f(inp["peer_w_q"][0])
    m["keys1"] = f(inp["peer_keys1"][0])
    m["keys2"] = f(inp["peer_keys2"][0])
    m["peer_u"] = f(inp["peer_u"][0])
    m["peer_v"] = f(inp["peer_v"][0])
    m["fng"] = f(inp["final_norm_g"])
    return m


_CACHE = {}


def kernel(**inputs):
    inp = {k: np.asarray(v) for k, v in inputs.items()}
    if "nc" not in _CACHE:
        _CACHE["nc"] = Builder().build()
    nc = _CACHE["nc"]
    in_maps = [host_inputs(inp, c) for c in range(NCORES)]
    res = run_bass_kernel_spmd(nc, in_maps, core_ids=list(range(NCORES)))
    R = res.results
    nc_ = NCORES
    y_p = np.stack([R[c]["y"][:TP] for c in range(nc_)], 0)
    y_s = np.concatenate([R[c]["y"][TP:].reshape(16, 8, D) for c in range(nc_)], 0)
    s5p = np.stack([R[c]["s5p"].reshape(2, 32, 64) for c in range(nc_)], 1)
    s5s = np.concatenate([R[c]["s5s"].reshape(2, 16, 32, 64) for c in range(nc_)], 1)
    wkvp = np.stack([R[c]["wkvp"].transpose(0, 2, 1) for c in range(nc_)], 0)
    shp = np.stack([R[c]["shp"].reshape(1792) for c in range(nc_)], 0)
    wkvs = np.concatenate([R[c]["wkvs"].reshape(2, 64, 4, 16, 64).transpose(3, 2, 0, 4, 1).reshape(16, 8, 64, 64) for c in range(nc_)], 0)
    shs = np.concatenate([R[c]["shs"] for c in range(nc_)], 0)
    f = lambda a: np.ascontiguousarray(a, dtype=np.float32)
    return (f(y_p), f(y_s), f(s5p[0][None]), f(s5p[1][None]), f(wkvp[None]), f(shp[None]),
            f(s5s[0][None]), f(s5s[1][None]), f(wkvs[None]), f(shs[None]))
```
